# Optimizing a Trainium2 kernel written in Bass

```python
import math
import jax
import jax.numpy as jnp
from jax import lax
import numpy as np

D_MODEL = 1024
BATCH = 8
SEQ = 4096
DEPTH = 4

CTX_LEN = 256
GRID_W = 64
NORM_EPS = 1e-6

HYENA_WIDTH = D_MODEL // 2
HYENA_ORDER = 2
HYENA_PROJ = (HYENA_ORDER + 1) * HYENA_WIDTH
SHORT_CONV = 3
FILTER_EMB = 33
FILTER_HIDDEN = 64
FILTER_FAST_DECAY = 0.3
FILTER_SLOW_DECAY = 1.5
FILTER_TARGET = 1e-2

DIFF_HEADS = 4
DIFF_HEAD_DIM = 64
DIFF_V_DIM = 2 * DIFF_HEAD_DIM
DIFF_QK_WIDTH = DIFF_HEADS * 2 * DIFF_HEAD_DIM
DIFF_WIDTH = DIFF_HEADS * DIFF_V_DIM
ROPE_BASE = 10000.0
ATTN_BLOCK = 128

IN_WIDTH = HYENA_PROJ + 2 * DIFF_QK_WIDTH + DIFF_WIDTH
MIX_WIDTH = HYENA_WIDTH + DIFF_WIDTH

POOL_WINDOWS = (2, 4, 8, 16)
POOL_GROUP = D_MODEL // len(POOL_WINDOWS)

N_EXPERTS = 16
EXPERT_FF = 2 * D_MODEL
EC_CAPACITY = 2

kernel_name = 'hybrid_hyena_diffattn_pool_ecmoe_dit'


def rms_norm(x, g):
    xf = x.astype(jnp.float32)
    y = xf * lax.rsqrt(jnp.mean(xf * xf, axis=-1, keepdims=True) + NORM_EPS)
    return (y * g.astype(jnp.float32)).astype(x.dtype)


def rope_1d(x, pos):
    nf = x.shape[-1] // 2
    inv = ROPE_BASE ** (-jnp.arange(nf, dtype=jnp.float32) / nf)
    ang = pos.astype(jnp.float32)[:, None] * inv[None, :]
    shape = (1, ang.shape[0]) + (1,) * (x.ndim - 3) + (nf,)
    cos = jnp.cos(ang).reshape(shape).astype(x.dtype)
    sin = jnp.sin(ang).reshape(shape).astype(x.dtype)
    x1, x2 = x[..., :nf], x[..., nf:]
    return jnp.concatenate([x1 * cos - x2 * sin, x1 * sin + x2 * cos], axis=-1)


def axial_rope(x, row, col):
    half = x.shape[-1] // 2
    return jnp.concatenate([rope_1d(x[..., :half], row), rope_1d(x[..., half:], col)], axis=-1)


def short_conv_centred(u, w, b):
    L = u.shape[1]
    r = SHORT_CONV // 2
    up = jnp.pad(u, ((0, 0), (r, r), (0, 0)))
    return sum(up[:, j:j + L] * w[j] for j in range(SHORT_CONV)) + b


def hyena_filters(L, w1, b1, w2, b2, w3, b3, freq, wout):
    f32 = jnp.float32
    t = jnp.linspace(0.0, 1.0, L, dtype=f32)[:, None]
    bands = (FILTER_EMB - 1) // 2
    w = 2.0 * math.pi * jnp.arange(L, dtype=f32)[:, None] / L
    f = jnp.linspace(1e-4, bands - 1, bands, dtype=f32)[None, :]
    z = jnp.concatenate([t, jnp.cos(f * w), -jnp.sin(f * w)], axis=-1)
    fr = freq.astype(f32)
    h = jnp.sin(fr * (z @ w1.astype(f32) + b1.astype(f32)))
    h = jnp.sin(fr * (h @ w2.astype(f32) + b2.astype(f32)))
    h = jnp.sin(fr * (h @ w3.astype(f32) + b3.astype(f32)))
    h = h @ wout.astype(f32)
    max_decay = math.log(FILTER_TARGET) / FILTER_FAST_DECAY
    min_decay = math.log(FILTER_TARGET) / FILTER_SLOW_DECAY
    deltas = jnp.abs(jnp.linspace(min_decay, max_decay, HYENA_WIDTH, dtype=f32))
    decay = jnp.exp(-t * deltas[None, :])
    return h.reshape(L, HYENA_ORDER, 2, HYENA_WIDTH) * decay[:, None, None, :]


def long_conv_bidir(u, h_fwd, h_bwd, bias):
    L = u.shape[1]
    n = 2 * L
    uf = u.astype(jnp.float32)
    hf = jnp.fft.rfft(h_fwd, n=n, axis=0)
    hb = jnp.fft.rfft(h_bwd, n=n, axis=0)
    y_f = jnp.fft.irfft(jnp.fft.rfft(uf, n=n, axis=1) * hf, n=n, axis=1)[:, :L]
    y_b = jnp.fft.irfft(jnp.fft.rfft(uf[:, ::-1], n=n, axis=1) * hb, n=n, axis=1)[:, :L][:, ::-1]
    return (y_f + y_b + uf * bias.astype(jnp.float32)).astype(u.dtype)


def hyena_operator(p, conv_w, conv_b, filt, bias):
    uc = short_conv_centred(p, conv_w, conv_b)
    *gates, z = jnp.split(uc, HYENA_ORDER + 1, axis=-1)
    for o, gate in enumerate(gates):
        z = gate * long_conv_bidir(z, filt[:, o, 0], filt[:, o, 1], bias[o])
    return z


def qk_heads(p):
    return p.reshape(p.shape[:2] + (DIFF_HEADS, 2, DIFF_HEAD_DIM))


def v_heads(p):
    return p.reshape(p.shape[:2] + (DIFF_HEADS, DIFF_V_DIM))


def diff_attend(q, k, v, lam):
    s = jnp.einsum('bqhmd,bkhmd->bhmqk', q, k).astype(jnp.float32) * (DIFF_HEAD_DIM ** -0.5)
    p = jax.nn.softmax(s, axis=-1)
    a = p[:, :, 0] - lam * p[:, :, 1]
    return jnp.einsum('bhqk,bkhe->bqhe', a.astype(v.dtype), v)


def latent_diff_attention(q, k_lat, v_lat, k_ctx, v_ctx, lam):
    B, S = q.shape[:2]
    k_all = jnp.concatenate([k_ctx, k_lat], axis=1)
    v_all = jnp.concatenate([v_ctx, v_lat], axis=1)
    nblk = S // ATTN_BLOCK
    qb = jnp.moveaxis(q.reshape((B, nblk, ATTN_BLOCK) + q.shape[2:]), 1, 0)
    o = lax.map(lambda qi: diff_attend(qi, k_all, v_all, lam), qb)
    return jnp.moveaxis(o, 0, 1).reshape((B, S) + o.shape[3:])


def diff_out(o, subln_g, lam_init):
    return (rms_norm(o, subln_g) * (1.0 - lam_init)).reshape(o.shape[:2] + (DIFF_WIDTH,))


def hybrid_mixer(a, ac, ctx_full, lam_init, row, col, w_in, w_out, conv_w, conv_b,
                 filt_params, hy_bias, q_g, k_g, lam_vec, subln_g):
    lv = lam_vec.astype(jnp.float32)
    lam = jnp.exp(jnp.sum(lv[0] * lv[1])) - jnp.exp(jnp.sum(lv[2] * lv[3])) + lam_init
    S = a.shape[1]
    q_off = HYENA_PROJ
    kv_off = HYENA_PROJ + DIFF_QK_WIDTH
    v_off = kv_off + DIFF_QK_WIDTH
    p = a @ w_in
    hy = hyena_operator(p[..., :HYENA_PROJ], conv_w, conv_b, hyena_filters(S, *filt_params), hy_bias)
    q = axial_rope(rms_norm(qk_heads(p[..., q_off:kv_off]), q_g), row, col)
    k = axial_rope(rms_norm(qk_heads(p[..., kv_off:v_off]), k_g), row, col)
    v = v_heads(p[..., v_off:])
    pc = ac @ (w_in if ctx_full else w_in[:, kv_off:])
    pc_kv = pc[..., kv_off:] if ctx_full else pc
    kc = rms_norm(qk_heads(pc_kv[..., :DIFF_QK_WIDTH]), k_g)
    vc = v_heads(pc_kv[..., DIFF_QK_WIDTH:])
    o = latent_diff_attention(q, k, v, kc, vc, lam)
    y = jnp.concatenate([hy, diff_out(o, subln_g, lam_init)], axis=-1) @ w_out
    if not ctx_full:
        return y, None
    Lc = ac.shape[1]
    hyc = hyena_operator(pc[..., :HYENA_PROJ], conv_w, conv_b, hyena_filters(Lc, *filt_params), hy_bias)
    qc = rms_norm(qk_heads(pc[..., q_off:kv_off]), q_g)
    oc = diff_attend(qc, kc, vc, lam)
    yc = jnp.concatenate([hyc, diff_out(oc, subln_g, lam_init)], axis=-1) @ w_out
    return y, yc


def multiscale_pool(h, pool_w, pool_scale):
    B, L, D = h.shape
    hf = h.astype(jnp.float32)
    cs = jnp.concatenate([jnp.zeros((B, 1, D), jnp.float32), jnp.cumsum(hf, axis=1)], axis=1)
    t = jnp.arange(L)
    groups = []
    for gi, w in enumerate(POOL_WINDOWS):
        lo = jnp.clip(t - w // 2, 0, L)
        hi = jnp.clip(t + w // 2, 0, L)
        sl = slice(gi * POOL_GROUP, (gi + 1) * POOL_GROUP)
        cg = cs[..., sl]
        cnt = (hi - lo).astype(jnp.float32)[None, :, None]
        mean = (jnp.take(cg, hi, axis=1) - jnp.take(cg, lo, axis=1)) / cnt
        groups.append(mean - hf[..., sl])
    p = jnp.stack(groups, axis=2).astype(h.dtype)
    y = jnp.einsum('blgc,gcd->blgd', p, pool_w).reshape(B, L, D)
    return y * pool_scale


def expert_choice_ffn(h, router_w, w_gate, w_up, w_down):
    B, n, D = h.shape
    cap = EC_CAPACITY * n // N_EXPERTS
    aff = jax.nn.softmax((h @ router_w).astype(jnp.float32), axis=-1)
    g, idx = lax.top_k(jnp.swapaxes(aff, 1, 2), cap)
    xs = jax.vmap(lambda hb, ib: hb[ib])(h, idx)
    a = jnp.einsum('becd,edf->becf', xs, w_gate)
    u = jnp.einsum('becd,edf->becf', xs, w_up)
    y = jnp.einsum('becf,efd->becd', jax.nn.silu(a) * u, w_down) * g[..., None].astype(h.dtype)
    return jax.vmap(lambda yb, ib: jnp.zeros((n, D), yb.dtype).at[ib.reshape(-1)].add(yb.reshape(-1, D)))(y, idx)


def setup_inputs(seed: int = 0) -> dict:
    key = jax.random.key(seed)
    ks = iter(jax.random.split(key, 32))

    def nrm(shape, scale):
        return jax.random.normal(next(ks), shape, jnp.float32) * scale

    D = D_MODEL
    n_even = (DEPTH + 1) // 2
    n_odd = DEPTH // 2
    return {
        'x': nrm((BATCH, SEQ, D), 1.0),
        'c': nrm((BATCH, D), 1.0),
        'ctx': nrm((BATCH, CTX_LEN, D), 1.0),
        'c_ctx': nrm((D,), 1.0),
        'ada_w': nrm((DEPTH, D, 6 * D), 0.5 * D ** -0.5),
        'ada_b': nrm((DEPTH, 6 * D), 0.02),
        'norm_mix_g': 1.0 + nrm((DEPTH, D), 0.02),
        'norm_ffn_g': 1.0 + nrm((DEPTH, D), 0.02),
        'w_in': nrm((n_even, D, IN_WIDTH), D ** -0.5),
        'w_out': nrm((n_even, MIX_WIDTH, D), MIX_WIDTH ** -0.5),
        'hy_conv_w': nrm((n_even, SHORT_CONV, HYENA_PROJ), SHORT_CONV ** -0.5),
        'hy_conv_b': nrm((n_even, HYENA_PROJ), 0.02),
        'hy_f_w1': nrm((n_even, FILTER_EMB, FILTER_HIDDEN), FILTER_EMB ** -0.5),
        'hy_f_b1': nrm((n_even, FILTER_HIDDEN), 0.02),
        'hy_f_w2': nrm((n_even, FILTER_HIDDEN, FILTER_HIDDEN), FILTER_HIDDEN ** -0.5),
        'hy_f_b2': nrm((n_even, FILTER_HIDDEN), 0.02),
        'hy_f_w3': nrm((n_even, FILTER_HIDDEN, FILTER_HIDDEN), FILTER_HIDDEN ** -0.5),
        'hy_f_b3': nrm((n_even, FILTER_HIDDEN), 0.02),
        'hy_f_freq': 1.0 + nrm((n_even, FILTER_HIDDEN), 0.02),
        'hy_f_wout': nrm((n_even, FILTER_HIDDEN, HYENA_ORDER * 2 * HYENA_WIDTH), 0.05 * FILTER_HIDDEN ** -0.5),
        'hy_bias': nrm((n_even, HYENA_ORDER, HYENA_WIDTH), 0.5),
        'q_norm_g': 1.0 + nrm((n_even, DIFF_HEAD_DIM), 0.02),
        'k_norm_g': 1.0 + nrm((n_even, DIFF_HEAD_DIM), 0.02),
        'diff_lambda': nrm((n_even, 4, DIFF_HEAD_DIM), 0.1),
        'subln_g': 1.0 + nrm((n_even, DIFF_V_DIM), 0.02),
        'pool_w': nrm((n_odd, len(POOL_WINDOWS), POOL_GROUP, POOL_GROUP), POOL_GROUP ** -0.5),
        'pool_scale': 1.0 + nrm((n_odd, D), 0.1),
        'router_w': nrm((DEPTH, D, N_EXPERTS), D ** -0.5),
        'exp_w_gate': nrm((DEPTH, N_EXPERTS, D, EXPERT_FF), D ** -0.5),
        'exp_w_up': nrm((DEPTH, N_EXPERTS, D, EXPERT_FF), D ** -0.5),
        'exp_w_down': nrm((DEPTH, N_EXPERTS, EXPERT_FF, D), EXPERT_FF ** -0.5),
    }


def reference(x, c, ctx, c_ctx, ada_w, ada_b, norm_mix_g, norm_ffn_g, w_in, w_out,
              hy_conv_w, hy_conv_b, hy_f_w1, hy_f_b1, hy_f_w2, hy_f_b2, hy_f_w3, hy_f_b3,
              hy_f_freq, hy_f_wout, hy_bias, q_norm_g, k_norm_g, diff_lambda, subln_g,
              pool_w, pool_scale, router_w, exp_w_gate, exp_w_up, exp_w_down):
    B, S, D = x.shape
    rows = S // GRID_W
    row = jnp.repeat(jnp.arange(rows, dtype=jnp.int32), GRID_W)
    col = jnp.tile(jnp.arange(GRID_W, dtype=jnp.int32), rows)
    last_attn = ((DEPTH - 1) // 2) * 2
    s_lat = jax.nn.silu(c)
    s_ctx = jax.nn.silu(c_ctx)
    h, hc = x, ctx
    for l in range(DEPTH):
        m = jnp.split((s_lat @ ada_w[l] + ada_b[l])[:, None, :], 6, axis=-1)
        mc = jnp.split(s_ctx @ ada_w[l] + ada_b[l], 6, axis=-1)
        ctx_full = l < last_attn
        a = rms_norm(h, norm_mix_g[l]) * (1 + m[1]) + m[0]
        ac = rms_norm(hc, norm_mix_g[l]) * (1 + mc[1]) + mc[0] if l <= last_attn else None
        if l % 2 == 0:
            e = l // 2
            lam_init = 0.8 - 0.6 * math.exp(-0.3 * l)
            filt_params = (hy_f_w1[e], hy_f_b1[e], hy_f_w2[e], hy_f_b2[e], hy_f_w3[e], hy_f_b3[e],
                           hy_f_freq[e], hy_f_wout[e])
            y, yc = hybrid_mixer(a, ac, ctx_full, lam_init, row, col, w_in[e], w_out[e],
                                 hy_conv_w[e], hy_conv_b[e], filt_params, hy_bias[e],
                                 q_norm_g[e], k_norm_g[e], diff_lambda[e], subln_g[e])
        else:
            o = l // 2
            y = multiscale_pool(a, pool_w[o], pool_scale[o])
            yc = multiscale_pool(ac, pool_w[o], pool_scale[o]) if ctx_full else None
        h = h + m[2] * y
        f_in = rms_norm(h, norm_ffn_g[l]) * (1 + m[4]) + m[3]
        h = h + m[5] * expert_choice_ffn(f_in, router_w[l], exp_w_gate[l], exp_w_up[l], exp_w_down[l])
        if ctx_full:
            hc = hc + mc[2] * yc
            fc_in = rms_norm(hc, norm_ffn_g[l]) * (1 + mc[4]) + mc[3]
            hc = hc + mc[5] * expert_choice_ffn(fc_in, router_w[l], exp_w_gate[l], exp_w_up[l], exp_w_down[l])
    return h
```

```python
import math
from contextlib import ExitStack
import numpy as np
import ml_dtypes
import concourse.bass as bass
import concourse.mybir as mybir
from concourse.bass_utils import run_bass_kernel_spmd

F32 = mybir.dt.float32
BF16 = mybir.dt.bfloat16
I32 = mybir.dt.int32
U32 = mybir.dt.uint32
ALU = mybir.AluOpType
AF = mybir.ActivationFunctionType
AX = mybir.AxisListType

D = 1024
S = 4096
CT = 256
T = S + CT
NT = T // 128
NLT = S // 128
DEPTH = 4
NE = 16
FF = 2048
EPS = 1e-6
SEM_CH = 20000


TRACKED = set()


def tokname(ap):
    n = ap.tensor.name
    return n if n in TRACKED else None


class Prog:
    ENGS = ['pe', 'act', 'dve', 'pool', 'sp']

    def __init__(self, nc):
        self.nc = nc
        self.ops = {e: [] for e in self.ENGS}
        self.last_w = {}
        self.reads = {}
        self.seq = {e: 0 for e in self.ENGS}
        self.dseq = {}
        self.latest = {}
        self.needed = set()

    def _deps(self, r, w, eng=None):
        deps = {}
        def add(sig):
            if sig is None:
                return
            k, v = sig
            if deps.get(k, 0) < v:
                deps[k] = v
        for t in r:
            add(self.last_w.get(t))
            if isinstance(t, str) and t.startswith('ps'):
                for k, v in self.reads.get(t, {}).items():
                    if k != eng:
                        add((k, v))
        for t in w:
            add(self.last_w.get(t))
            for k, v in self.reads.get(t, {}).items():
                add((k, v))
        return deps

    def _update(self, r, w, sig):
        k, v = sig
        for t in r:
            d = self.reads.setdefault(t, {})
            if d.get(k, 0) < v:
                d[k] = v
        for t in w:
            self.last_w[t] = sig
            self.reads[t] = {}
        self.latest[k] = v

    def op(self, eng, fn, r=(), w=()):
        self.seq[eng] += 1
        sig = (eng, self.seq[eng])
        deps = self._deps(r, w, eng)
        self._update(r, w, sig)
        for kv in deps.items():
            self.needed.add(kv)
        self.ops[eng].append(dict(fn=fn, deps=deps, sig=sig, kind='c'))

    def dma(self, q, fn, r=(), w=(), stream='d', nbuf=2):
        i = self.dseq.get(stream, 0)
        self.dseq[stream] = i + 1
        key = ('d', stream, i % nbuf)
        val = i // nbuf + 1
        deps = self._deps(r, w)
        if val > 1:
            if deps.get(key, 0) < val - 1:
                deps[key] = val - 1
        sig = (key, val)
        self._update(r, w, sig)
        for kv in deps.items():
            self.needed.add(kv)
        self.ops[q].append(dict(fn=fn, deps=deps, sig=sig, kind='d'))

    def barrier(self):
        deps = dict(self.latest)
        for kv in deps.items():
            self.needed.add(kv)
        for e in self.ENGS:
            self.ops[e].append(dict(fn=None, deps=dict(deps), sig=None, kind='b'))
        self.last_w = {}
        self.reads = {}

    def emit(self, es):
        nc = self.nc
        inc_idx = {}
        nsem = {}
        for e in self.ENGS:
            n = 0
            for o in self.ops[e]:
                if o['kind'] == 'c' and o['sig'] in self.needed:
                    n += 1
                    inc_idx[o['sig']] = n
            nsem[e] = (n + SEM_CH - 1) // SEM_CH
        sems = {}
        for e in self.ENGS:
            sems[e] = [es.enter_context(nc.semaphore(f"s_{e}_{i}")) for i in range(nsem[e])]
        dsems = {}
        for e in self.ENGS:
            for o in self.ops[e]:
                if o['kind'] == 'd':
                    k = o['sig'][0]
                    if k not in dsems:
                        dsems[k] = es.enter_context(nc.semaphore(f"d_{k[1]}_{k[2]}"))
        self.n_sems = sum(nsem.values()) + len(dsems)

        def resolve(k, v):
            if isinstance(k, tuple):
                return dsems[k], 16 * v
            n = inc_idx[(k, v)]
            return sems[k][(n - 1) // SEM_CH], (n - 1) % SEM_CH + 1

        def run(eng_name, eng):
            known = {}
            for o in self.ops[eng_name]:
                for k, v in o['deps'].items():
                    if k == 'pe' and eng_name == 'pe':
                        continue
                    if known.get(k, 0) >= v:
                        continue
                    known[k] = v
                    s, val = resolve(k, v)
                    eng.wait_ge(s, val)
                if o['fn'] is None:
                    continue
                ins = o['fn'](eng)
                if o['kind'] == 'd':
                    s, _ = resolve(*o['sig'])
                    ins.then_inc(s, 16)
                elif o['sig'] in inc_idx:
                    n = inc_idx[o['sig']]
                    ins.then_inc(sems[eng_name][(n - 1) // SEM_CH], 1)

        with nc.Block() as block:
            @block.tensor
            def _(e):
                run('pe', e)

            @block.scalar
            def _(e):
                run('act', e)

            @block.vector
            def _(e):
                run('dve', e)

            @block.gpsimd
            def _(e):
                run('pool', e)

            @block.sync
            def _(e):
                run('sp', e)

    def _rw(self, r, w, ins, outs):
        if r is None:
            r = [tokname(a) for a in ins if a is not None and not isinstance(a, (int, float))]
        if w is None:
            w = [tokname(a) for a in outs]
        r = [t for t in r if t is not None]
        w = [t for t in w if t is not None]
        return r, w

    def mm(self, out, lhsT, rhs, start=True, stop=True, r=None, w=None):
        r, w = self._rw(r, w, [lhsT, rhs], [out])
        self.op('pe', lambda e: e.matmul(out, lhsT, rhs, start=start, stop=stop), r, w)

    def tr(self, out, in_, ident, r=None, w=None):
        r, w = self._rw(r, w, [in_, ident], [out])
        self.op('pe', lambda e: e.transpose(out, in_, ident), r, w)

    def act(self, out, in_, func, bias=None, scale=None, accum_out=None, r=None, w=None):
        ins = [in_]
        if bias is not None and not isinstance(bias, (int, float)):
            ins.append(bias)
        if scale is not None and not isinstance(scale, (int, float)):
            ins.append(scale)
        outs = [out] + ([accum_out] if accum_out is not None else [])
        r, w = self._rw(r, w, ins, outs)
        kw = {}
        if bias is not None:
            kw['bias'] = bias
        if scale is not None:
            kw['scale'] = scale
        if accum_out is not None:
            kw['accum_out'] = accum_out
        self.op('act', lambda e: e.activation(out, in_, func, **kw), r, w)

    def tt(self, eng, out, in0, in1, op, r=None, w=None):
        r, w = self._rw(r, w, [in0, in1], [out])
        self.op(eng, lambda e: e.tensor_tensor(out, in0, in1, op), r, w)

    def ts(self, eng, out, in0, s1, s2, op0, op1=None, accum_out=None, r=None, w=None):
        ins = [in0] + [s for s in (s1, s2) if s is not None and not isinstance(s, (int, float))]
        outs = [out] + ([accum_out] if accum_out is not None else [])
        r, w = self._rw(r, w, ins, outs)
        kw = {}
        if op1 is not None:
            kw['op1'] = op1
        if accum_out is not None:
            kw['accum_out'] = accum_out
        self.op(eng, lambda e: e.tensor_scalar(out, in0, s1, s2, op0, **kw), r, w)

    def stt(self, out, in0, scalar, in1, op0, op1, r=None, w=None):
        ins = [in0, in1] + ([scalar] if not isinstance(scalar, (int, float)) else [])
        r, w = self._rw(r, w, ins, [out])
        self.op('dve', lambda e: e.scalar_tensor_tensor(out, in0, scalar, in1, op0, op1), r, w)

    def copy(self, eng, out, in_, r=None, w=None):
        r, w = self._rw(r, w, [in_], [out])
        if eng == 'act':
            self.op(eng, lambda e: e.copy(out, in_), r, w)
        else:
            self.op(eng, lambda e: e.tensor_copy(out, in_), r, w)

    def memset(self, eng, ap, val, w=None):
        _, w = self._rw([], w, [], [ap])
        self.op(eng, lambda e: e.memset(ap, val), [], w)

    def recip(self, out, in_, r=None, w=None):
        r, w = self._rw(r, w, [in_], [out])
        self.op('dve', lambda e: e.reciprocal(out, in_), r, w)

    def ld(self, q, out, in_, stream, nbuf=2, r=None, w=None, **kw):
        r, w = self._rw(r, w, [in_], [out])
        self.dma(q, lambda e: e.dma_start(out=out, in_=in_, **kw), r, w, stream, nbuf)


class Builder:
    def __init__(self, n_layers=DEPTH, dbg=()):
        self.nc = nc = bass.Bass("TRN2", target_bir_lowering=False)
        self.P = Prog(nc)
        self.n_layers = n_layers
        self.dbg = set(dbg)
        self.dram = {}
        self.es = ExitStack()

    def din(self, name, shape, dt=F32):
        t = self.nc.dram_tensor(name, list(shape), dt, kind="ExternalInput")
        self.dram[name] = t
        return t.ap()

    def dout(self, name, shape, dt=F32):
        t = self.nc.dram_tensor(name, list(shape), dt, kind="ExternalOutput")
        self.dram[name] = t
        return t.ap()

    def dscr(self, name, shape, dt=F32):
        kind = "ExternalOutput" if name in self.dbg else "Internal"
        t = self.nc.dram_tensor(name, list(shape), dt, kind=kind)
        self.dram[name] = t
        return t.ap()

    def sb(self, stack, name, shape, dt=F32):
        self._uid = getattr(self, '_uid', 0) + 1
        name = f"{name}_u{self._uid}"
        TRACKED.add(name)
        return stack.enter_context(self.nc.sbuf_tensor(name, list(shape), dt))

    def build(self):
        nc, P = self.nc, self.P
        self.x = self.din("x", [S, D])
        self.ctx = self.din("ctx", [CT, D])
        self.cT = self.din("cT", [128, 8, 2])
        self.ada_w = self.din("ada_w", [DEPTH, D, 6 * D])
        self.ada_b = self.din("ada_b", [DEPTH, 6 * D])
        self.norm_g = self.din("norm_g", [DEPTH, 2, D])
        self.ident_in = self.din("ident", [128, 128])
        self.w_in = self.din("w_in", [2, D, 3072])
        self.convw = self.din("convw", [2, 128, 12, 4])
        self.qkg = self.din("qkg", [2, 128, 2])
        self.rope_cos = self.din("rope_cos", [128, T])
        self.rope_sin = self.din("rope_sin", [128, T])
        self.rope_perm = self.din("rope_perm", [128, 128], BF16)
        self.blockones_in = self.din("blockones", [128, 128], BF16)
        self.router_w = self.din("router_w", [DEPTH, D, NE])
        self.tokhl_in = self.din("tokhl", [128, NT, 2])
        self.w_gate = self.din("w_gate", [DEPTH, NE, D, FF])
        self.w_up = self.din("w_up", [DEPTH, NE, D, FF])
        self.w_down = self.din("w_down", [DEPTH, NE, FF, D])
        self.diff_lambda = self.din("diff_lambda", [2, 4, 64])
        self.subln_g = self.din("subln_g", [2, 128])
        self.hy_f_w1 = self.din("hy_f_w1", [2, 33, 64])
        self.hy_f_w2 = self.din("hy_f_w2", [2, 64, 64])
        self.hy_f_w3 = self.din("hy_f_w3", [2, 64, 64])
        self.hy_f_wout = self.din("hy_f_wout", [2, 64, 2048])
        self.hy_fvec = self.din("hy_fvec", [2, 64, 4])
        self.hy_bias = self.din("hy_bias", [2, 2, 512])
        self.zT = self.din("zT", [33, S])
        self.zTc = self.din("zTc", [33, CT])
        self.decay = self.din("decay", [S, 512])
        self.decayc = self.din("decayc", [CT, 512])
        for nm in ("TFC", "TFS", "TIC", "TIS"):
            setattr(self, nm, self.din(nm, [S // 128, 128, S // 128, 128], BF16))
            setattr(self, nm + "c", self.din(nm + "c", [CT // 128, 128, CT // 128, 128], BF16))
        self.w_out = self.din("w_out", [2, D, D])
        self.pool_w = self.din("pool_w", [2, 4, 256, 256])
        self.pool_scale = self.din("pool_scale", [2, D])
        self.poolmt = self.din("poolmt", [128, 4, 5, 128], BF16)
        self.out = self.dout("y", [S, D])
        self.hc = self.dscr("hc", [CT, D])
        self.MOD = self.dscr("MOD", [2, DEPTH * 6 * D])

        top = self.es
        self.ps = [top.enter_context(nc.psum_tensor(f"ps{i}", [128, 512], F32)) for i in range(8)]
        for i in range(8):
            TRACKED.add(f"ps{i}")
        self.ident = self.sb(top, "identf", [128, 128], F32)
        self.identb = self.sb(top, "identb", [128, 128], BF16)
        P.ld('sp', self.ident[:], self.ident_in, 'const', 1)
        P.copy('dve', self.identb[:], self.ident[:])
        self.epsc = self.sb(top, "epsc", [128, 1], F32)
        P.memset('dve', self.epsc[:], EPS)

        self.prologue()
        P.barrier()
        for l in range(self.n_layers):
            self.layer(l)
        P.barrier()
        P.emit(self.es)
        self.es.close()
        return nc

    def prologue(self):
        nc, P = self.nc, self.P
        with ExitStack() as st:
            cT = self.sb(st, "cTs", [128, 8, 2])
            sT = self.sb(st, "sTs", [128, 8, 2])
            wt = [self.sb(st, f"adaw{i}", [128, 8, 512]) for i in range(2)]
            bias = [self.sb(st, f"adabs{i}", [2, 6 * D]) for i in range(2)]
            gsb = [self.sb(st, f"gsb{i}", [2, 2 * D]) for i in range(2)]
            mod = [self.sb(st, f"modsb{i}", [2, 6 * D]) for i in range(2)]
            P.ld('sp', cT[:], self.cT, 'const', 1)
            P.act(sT[:], cT[:], AF.Silu)
            n = 0
            for l in range(DEPTH):
                bi, gs, mo = bias[l % 2], gsb[l % 2], mod[l % 2]
                P.ld('sp', bi[:], self.ada_b[l:l + 1, :].partition_broadcast(2), 'pro_b', 2)
                P.ld('sp', gs[:], self.norm_g[l:l + 1].rearrange("o k n -> o (k n)").partition_broadcast(2), 'pro_g', 2)
                for cc in range(12):
                    w = wt[n % 2]
                    P.ld('sp', w[:], self.ada_w[l, :, cc * 512:(cc + 1) * 512].rearrange("(j p) n -> p j n", p=128),
                         'adaw', 2)
                    pst = self.ps[n % 2]
                    for j in range(8):
                        P.mm(pst[0:2, :], sT[:, j, :], w[:, j, :], start=(j == 0), stop=(j == 7))
                    c0 = cc * 512
                    P.tt('dve', mo[:, c0:c0 + 512], pst[0:2, :], bi[:, c0:c0 + 512], ALU.add)
                    n += 1
                for v, k in ((1, 0), (4, 1)):
                    P.stt(mo[:, v * D:(v + 1) * D], mo[:, v * D:(v + 1) * D], 1.0, gs[:, k * D:(k + 1) * D],
                          ALU.add, ALU.mult)
                P.ld('sp', self.MOD[:, l * 6 * D:(l + 1) * 6 * D], mo[:], 'pro_st', 2)

    def modrow(self, l, s, v):
        c0 = l * 6 * D + v * D
        return self.MOD[s:s + 1, c0:c0 + D].partition_broadcast(128)

    def layer(self, l):
        P = self.P
        ctx_mode = ['full', 'full', 'kv', 'none'][l]
        if 'moe_only' in self.dbg:
            self.mixer_passthrough(l, ctx_mode)
        elif l % 2 == 0:
            self.even_proj(l)
            P.barrier()
            if 'no_hyena' not in self.dbg:
                self.hyena_filters(l, S)
                P.barrier()
                self.hyena_conv(l, S, 0)
                P.barrier()
                if ctx_mode == 'full':
                    self.hyena_filters(l, CT)
                    P.barrier()
                    self.hyena_conv(l, CT, S)
                    P.barrier()
            if 'no_attn' not in self.dbg:
                self.even_attn(l, ctx_mode)
            P.barrier()
            self.even_wout(l, ctx_mode)
        else:
            self.odd_pool(l, ctx_mode)
        P.barrier()
        if 'no_moe' in self.dbg:
            return
        self.moe_topk(l, ctx_mode)
        P.barrier()
        self.moe_experts(l, ctx_mode)
        P.barrier()

    def hsrc(self, l, j):
        if j < NLT:
            t = self.x if l == 0 else self.out
            return t[j * 128:(j + 1) * 128, :]
        t = self.ctx if l == 0 else self.hc
        return t[(j - NLT) * 128:(j - NLT + 1) * 128, :]

    def hdst(self, j):
        if j < NLT:
            return self.out[j * 128:(j + 1) * 128, :]
        return self.hc[(j - NLT) * 128:(j - NLT + 1) * 128, :]

    def norm_mod(self, st_bufs, h, a_out, G, Bv, n):
        P = self.P
        junk, ss, sd, rstd, tmp = st_bufs
        k = n % 2
        P.act(junk[k][:], h, AF.Square, accum_out=ss[k][:])
        P.act(sd[k][:], ss[k][:], AF.Sqrt, scale=1.0 / D, bias=self.epsc[:])
        P.recip(rstd[k][:], sd[k][:])
        P.stt(tmp[k][:], h, rstd[k][:], G, ALU.mult, ALU.mult)
        P.tt('dve', a_out, tmp[k][:], Bv, ALU.add)

    def alloc_norm_bufs(self, st):
        junk = [self.sb(st, f"nm_junk{i}", [128, D], BF16) for i in range(2)]
        ss = [self.sb(st, f"nm_ss{i}", [128, 1]) for i in range(2)]
        sd = [self.sb(st, f"nm_sd{i}", [128, 1]) for i in range(2)]
        rstd = [self.sb(st, f"nm_rstd{i}", [128, 1]) for i in range(2)]
        tmp = [self.sb(st, f"nm_tmp{i}", [128, D]) for i in range(2)]
        return junk, ss, sd, rstd, tmp

    def load_modrows(self, st, l, vs, with_ctx, tag):
        P = self.P
        rows = {}
        for s_ in ([0, 1] if with_ctx else [0]):
            for v in vs:
                t = self.sb(st, f"mr_{tag}_{s_}_{v}", [128, D])
                P.ld('sp', t[:], self.modrow(l, s_, v), 'modrow', 1)
                rows[(s_, v)] = t
        return rows

    def even_proj(self, l):
        nc, P = self.nc, self.P
        e = l // 2
        ctx_mode = ['full', 'full', 'kv', 'none'][l]
        ntiles = NT if ctx_mode != 'none' else NLT
        with ExitStack() as st:
            AT = self.sb(st, "AT", [128, 8, T], BF16)
            Win = self.sb(st, "Win", [128, 8, 3072], BF16)
            for g in range(6):
                for j in range(8):
                    P.ld('pool', Win[:, j, g * 512:(g + 1) * 512],
                         self.w_in[e, j * 128:(j + 1) * 128, g * 512:(g + 1) * 512], 'winld', 2,
                         w=[('Win', g)])
            with ExitStack() as st2:
                rows = self.load_modrows(st2, l, [0, 1], ctx_mode != 'none', 'mix')
                nb = self.alloc_norm_bufs(st2)
                hb = [self.sb(st2, f"hb{i}", [128, D]) for i in range(2)]
                ab = [self.sb(st2, f"ab{i}", [128, D], BF16) for i in range(2)]
                for j in range(ntiles):
                    s_ = 0 if j < NLT else 1
                    k = j % 2
                    P.ld('sp', hb[k][:], self.hsrc(l, j), 'hld', 2)
                    self.norm_mod(nb, hb[k][:], ab[k][:], rows[(s_, 1)][:], rows[(s_, 0)][:], j)
                    pst = self.ps[j % 2]
                    pb = pst[:].bitcast(BF16)
                    for c in range(8):
                        P.tr(pb[:, c * 128:(c + 1) * 128], ab[k][:, c * 128:(c + 1) * 128], self.identb[:])
                    src = pb.rearrange("p (c t) -> p c t", c=8)
                    if j % 2 == 0:
                        P.copy('act', AT[:, :, j * 128:(j + 1) * 128], src, w=[('AT', j)])
                    else:
                        P.copy('dve', AT[:, :, j * 128:(j + 1) * 128], src, w=[('AT', j)])
                if 'ATd' in self.dbg:
                    P.ld('sp', self.dscr("ATd", [128, 8, T], BF16), AT[:], 'dbg', 1, r=[('AT', j) for j in range(ntiles)])
            P.barrier()
            self.proj_hyena(l, st, AT, Win, ctx_mode)
            P.barrier()

    def proj_hyena(self, l, st, AT, Win, ctx_mode):
        nc, P = self.nc, self.P
        e = l // 2
        segs = [(0, S)] + ([(S, CT)] if ctx_mode == 'full' else [])
        X2T = self.dscr(f"X2T", [512, T]) if not hasattr(self, 'X2T') else self.X2T
        self.X2T = X2T
        if not hasattr(self, 'X1'):
            self.X1 = self.dscr("X1", [T, 512])
            self.X2 = self.dscr("X2", [T, 512])
            self.VH = self.dscr("VH", [T, 512], BF16)
            self.QT = self.dscr("QT", [4, 128, T], BF16)
            self.KT = self.dscr("KT", [4, 128, T], BF16)
            self.VA = self.dscr("VA", [T, 512], BF16)
        with ExitStack() as st2:
            cw = self.sb(st2, "convw", [128, 12, 4])
            P.ld('sp', cw[:], self.convw[e], 'const', 1)
            PT = [self.sb(st2, f"PT{i}", [128, S + 2]) for i in range(1)]
            UC = [self.sb(st2, f"UC{i}", [128, S]) for i in range(2)]
            UCb = self.sb(st2, "UCb", [128, S], BF16)
            stg = [self.sb(st2, f"stg{i}", [128, 32, 128]) for i in range(1)]
            stgb = [self.sb(st2, f"stgb{i}", [128, 32, 128], BF16) for i in range(1)]
            n = 0
            npsum = 0
            for cc in range(12):
                for (t0, L) in segs:
                    k = n % 2
                    pt, uc = PT[0], UC[k]
                    P.memset('pool', pt[:, 0:1], 0.0)
                    P.memset('pool', pt[:, L + 1:L + 2], 0.0)
                    nb = (L + 511) // 512
                    for tb in range(nb):
                        w_ = min(512, L - tb * 512)
                        pst = self.ps[npsum % 4]
                        npsum += 1
                        c0 = t0 + tb * 512
                        for kk in range(8):
                            P.mm(pst[:, 0:w_], Win[:, kk, cc * 128:(cc + 1) * 128], AT[:, kk, c0:c0 + w_],
                                 start=(kk == 0), stop=(kk == 7),
                                 r=[('Win', cc // 4)] + [('AT', c0 // 128 + i) for i in range(w_ // 128)])
                        if tb % 2 == 0:
                            P.copy('act', pt[:, 1 + tb * 512:1 + tb * 512 + w_], pst[:, 0:w_])
                        else:
                            P.copy('dve', pt[:, 1 + tb * 512:1 + tb * 512 + w_], pst[:, 0:w_])
                    P.ts('dve', uc[:, 0:L], pt[:, 1:L + 1], cw[:, cc, 1:2], cw[:, cc, 3:4], ALU.mult, ALU.add)
                    P.stt(uc[:, 0:L], pt[:, 0:L], cw[:, cc, 0:1], uc[:, 0:L], ALU.mult, ALU.add)
                    P.stt(uc[:, 0:L], pt[:, 2:L + 2], cw[:, cc, 2:3], uc[:, 0:L], ALU.mult, ALU.add)
                    grp = cc // 4
                    ci = cc % 4
                    if grp in (0, 1):
                        Xd = self.X1 if grp == 0 else self.X2
                        sg = stg[0]
                        nt_ = L // 128
                        for q4 in range((nt_ + 3) // 4):
                            pst = self.ps[4 + (npsum % 4)]
                            npsum += 1
                            m4 = min(4, nt_ - q4 * 4)
                            for i in range(m4):
                                tt_ = q4 * 4 + i
                                P.tr(pst[:, i * 128:(i + 1) * 128], uc[:, tt_ * 128:(tt_ + 1) * 128], self.ident[:])
                            P.copy('act', sg[:, q4 * 4:q4 * 4 + m4, :], pst[:, 0:m4 * 128].rearrange("p (a c) -> p a c", a=m4))
                        for a0 in range(0, nt_, 4):
                            a1 = min(nt_, a0 + 4)
                            P.ld('sp', Xd[t0 + a0 * 128:t0 + a1 * 128, ci * 128:(ci + 1) * 128]
                                 .rearrange("(a p) c -> p a c", p=128), sg[:, a0:a1, :], 'x1st', 2)
                    else:
                        sg = stgb[0]
                        nt_ = L // 128
                        P.copy('act', UCb[:, 0:L], uc[:, 0:L])
                        for q8 in range((nt_ + 7) // 8):
                            pst = self.ps[4 + (npsum % 4)]
                            npsum += 1
                            pb = pst[:].bitcast(BF16)
                            m_ = min(8, nt_ - q8 * 8)
                            for i in range(m_):
                                tt_ = q8 * 8 + i
                                P.tr(pb[:, i * 128:(i + 1) * 128], UCb[:, tt_ * 128:(tt_ + 1) * 128], self.identb[:])
                            P.copy('act', sg[:, q8 * 8:q8 * 8 + m_, :],
                                   pb[:, 0:m_ * 128].rearrange("p (a c) -> p a c", a=m_))
                        for a0 in range(0, nt_, 4):
                            a1 = min(nt_, a0 + 4)
                            P.ld('sp', self.VH[t0 + a0 * 128:t0 + a1 * 128, ci * 128:(ci + 1) * 128]
                                 .rearrange("(a p) c -> p a c", p=128), sg[:, a0:a1, :], 'vhst', 2)
                    n += 1
        P.barrier()
        if 'skip_qk' not in self.dbg:
            self.proj_qk(l, AT, Win, ctx_mode)

    def proj_qk(self, l, AT, Win, ctx_mode):
        nc, P = self.nc, self.P
        e = l // 2
        with ExitStack() as st2:
            cosT = self.sb(st2, "cosT", [128, T])
            sinT = self.sb(st2, "sinT", [128, T])
            P.ld('sp', cosT[:], self.rope_cos, 'const', 1)
            P.ld('sp', sinT[:], self.rope_sin, 'const', 1)
            Rm = self.sb(st2, "Rm", [128, 128], BF16)
            bo = self.sb(st2, "blockones", [128, 128], BF16)
            P.ld('sp', Rm[:], self.rope_perm, 'const', 1)
            P.ld('sp', bo[:], self.blockones_in, 'const', 1)
            qkg = self.sb(st2, "qkg", [128, 2])
            P.ld('sp', qkg[:], self.qkg[e], 'const', 1)
            q32 = [self.sb(st2, f"q32_{i}", [128, 512]) for i in range(2)]
            sq = [self.sb(st2, f"sq_{i}", [128, 512], BF16) for i in range(2)]
            sd = [self.sb(st2, f"qsd_{i}", [128, 512]) for i in range(2)]
            rs = [self.sb(st2, f"qrs_{i}", [128, 512]) for i in range(2)]
            qn = [self.sb(st2, f"qn_{i}", [128, 512]) for i in range(2)]
            qnb = [self.sb(st2, f"qnb_{i}", [128, 512], BF16) for i in range(2)]
            t1 = [self.sb(st2, f"qt1_{i}", [128, 512]) for i in range(2)]
            t2 = [self.sb(st2, f"qt2_{i}", [128, 512]) for i in range(2)]
            qo = [self.sb(st2, f"qo_{i}", [128, T], BF16) for i in range(2)]
            n = 0
            for which in ((0, 1) if 'skip_qkloop' not in self.dbg else ()):
                if which == 0:
                    segs = [(0, S)] + ([(S, CT)] if ctx_mode == 'full' else [])
                else:
                    segs = [(0, S)] + ([(S, CT)] if ctx_mode in ('full', 'kv') else [])
                dst = self.QT if which == 0 else self.KT
                for h in range(4):
                    col0 = 1536 + which * 512 + h * 128
                    qout = qo[(which * 4 + h) % 2]
                    tend = 0
                    for (t0, L) in segs:
                        for tb in range((L + 511) // 512):
                            w_ = min(512, L - tb * 512)
                            c0 = t0 + tb * 512
                            k = n % 2
                            pA, pB, pC = self.ps[(n % 2) * 3], self.ps[(n % 2) * 3 + 1], self.ps[(n % 2) * 3 + 2]
                            for kk in range(8):
                                P.mm(pA[:, 0:w_], Win[:, kk, col0:col0 + 128], AT[:, kk, c0:c0 + w_],
                                     start=(kk == 0), stop=(kk == 7),
                                     r=[('Win', col0 // 512)] + [('AT', c0 // 128 + i) for i in range(w_ // 128)])
                            import os
                            lim = int(os.environ.get('QKSTEP', 99))
                            if lim >= 2: P.act(sq[k][:, 0:w_], pA[:, 0:w_], AF.Square)
                            if lim >= 3: P.copy('dve', q32[k][:, 0:w_], pA[:, 0:w_])
                            if lim >= 4: P.mm(pB[:, 0:w_], bo[:], sq[k][:, 0:w_])
                            if lim >= 5: P.act(sd[k][:, 0:w_], pB[:, 0:w_], AF.Sqrt, scale=1.0 / 64, bias=self.epsc[:])
                            if lim >= 6: P.recip(rs[k][:, 0:w_], sd[k][:, 0:w_])
                            if lim >= 7: P.stt(qn[k][:, 0:w_], q32[k][:, 0:w_], qkg[:, which:which + 1], rs[k][:, 0:w_],
                                  ALU.mult, ALU.mult)
                            if lim >= 8: P.copy('act', qnb[k][:, 0:w_], qn[k][:, 0:w_])
                            if lim >= 9: P.mm(pC[:, 0:w_], Rm[:], qnb[k][:, 0:w_])
                            if lim >= 10: P.tt('dve', t1[k][:, 0:w_], qn[k][:, 0:w_], cosT[:, c0:c0 + w_], ALU.mult)
                            if lim >= 11: P.tt('dve', t2[k][:, 0:w_], pC[:, 0:w_], sinT[:, c0:c0 + w_], ALU.mult)
                            if lim >= 12: P.tt('dve', qout[:, c0:c0 + w_], t1[k][:, 0:w_], t2[k][:, 0:w_], ALU.add)
                            n += 1
                        tend = t0 + L
                    if lim >= 13: P.ld('sp', dst[h, :, 0:tend], qout[:, 0:tend], 'qst', 2)
        P.barrier()
        if 'skip_va' in self.dbg:
            return
        with ExitStack() as st2:
            vst = [self.sb(st2, f"vst{i}", [128, 512], BF16) for i in range(2)]
            ntiles = NT if ctx_mode in ('full', 'kv') else NLT
            for j in range(ntiles):
                pst = self.ps[j % 2]
                for kk in range(8):
                    P.mm(pst[:], AT[:, kk, j * 128:(j + 1) * 128], Win[:, kk, 2560:3072],
                         start=(kk == 0), stop=(kk == 7), r=[('Win', 5), ('AT', j)])
                if j % 2 == 0:
                    P.copy('act', vst[j % 2][:], pst[:])
                else:
                    P.copy('dve', vst[j % 2][:], pst[:])
                P.ld('sp', self.VA[j * 128:(j + 1) * 128, :], vst[j % 2][:], 'vast', 2)


    def sin_block(self, B_, out, ps, A, Bc, w_):
        P = self.P
        y, yi, yf, m1, m2 = B_
        P.ts('dve', y[:, 0:w_], ps, A, Bc, ALU.mult, ALU.add)
        P.copy('dve', yi[:, 0:w_], y[:, 0:w_])
        P.copy('dve', yf[:, 0:w_], yi[:, 0:w_])
        P.tt('dve', y[:, 0:w_], y[:, 0:w_], yf[:, 0:w_], ALU.subtract)
        P.ts('dve', m1[:, 0:w_], y[:, 0:w_], 0.5, None, ALU.is_gt)
        P.ts('dve', m2[:, 0:w_], y[:, 0:w_], -0.5, None, ALU.is_lt)
        P.tt('dve', y[:, 0:w_], y[:, 0:w_], m1[:, 0:w_], ALU.subtract)
        P.tt('dve', y[:, 0:w_], y[:, 0:w_], m2[:, 0:w_], ALU.add)
        P.act(out, y[:, 0:w_], AF.Sin, scale=2.0 * math.pi * (1.0 - 1e-6))

    def dft_forward(self, st, L, tabs, srcA, srcB, evac):
        P = self.P
        TC, TS, TFC, TFS = tabs
        nch = L // 128
        for fc in range(nch):
            tc_, ts_ = TC[fc % 2], TS[fc % 2]
            P.ld('sp', tc_[:, 0:nch, :], TFC[fc], 'tabc', 2)
            P.ld('sp', ts_[:, 0:nch, :], TFS[fc], 'tabs', 2)
            pA, pB = self.ps[(fc % 2) * 2], self.ps[(fc % 2) * 2 + 1]
            for sc in range(nch):
                P.mm(pA[:], tc_[:, sc, :], srcA[:, sc, :], start=(sc == 0), stop=(sc == nch - 1))
            for sc in range(nch):
                P.mm(pB[:], ts_[:, sc, :], srcB[:, sc, :], start=(sc == 0), stop=(sc == nch - 1))
            evac(fc, pA, pB)

    def hyena_tables(self, L):
        sfx = "" if L == S else "c"
        return (getattr(self, "TFC" + sfx), getattr(self, "TFS" + sfx), getattr(self, "TIC" + sfx), getattr(self, "TIS" + sfx))

    def hyena_filters(self, l, L):
        P = self.P
        e = l // 2
        sfx = "" if L == S else "c"
        nch = L // 128
        if not hasattr(self, 'KR' + sfx):
            setattr(self, 'KR' + sfx, self.dscr('KR' + sfx, [2, L, 512]))
            setattr(self, 'KI' + sfx, self.dscr('KI' + sfx, [2, L, 512]))
        KR, KI = getattr(self, 'KR' + sfx), getattr(self, 'KI' + sfx)
        zTd = self.zT if L == S else self.zTc
        decd = self.decay if L == S else self.decayc
        TFC, TFS, _, _ = self.hyena_tables(L)
        with ExitStack() as st:
            zT = self.sb(st, "zTs", [33, L])
            P.ld('sp', zT[:], zTd, 'const', 1)
            w1 = self.sb(st, "fw1", [33, 64])
            w2 = self.sb(st, "fw2", [64, 64])
            w3 = self.sb(st, "fw3", [64, 64])
            wo = self.sb(st, "fwo", [64, 2048])
            P.ld('sp', w1[:], self.hy_f_w1[e], 'const', 1)
            P.ld('sp', w2[:], self.hy_f_w2[e], 'const', 1)
            P.ld('sp', w3[:], self.hy_f_w3[e], 'const', 1)
            P.ld('sp', wo[:], self.hy_f_wout[e], 'const', 1)
            fv = self.sb(st, "fvec", [64, 4])
            P.ld('sp', fv[:], self.hy_fvec[e], 'const', 1)
            A = self.sb(st, "fA", [64, 1])
            Bc = self.sb(st, "fBc", [64, 3])
            P.ts('dve', A[:], fv[:, 0:1], 1.0 / (2.0 * math.pi), None, ALU.mult)
            for i in range(3):
                P.tt('dve', Bc[:, i:i + 1], fv[:, i + 1:i + 2], A[:], ALU.mult)
            B_ = (self.sb(st, "sy", [64, 512]), self.sb(st, "syi", [64, 512], I32), self.sb(st, "syf", [64, 512]),
                  self.sb(st, "sm1", [64, 512]), self.sb(st, "sm2", [64, 512]))
            h1 = self.sb(st, "fh1", [64, 512])
            h2 = self.sb(st, "fh2", [64, 512])
            h3T = self.sb(st, "fh3T", [64, L])
            for cb in range((L + 511) // 512):
                w_ = min(512, L - cb * 512)
                c0 = cb * 512
                ps = self.ps[cb % 2]
                P.mm(ps[0:64, 0:w_], w1[:], zT[:, c0:c0 + w_])
                self.sin_block(B_, h1[:, 0:w_], ps[0:64, 0:w_], A[:], Bc[:, 0:1], w_)
                ps2 = self.ps[2 + cb % 2]
                P.mm(ps2[0:64, 0:w_], w2[:], h1[:, 0:w_])
                self.sin_block(B_, h2[:, 0:w_], ps2[0:64, 0:w_], A[:], Bc[:, 1:2], w_)
                ps3 = self.ps[4 + cb % 2]
                P.mm(ps3[0:64, 0:w_], w3[:], h2[:, 0:w_])
                self.sin_block(B_, h3T[:, c0:c0 + w_], ps3[0:64, 0:w_], A[:], Bc[:, 2:3], w_)
            if 'H3T' in self.dbg and L == S:
                P.ld('sp', self.dscr("H3T", [64, L]), h3T[:], 'dbg', 1)
            KS = self.sb(st, "KS", [128, nch, 512], BF16)
            KD = self.sb(st, "KD", [128, nch, 512], BF16)
            TC = [self.sb(st, f"TCk{i}", [128, nch, 128], BF16) for i in range(2)]
            TS = [self.sb(st, f"TSk{i}", [128, nch, 128], BF16) for i in range(2)]
            dec = [self.sb(st, f"dec{i}", [128, 512]) for i in range(2)]
            hf = [self.sb(st, f"hf{i}", [128, 512]) for i in range(2)]
            hb = [self.sb(st, f"hbk{i}", [128, 512]) for i in range(2)]
            brow = self.sb(st, "hybias", [1, 2, 512])
            P.ld('sp', brow[:], self.hy_bias[e:e + 1], 'const', 1)
            kr = [self.sb(st, f"kr{i}", [128, 512]) for i in range(2)]
            ki = [self.sb(st, f"ki{i}", [128, 512]) for i in range(2)]
            sc2 = 2.0 / (2 * L)
            for o in range(2):
                for tt_ in range(nch):
                    k = tt_ % 2
                    P.ld('sp', dec[k][:], decd[tt_ * 128:(tt_ + 1) * 128, :], 'decld', 2)
                    pf, pb_ = self.ps[4 + k * 2], self.ps[5 + k * 2]
                    P.mm(pf[:], h3T[:, tt_ * 128:(tt_ + 1) * 128], wo[:, (o * 2) * 512:(o * 2 + 1) * 512])
                    P.mm(pb_[:], h3T[:, tt_ * 128:(tt_ + 1) * 128], wo[:, (o * 2 + 1) * 512:(o * 2 + 2) * 512])
                    P.tt('dve', hf[k][:], pf[:], dec[k][:], ALU.mult)
                    P.tt('dve', hb[k][:], pb_[:], dec[k][:], ALU.mult)
                    if tt_ == 0:
                        P.tt('dve', hf[k][0:1, :], hf[k][0:1, :], brow[0:1, o, :], ALU.add)
                    P.tt('dve', KS[:, tt_, :], hf[k][:], hb[k][:], ALU.add)
                    P.tt('dve', KD[:, tt_, :], hb[k][:], hf[k][:], ALU.subtract)
                if 'KSd' in self.dbg and L == S and o == 0:
                    P.ld('sp', self.dscr("KSd", [128, nch, 512], BF16), KS[:], 'dbg', 1)

                def evac(fc, pA, pB, o=o):
                    k = fc % 2
                    P.ts('dve', kr[k][:], pA[:], sc2, None, ALU.mult)
                    P.act(ki[k][:], pB[:], AF.Copy, scale=sc2)
                    P.ld('pool', KR[o, fc * 128:(fc + 1) * 128, :], kr[k][:], 'krst', 2)
                    P.ld('pool', KI[o, fc * 128:(fc + 1) * 128, :], ki[k][:], 'kist', 2)
                self.dft_forward(st, L, (TC, TS, TFC, TFS), KS, KD, evac)

    def hyena_conv(self, l, L, t0):
        P = self.P
        sfx = "" if L == S else "c"
        nch = L // 128
        KR, KI = getattr(self, 'KR' + sfx), getattr(self, 'KI' + sfx)
        TFC, TFS, TIC, TIS = self.hyena_tables(L)
        if not hasattr(self, 'MIXT'):
            self.MIXT = self.dscr("MIXT", [D, T], BF16)
        with ExitStack() as st:
            U = [self.sb(st, f"U{i}", [128, nch, 512], BF16) for i in range(2)]
            Yr = self.sb(st, "Yr", [128, nch, 512], BF16)
            Yi = self.sb(st, "Yi", [128, nch, 512], BF16)
            TC = [self.sb(st, f"TCc{i}", [128, nch, 128], BF16) for i in range(2)]
            TS = [self.sb(st, f"TSc{i}", [128, nch, 128], BF16) for i in range(2)]
            kr = [self.sb(st, f"ckr{i}", [128, 512]) for i in range(2)]
            ki = [self.sb(st, f"cki{i}", [128, 512]) for i in range(2)]
            tmp = [self.sb(st, f"ctmp{i}", [128, 512]) for i in range(4)]
            gt = [self.sb(st, f"cgt{i}", [128, 512]) for i in range(2)]
            zb = [self.sb(st, f"czb{i}", [128, 512], BF16) for i in range(2)]
            zT = [self.sb(st, f"czT{i}", [128, 4, 128], BF16) for i in range(2)]
            for a0 in range(0, nch, 8):
                a1 = min(nch, a0 + 8)
                P.ld('sp', U[0][:, a0:a1, :], self.VH[t0 + a0 * 128:t0 + a1 * 128, :].rearrange("(a p) c -> p a c", p=128),
                     'uld', 2, w=[(U[0].name, a0 // 8)])
            for o in range(2):
                Uin, Uout = U[o % 2], U[(o + 1) % 2]
                gate_d = self.X1 if o == 0 else self.X2

                def evac(fc, pA, pB, o=o):
                    k = fc % 2
                    P.ld('sp', kr[k][:], KR[o, fc * 128:(fc + 1) * 128, :], 'krld', 2)
                    P.ld('sp', ki[k][:], KI[o, fc * 128:(fc + 1) * 128, :], 'kild', 2)
                    P.tt('dve', tmp[0][:], pA[:], kr[k][:], ALU.mult)
                    P.tt('dve', tmp[1][:], pB[:], ki[k][:], ALU.mult)
                    P.tt('pool', Yr[:, fc, :], tmp[0][:], tmp[1][:], ALU.add, w=[(Yr.name, fc)])
                    P.tt('dve', tmp[2][:], pA[:], ki[k][:], ALU.mult)
                    P.tt('dve', tmp[3][:], pB[:], kr[k][:], ALU.mult)
                    P.tt('pool', Yi[:, fc, :], tmp[2][:], tmp[3][:], ALU.subtract, w=[(Yi.name, fc)])
                if o == 0:
                    rU = [(Uin.name, a) for a in range((nch + 7) // 8)]
                else:
                    rU = [(Uin.name, 'z', a) for a in range(nch)]
                self._dft_fwd_tok(L, (TC, TS, TFC, TFS), Uin, rU, evac)
                for tc in range(nch):
                    tc_, ts_ = TC[tc % 2], TS[tc % 2]
                    P.ld('sp', tc_[:, 0:nch, :], TIC[tc], 'tabc', 2)
                    P.ld('sp', ts_[:, 0:nch, :], TIS[tc], 'tabs', 2)
                    pY = self.ps[4 + tc % 2]
                    for fk in range(nch):
                        P.mm(pY[:], tc_[:, fk, :], Yr[:, fk, :], start=(fk == 0), stop=False, r=[tc_.name, (Yr.name, fk)])
                    for fk in range(nch):
                        P.mm(pY[:], ts_[:, fk, :], Yi[:, fk, :], start=False, stop=(fk == nch - 1), r=[ts_.name, (Yi.name, fk)])
                    g_ = gt[tc % 2]
                    P.ld('sp', g_[:], gate_d[t0 + tc * 128:t0 + (tc + 1) * 128, :], 'gld', 2)
                    if o == 0:
                        P.tt('dve', Uout[:, tc, :], pY[:], g_[:], ALU.mult, w=[(Uout.name, 'z', tc)])
                    else:
                        z_ = zb[tc % 2]
                        P.tt('dve', z_[:], pY[:], g_[:], ALU.mult)
                        pb = self.ps[6 + tc % 2][:].bitcast(BF16)
                        for c in range(4):
                            P.tr(pb[:, c * 128:(c + 1) * 128], z_[:, c * 128:(c + 1) * 128], self.identb[:])
                        zt_ = zT[tc % 2]
                        P.copy('act', zt_[:], pb[:, 0:512].rearrange("p (c t) -> p c t", c=4))
                        P.ld('pool', self.MIXT[0:512, t0 + tc * 128:t0 + (tc + 1) * 128].rearrange("(c p) t -> p c t", p=128),
                             zt_[:], 'hyst', 2)

    def _dft_fwd_tok(self, L, tabs, src, rsrc, evac):
        P = self.P
        TC, TS, TFC, TFS = tabs
        nch = L // 128
        for fc in range(nch):
            tc_, ts_ = TC[fc % 2], TS[fc % 2]
            P.ld('sp', tc_[:, 0:nch, :], TFC[fc], 'tabc', 2)
            P.ld('sp', ts_[:, 0:nch, :], TFS[fc], 'tabs', 2)
            pA, pB = self.ps[(fc % 2) * 2], self.ps[(fc % 2) * 2 + 1]
            for sc in range(nch):
                P.mm(pA[:], tc_[:, sc, :], src[:, sc, :], start=(sc == 0), stop=(sc == nch - 1), r=[tc_.name] + rsrc)
            for sc in range(nch):
                P.mm(pB[:], ts_[:, sc, :], src[:, sc, :], start=(sc == 0), stop=(sc == nch - 1), r=[ts_.name] + rsrc)
            evac(fc, pA, pB)

    def even_attn(self, l, ctx_mode):
        nc, P = self.nc, self.P
        e = l // 2
        lam_init = 0.8 - 0.6 * math.exp(-0.3 * l)
        if not hasattr(self, 'MIXT'):
            self.MIXT = self.dscr("MIXT", [D, T], BF16)
        with ExitStack() as st:
            ones = self.sb(st, "ones_bf", [128, 128], BF16)
            P.memset('dve', ones[:], 1.0)
            lv = self.sb(st, "lv", [128, 4, 64])
            P.ld('sp', lv[:], self.diff_lambda[e:e + 1].rearrange("o a d -> o (a d)").partition_broadcast(128), 'const', 1)
            pr = self.sb(st, "lvpr", [128, 2, 64])
            s2 = self.sb(st, "lvs", [128, 2])
            P.tt('dve', pr[:, 0, :], lv[:, 0, :], lv[:, 1, :], ALU.mult)
            P.tt('dve', pr[:, 1, :], lv[:, 2, :], lv[:, 3, :], ALU.mult)
            P.op('dve', lambda en: en.tensor_reduce(s2[:], pr[:], AX.X, ALU.add), [pr.name], [s2.name])
            ex = self.sb(st, "lvex", [128, 2])
            P.act(ex[:], s2[:], AF.Exp)
            neglam = self.sb(st, "neglam", [128, 1])
            P.tt('dve', neglam[:], ex[:, 1:2], ex[:, 0:1], ALU.subtract)
            P.ts('dve', neglam[:], neglam[:], -lam_init, None, ALU.add)
            gsc = self.sb(st, "gsc", [128, 1])
            P.ld('sp', gsc[:], self.subln_g[e].rearrange("(p o) -> p o", o=1), 'const', 1)
            P.ts('dve', gsc[:], gsc[:], 1.0 - lam_init, None, ALU.mult)
            K0s = [self.sb(st, f"K0s{i}", [128, T], BF16) for i in range(2)]
            K1s = [self.sb(st, f"K1s{i}", [128, T], BF16) for i in range(2)]
            for i in range(2):
                P.memset('dve', K0s[i][64:128, :], 0.0)
                P.memset('dve', K1s[i][0:64, :], 0.0)
            QTs = [self.sb(st, f"QTs{i}", [128, T], BF16) for i in range(2)]
            Vs = [self.sb(st, f"Vs{i}", [128, NT, 128], BF16) for i in range(2)]
            PT_ = [self.sb(st, f"PTe{i}", [128, 512], BF16) for i in range(4)]
            r_ = [self.sb(st, f"att_r{i}", [128, 512]) for i in range(2)]
            o_ = [self.sb(st, f"att_o{i}", [128, 512]) for i in range(2)]
            oo = self.sb(st, "att_oo", [128, 512])
            sq = self.sb(st, "att_sq", [128, 512], BF16)
            sd = self.sb(st, "att_sd", [128, 512])
            ob = [self.sb(st, f"att_ob{i}", [128, 512], BF16) for i in range(2)]
            pSb = [self.ps[0], self.ps[1], self.ps[7]]
            pO = [self.ps[2], self.ps[3]]
            pD = [self.ps[4], self.ps[5]]
            LOOK = 2
            npt = 0
            nq = 0
            for h in range(4):
                k0_, k1_, qt_, v_ = K0s[h % 2], K1s[h % 2], QTs[h % 2], Vs[h % 2]
                P.ld('sp', k0_[0:64, :], self.KT[h, 0:64, :], 'attk', 2)
                P.ld('sp', k1_[64:128, :], self.KT[h, 64:128, :], 'attk1', 2)
                P.ld('sp', qt_[:], self.QT[h], 'attq', 2)
                for a0 in range(0, NT, 8):
                    a1 = min(NT, a0 + 8)
                    P.ld('sp', v_[:, a0:a1, :], self.VA[a0 * 128:a1 * 128, h * 128:(h + 1) * 128]
                         .rearrange("(a p) c -> p a c", p=128), 'attv', 2, w=[(v_.name, a0 // 8)])
                qblocks = [(qb * 512, 512, list(range(NT))) for qb in range(8)]
                if ctx_mode == 'full':
                    qblocks.append((S, CT, [NLT, NLT + 1]))
                items = []
                for (q0, qw, ktiles) in qblocks:
                    for ki, kt in enumerate(ktiles):
                        for m in range(2):
                            items.append((q0, qw, ki, kt, m, len(ktiles)))

                def emit_qk(it, idx):
                    q0, qw, ki, kt, m, nk = it
                    pS = pSb[idx % 3]
                    km = k0_ if m == 0 else k1_
                    P.mm(pS[:, 0:qw], km[:, kt * 128:(kt + 1) * 128], qt_[:, q0:q0 + qw])

                def emit_pv(it, idx):
                    nonlocal nq
                    q0, qw, ki, kt, m, nk = it
                    pS = pSb[idx % 3]
                    pt = PT_[idx % 4]
                    P.act(pt[:, 0:qw], pS[:, 0:qw], AF.Exp, scale=0.125)
                    P.mm(pO[m][:, 0:qw], v_[:, kt, :], pt[:, 0:qw], start=(ki == 0), stop=(ki == nk - 1),
                         r=[(v_.name, kt // 8), pt.name])
                    P.mm(pD[m][:, 0:qw], ones[:], pt[:, 0:qw], start=(ki == 0), stop=(ki == nk - 1))
                    if ki == nk - 1 and m == 1:
                        for mm_ in range(2):
                            P.recip(r_[mm_][:, 0:qw], pD[mm_][:, 0:qw])
                            P.tt('dve', o_[mm_][:, 0:qw], pO[mm_][:, 0:qw], r_[mm_][:, 0:qw], ALU.mult)
                        P.stt(oo[:, 0:qw], o_[1][:, 0:qw], neglam[:], o_[0][:, 0:qw], ALU.mult, ALU.add)
                        P.act(sq[:, 0:qw], oo[:, 0:qw], AF.Square)
                        pM = self.ps[6]
                        P.mm(pM[:, 0:qw], ones[:], sq[:, 0:qw])
                        P.act(sd[:, 0:qw], pM[:, 0:qw], AF.Sqrt, scale=1.0 / 128, bias=self.epsc[:])
                        P.recip(sd[:, 0:qw], sd[:, 0:qw])
                        obk = ob[nq % 2]
                        nq += 1
                        P.stt(obk[:, 0:qw], oo[:, 0:qw], gsc[:], sd[:, 0:qw], ALU.mult, ALU.mult)
                        P.ld('pool', self.MIXT[512 + h * 128:512 + (h + 1) * 128, q0:q0 + qw], obk[:, 0:qw], 'attst', 2)

                base = npt
                for i in range(min(LOOK, len(items))):
                    emit_qk(items[i], base + i)
                for i in range(len(items)):
                    if i + LOOK < len(items):
                        emit_qk(items[i + LOOK], base + i + LOOK)
                    emit_pv(items[i], base + i)
                npt += len(items)

    def epi_alloc(self, st, l, ctx_mode):
        P = self.P
        E = {}
        E['rows'] = self.load_modrows(st, l, [3, 4], ctx_mode == 'full', 'ffn')
        E['nb'] = self.alloc_norm_bufs(st)
        E['f32'] = [self.sb(st, f"f32_{i}", [128, D]) for i in range(2)]
        E['fb'] = [self.sb(st, f"fb_{i}", [128, D], BF16) for i in range(2)]
        E['fT'] = [self.sb(st, f"fT_{i}", [128, 8, 128]) for i in range(2)]
        E['rw'] = self.sb(st, "rw32", [128, 8, NE])
        P.ld('sp', E['rw'][:], self.router_w[l].rearrange("(k p) e -> p k e", p=128), 'const', 1)
        E['aff'] = self.sb(st, "AFFsb", [128, NT, NE])
        E['mx'] = [self.sb(st, f"rmx{i}", [128, 1]) for i in range(2)]
        E['sm'] = [self.sb(st, f"rsm{i}", [128, 1]) for i in range(2)]
        E['ex'] = [self.sb(st, f"rex{i}", [128, NE]) for i in range(2)]
        if not hasattr(self, 'F'):
            self.F = self.dscr("F", [T, D], BF16)
            self.AFFD = self.dscr("AFFD", [128, NT, NE])
        return E

    def epilogue_tile(self, E, l, j, hnew, n):
        P = self.P
        s_ = 0 if j < NLT else 1
        k = n % 2
        f32, fb, fT = E['f32'][k], E['fb'][k], E['fT'][k]
        self.norm_mod(E['nb'], hnew, f32[:], E['rows'][(s_, 4)][:], E['rows'][(s_, 3)][:], n)
        P.copy('act', fb[:], f32[:])
        P.ld('sp', self.F[j * 128:(j + 1) * 128, :], fb[:], 'fst', 2)
        pa, pb_, pl = self.ps[6], self.ps[7], self.ps[5]
        for c in range(8):
            pp = pa if c < 4 else pb_
            P.tr(pp[:, (c % 4) * 128:(c % 4 + 1) * 128], f32[:, c * 128:(c + 1) * 128], self.ident[:])
        P.copy('act', fT[:, 0:4, :], pa[:].rearrange("p (c t) -> p c t", c=4))
        P.copy('dve', fT[:, 4:8, :], pb_[:].rearrange("p (c t) -> p c t", c=4))
        for c in range(8):
            P.mm(pl[:, 0:NE], fT[:, c, :], E['rw'][:, c, :], start=(c == 0), stop=(c == 7))
        mx, sm, ex = E['mx'][k], E['sm'][k], E['ex'][k]
        P.op('dve', lambda e: e.tensor_reduce(mx[:], pl[:, 0:NE], AX.X, ALU.max, negate=True),
             ['ps5'], [mx.name])
        P.act(ex[:], pl[:, 0:NE], AF.Exp, bias=mx[:], accum_out=sm[:])
        P.recip(sm[:], sm[:])
        P.ts('dve', E['aff'][:, j, :], ex[:], sm[:], None, ALU.mult)

    def even_wout(self, l, ctx_mode):
        P = self.P
        e = l // 2
        ntiles = NT if ctx_mode == 'full' else NLT
        with ExitStack() as st:
            E = self.epi_alloc(st, l, ctx_mode)
            g1 = self.load_modrows(st, l, [2], ctx_mode == 'full', 'g1')
            Wo = self.sb(st, "Wout", [128, 8, D], BF16)
            P.ld('pool', Wo[:], self.w_out[e].rearrange("(k p) n -> p k n", p=128), 'woutld', 1)
            Mt = [self.sb(st, f"Mt{i}", [128, 8, 512], BF16) for i in range(2)]
            hb = [self.sb(st, f"hbw{i}", [128, D]) for i in range(2)]
            hn = [self.sb(st, f"hnw{i}", [128, D]) for i in range(2)]
            for j in range(ntiles):
                s_ = 0 if j < NLT else 1
                if j % 4 == 0:
                    nt4 = min(4, ntiles - j)
                    m_ = Mt[(j // 4) % 2]
                    P.ld('sp', m_[:, :, 0:nt4 * 128], self.MIXT[:, j * 128:(j + nt4) * 128].rearrange("(k p) t -> p k t", p=128),
                         'mixld', 2)
                m_ = Mt[(j // 4) % 2]
                k = j % 2
                P.ld('sp', hb[k][:], self.hsrc(l, j), 'hld', 2)
                for half in range(2):
                    pY = self.ps[(j % 2) * 2 + half]
                    for kk in range(8):
                        P.mm(pY[:], m_[:, kk, (j % 4) * 128:(j % 4 + 1) * 128], Wo[:, kk, half * 512:(half + 1) * 512],
                             start=(kk == 0), stop=(kk == 7))
                    hs = slice(half * 512, (half + 1) * 512)
                    P.tt('dve', hn[k][:, hs], pY[:], g1[(s_, 2)][:, hs], ALU.mult)
                    P.tt('pool', hn[k][:, hs], hn[k][:, hs], hb[k][:, hs], ALU.add)
                P.ld('sp', self.hdst(j), hn[k][:], 'hst', 2)
                self.epilogue_tile(E, l, j, hn[k][:], j)
            P.ld('sp', self.AFFD, E['aff'][:], 'affst', 1)

    def odd_pool(self, l, ctx_mode):
        P = self.P
        o = l // 2
        ntiles = NT if ctx_mode == 'full' else NLT
        with ExitStack() as st:
            E = self.epi_alloc(st, l, ctx_mode)
            A = self.sb(st, "A_tm", [128, ntiles, D], BF16)
            with ExitStack() as st2:
                rows = self.load_modrows(st2, l, [0, 1], ctx_mode == 'full', 'mixp')
                hb = [self.sb(st2, f"hbp{i}", [128, D]) for i in range(2)]
                for j in range(ntiles):
                    s_ = 0 if j < NLT else 1
                    P.ld('sp', hb[j % 2][:], self.hsrc(l, j), 'hld', 2)
                    self.norm_mod(E['nb'], hb[j % 2][:], A[:, j, :], rows[(s_, 1)][:], rows[(s_, 0)][:], j)
            P.barrier()
            g1 = self.load_modrows(st, l, [2], ctx_mode == 'full', 'g1p')
            psr = self.sb(st, "pscale", [128, D])
            P.ld('sp', psr[:], self.pool_scale[o:o + 1, :].partition_broadcast(128), 'const', 1)
            for k_ in g1:
                P.tt('dve', g1[k_][:], g1[k_][:], psr[:], ALU.mult)
            MT = self.sb(st, "poolMT", [128, 4, 5, 128], BF16)
            P.ld('sp', MT[:], self.poolmt, 'const', 1)
            PW = self.sb(st, "poolW", [128, 4, 2, 256], BF16)
            P.ld('pool', PW[:], self.pool_w[o].rearrange("g (c p) d -> p g c d", p=128), 'woutld', 1)
            pT = [self.sb(st, f"poolpT{i}", [128, 8, 128], BF16) for i in range(2)]
            hb = [self.sb(st, f"hbq{i}", [128, D]) for i in range(2)]
            hn = [self.sb(st, f"hnq{i}", [128, D]) for i in range(2)]
            for j in range(ntiles):
                s_ = 0 if j < NLT else 1
                first = j in (0, NLT)
                last = j in (NLT - 1, NT - 1)
                k = j % 2
                P.ld('sp', hb[k][:], self.hsrc(l, j), 'hld', 2)
                pp = [self.ps[(j % 2) * 2], self.ps[(j % 2) * 2 + 1]]
                for cc in range(8):
                    g = cc // 2
                    dst = pp[cc // 4][:, (cc % 4) * 128:(cc % 4 + 1) * 128]
                    cs = slice(cc * 128, (cc + 1) * 128)
                    terms = []
                    if not first:
                        terms.append((A[:, j - 1, cs], MT[:, g, 3, :]))
                    terms.append((A[:, j, cs], MT[:, g, 1 if first else (2 if last else 0), :]))
                    if not last:
                        terms.append((A[:, j + 1, cs], MT[:, g, 4, :]))
                    for ti, (lh, rh) in enumerate(terms):
                        P.mm(dst, lh, rh, start=(ti == 0), stop=(ti == len(terms) - 1))
                pt = pT[k]
                P.copy('act', pt[:, 0:4, :], pp[0][:].rearrange("p (c t) -> p c t", c=4))
                P.copy('dve', pt[:, 4:8, :], pp[1][:].rearrange("p (c t) -> p c t", c=4))
                for half in range(2):
                    pY = self.ps[4 + (j % 2) * 2 + half]
                    for gg in range(2):
                        g = half * 2 + gg
                        for cc in range(2):
                            P.mm(pY[:, gg * 256:(gg + 1) * 256], pt[:, g * 2 + cc, :], PW[:, g, cc, :],
                                 start=(cc == 0), stop=(cc == 1))
                    hs = slice(half * 512, (half + 1) * 512)
                    P.tt('dve', hn[k][:, hs], pY[:], g1[(s_, 2)][:, hs], ALU.mult)
                    P.tt('pool', hn[k][:, hs], hn[k][:, hs], hb[k][:, hs], ALU.add)
                P.ld('sp', self.hdst(j), hn[k][:], 'hst', 2)
                self.epilogue_tile(E, l, j, hn[k][:], j)
            P.ld('sp', self.AFFD, E['aff'][:], 'affst', 1)

    def mixer_passthrough(self, l, ctx_mode):
        P = self.P
        ntiles = NT if ctx_mode == 'full' else NLT
        with ExitStack() as st:
            E = self.epi_alloc(st, l, ctx_mode)
            hb = [self.sb(st, f"hb{i}", [128, D]) for i in range(2)]
            for j in range(ntiles):
                P.ld('sp', hb[j % 2][:], self.hsrc(l, j), 'hld', 2)
                if l == 0:
                    P.ld('sp', self.hdst(j), hb[j % 2][:], 'hst', 2)
                self.epilogue_tile(E, l, j, hb[j % 2][:], j)
            P.ld('sp', self.AFFD, E['aff'][:], 'affst', 1)

    def moe_topk(self, l, ctx_mode):
        P = self.P
        groups = [(0, NLT, 512)] + ([(NLT, 2, 32)] if ctx_mode == 'full' else [])
        if not hasattr(self, 'IDXD'):
            self.IDXD = self.dscr("IDXD", [128, NE, 5], I32)
            self.GD = self.dscr("GD", [128, NE, 5])
        with ExitStack() as st:
            aff = self.sb(st, "aff_tm", [128, NT, NE])
            P.ld('sp', aff[:], self.AFFD, 'const', 1)
            ones = self.sb(st, "ones_scan", [NE, S])
            P.memset('dve', ones[:], 1.0)
            iota = self.sb(st, "iota512", [128, 512])
            P.op('pool', lambda e: e.iota(iota[:], [[1, 512]], base=0, channel_multiplier=0,
                                          allow_small_or_imprecise_dtypes=True), [], [iota.name])
            slot = self.sb(st, "slot_tm", [128, NT, NE])
            rhs = self.sb(st, "ohrhs", [128, NT, NE, 4], BF16)
            tokhl = self.sb(st, "tokhl", [128, NT, 2])
            P.ld('sp', tokhl[:], self.tokhl_in, 'const', 1)
            ahi = self.sb(st, "ahi", [128, NT, NE], BF16)
            alo = self.sb(st, "alo", [128, NT, NE])
            P.copy('dve', ahi[:], aff[:])
            P.tt('dve', alo[:], aff[:], ahi[:], ALU.subtract)
            for j in range(NT):
                P.copy('dve', rhs[:, j, :, 0:2], tokhl[:, j:j + 1, :].to_broadcast([128, NE, 2]))
            P.copy('dve', rhs[:, :, :, 2], ahi[:])
            P.copy('dve', rhs[:, :, :, 3], alo[:])
            for (j0, ntl, cap) in groups:
                L = ntl * 128
                with ExitStack() as st2:
                    affT = self.sb(st2, "affT", [NE, L])
                    junk = self.sb(st2, "tk_junk", [NE, L])
                    maskT = self.sb(st2, "maskT", [NE, L])
                    posT = self.sb(st2, "posT", [NE, L])
                    slotT = self.sb(st2, "slotT", [NE, L])
                    sc_ = {n_: self.sb(st2, f"tk_{n_}", [NE, 1]) for n_ in ('lo', 'hi', 'mid', 'cnt', 'pred', 'd')}
                    for q in range((ntl + 3) // 4):
                        pst = self.ps[q % 2]
                        m_ = min(4, ntl - q * 4)
                        for i in range(m_):
                            P.tr(pst[0:NE, i * 128:(i + 1) * 128], aff[:, j0 + q * 4 + i, :], self.ident[:])
                        P.copy('act', affT[:, q * 512:q * 512 + m_ * 128], pst[0:NE, 0:m_ * 128])
                    lo, hi, mid, cnt, pred, d_ = (sc_[n_] for n_ in ('lo', 'hi', 'mid', 'cnt', 'pred', 'd'))
                    P.memset('dve', lo[:], 0.0)
                    P.memset('dve', hi[:], 1.0)
                    for it in range(30):
                        P.ts('dve', mid[:], lo[:], hi[:], 0.5, ALU.add, ALU.mult)
                        P.ts('dve', junk[:], affT[:], mid[:], None, ALU.is_ge, ALU.add, accum_out=cnt[:])
                        P.ts('dve', pred[:], cnt[:], float(cap), None, ALU.is_ge)
                        P.tt('dve', d_[:], mid[:], lo[:], ALU.subtract)
                        P.stt(lo[:], d_[:], pred[:], lo[:], ALU.mult, ALU.add)
                        P.tt('dve', d_[:], hi[:], mid[:], ALU.subtract)
                        P.stt(hi[:], d_[:], pred[:], mid[:], ALU.mult, ALU.add)
                    P.ts('dve', maskT[:], affT[:], lo[:], None, ALU.is_ge)
                    P.op('dve', lambda e, posT=posT, maskT=maskT, L=L: e.tensor_tensor_scan(
                        posT[:], ones[:, 0:L], maskT[:], 0.0, ALU.mult, ALU.add),
                        [ones.name, maskT.name], [posT.name])
                    BIG = 8192.0
                    P.stt(slotT[:], posT[:], -1.0 - BIG, maskT[:], ALU.add, ALU.mult)
                    P.ts('dve', slotT[:], slotT[:], BIG, None, ALU.add)
                    for q in range(ntl):
                        pst = self.ps[2 + q % 2]
                        P.tr(pst[:, 0:NE], slotT[:, q * 128:(q + 1) * 128], self.ident[0:NE, 0:NE])
                        P.copy('act', slot[:, j0 + q, :], pst[:, 0:NE])
            if 'SLOTD' in self.dbg:
                P.ld('sp', self.dscr("SLOTD", [128, NT, NE]), slot[:], 'dbg', 1)
            with ExitStack() as st2:
                oh = [self.sb(st2, f"oh{i}", [128, NLT, 512], BF16) for i in range(2)]
                ohc = [self.sb(st2, f"ohc{i}", [128, 2, 32], BF16) for i in range(2)]
                res = self.sb(st2, "tk_res", [128, NE, 5, 4])
                P.memset('dve', res[:], 0.0)
                for e_ in range(NE):
                    o = oh[e_ % 2]
                    for j in range(NLT):
                        P.ts('dve', o[:, j, :], iota[:], slot[:, j, e_:e_ + 1], None, ALU.is_equal)
                    pst = self.ps[4 + e_ % 2]
                    for sc in range(4):
                        for j in range(NLT):
                            P.mm(pst[:, sc * 4:(sc + 1) * 4], o[:, j, sc * 128:(sc + 1) * 128], rhs[:, j, e_, :],
                                 start=(j == 0), stop=(j == NLT - 1))
                    P.copy('act', res[:, e_, 0:4, :], pst[:, 0:16].rearrange("p (a b) -> p a b", a=4))
                    if ctx_mode == 'full':
                        oc = ohc[e_ % 2]
                        for jj in range(2):
                            P.ts('dve', oc[:, jj, :], iota[:, 0:32], slot[:, NLT + jj, e_:e_ + 1], None, ALU.is_equal)
                        pst2 = self.ps[6 + e_ % 2]
                        for jj in range(2):
                            P.mm(pst2[0:32, 0:4], oc[:, jj, :], rhs[:, NLT + jj, e_, :], start=(jj == 0), stop=(jj == 1))
                        P.copy('act', res[0:32, e_, 4, :], pst2[0:32, 0:4])
                idxf = self.sb(st2, "idxf", [128, NE, 5])
                idxi = self.sb(st2, "idxi", [128, NE, 5], I32)
                gg = self.sb(st2, "gg", [128, NE, 5])
                P.stt(idxf[:], res[:, :, :, 0], 64.0, res[:, :, :, 1], ALU.mult, ALU.add)
                P.copy('dve', idxi[:], idxf[:])
                P.tt('dve', gg[:], res[:, :, :, 2], res[:, :, :, 3], ALU.add)
                P.ld('sp', self.IDXD, idxi[:], 'tkst', 1)
                P.ld('sp', self.GD, gg[:], 'tkst2', 1)

    def moe_experts(self, l, ctx_mode):
        nc, P = self.nc, self.P
        has_ctx = ctx_mode == 'full'
        NS = 544 if has_ctx else 512
        with ExitStack() as st:
            rows = self.load_modrows(st, l, [5], has_ctx, 'g2')
            idxs = self.sb(st, "idxs", [128, NE, 5], I32)
            idxc = self.sb(st, "idxc", [128, NE], I32)
            G = self.sb(st, "Gs", [128, NE, 5])
            P.ld('sp', idxs[:], self.IDXD, 'const', 1)
            P.ld('sp', G[:], self.GD, 'const', 1)
            if has_ctx:
                P.ts('dve', idxc[:], idxs[:, :, 4], -float(S), None, ALU.add)
            xs = [self.sb(st, f"xs{i}", [128, 5, D], BF16) for i in range(2)]
            xsT = [self.sb(st, f"xsT{i}", [128, 8, 544], BF16) for i in range(2)]
            h1T = self.sb(st, "h1T", [128, 16, 544], BF16)
            sa = [self.sb(st, f"sa{i}", [128, 544], BF16) for i in range(2)]
            WG = [self.sb(st, f"WG{i}", [128, 8, 512], BF16) for i in range(2)]
            WU = [self.sb(st, f"WU{i}", [128, 8, 512], BF16) for i in range(2)]
            WD = [[self.sb(st, f"WD{i}_{f}", [128, 4, D], BF16) for f in range(4)] for i in range(2)]
            yout = [self.sb(st, f"yout{i}", [128, D]) for i in range(2)]

            def gather(e_):
                x_ = xs[e_ % 2]
                for sc in range(4):
                    P.dma('pool', lambda e, x_=x_, sc=sc, e_=e_: e.indirect_dma_start(
                        out=x_[:, sc, :], out_offset=None, in_=self.F,
                        in_offset=bass.IndirectOffsetOnAxis(ap=idxs[:, e_, sc:sc + 1], axis=0)),
                        r=[idxs.name], w=[(x_.name, sc)], stream='gath', nbuf=2)
                if has_ctx:
                    P.dma('pool', lambda e, x_=x_, e_=e_: e.indirect_dma_start(
                        out=x_[0:32, 4, :], out_offset=None, in_=self.F,
                        in_offset=bass.IndirectOffsetOnAxis(ap=idxs[0:32, e_, 4:5], axis=0)),
                        r=[idxs.name], w=[(x_.name, 4)], stream='gath', nbuf=2)

            def load_gu(e_, fg):
                n_ = e_ * 4 + fg
                P.ld('pool', WG[n_ % 2][:], self.w_gate[l, e_, :, fg * 512:(fg + 1) * 512]
                     .rearrange("(k p) n -> p k n", p=128), 'wg', 2)
                P.ld('pool', WU[n_ % 2][:], self.w_up[l, e_, :, fg * 512:(fg + 1) * 512]
                     .rearrange("(k p) n -> p k n", p=128), 'wu', 2)

            def load_d(e_, fg):
                P.ld('pool', WD[e_ % 2][fg][:], self.w_down[l, e_, fg * 512:(fg + 1) * 512, :]
                     .rearrange("(k p) n -> p k n", p=128), 'wd', 8)

            gather(0)
            load_gu(0, 0)
            nmm = 0
            for e_ in range(NE):
                x_, xt = xs[e_ % 2], xsT[e_ % 2]
                for sc in range(5 if has_ctx else 4):
                    pb = self.ps[5][:].bitcast(BF16)
                    if sc < 4:
                        for c in range(8):
                            P.tr(pb[:, c * 128:(c + 1) * 128], x_[:, sc, c * 128:(c + 1) * 128], self.identb[:],
                                 r=[(x_.name, sc), self.identb.name])
                        P.copy('act', xt[:, :, sc * 128:(sc + 1) * 128], pb.rearrange("p (c t) -> p c t", c=8))
                    else:
                        for c in range(8):
                            P.tr(pb[:, c * 32:(c + 1) * 32], x_[0:32, 4, c * 128:(c + 1) * 128], self.identb[0:32, 0:32],
                                 r=[(x_.name, 4), self.identb.name])
                        P.copy('act', xt[:, :, 512:544], pb[:, 0:256].rearrange("p (c t) -> p c t", c=8))
                if e_ + 1 < NE:
                    gather(e_ + 1)
                for fg in range(4):
                    if fg < 3:
                        load_gu(e_, fg + 1)
                    elif e_ + 1 < NE:
                        load_gu(e_ + 1, 0)
                    load_d(e_, fg)
                    n_ = e_ * 4 + fg
                    wg, wu = WG[n_ % 2], WU[n_ % 2]
                    for fc in range(4):
                        f = fg * 4 + fc
                        pA, pU = self.ps[nmm % 2], self.ps[2 + nmm % 2]
                        pC = self.ps[4]
                        k2 = nmm % 2
                        nmm += 1
                        for k in range(8):
                            P.mm(pA[:, 0:512], wg[:, k, fc * 128:(fc + 1) * 128], xt[:, k, 0:512], start=(k == 0), stop=(k == 7))
                        for k in range(8):
                            P.mm(pU[:, 0:512], wu[:, k, fc * 128:(fc + 1) * 128], xt[:, k, 0:512], start=(k == 0), stop=(k == 7))
                        P.act(sa[k2][:, 0:512], pA[:, 0:512], AF.Silu)
                        P.tt('dve', h1T[:, f, 0:512], sa[k2][:, 0:512], pU[:, 0:512], ALU.mult)
                        if has_ctx:
                            for k in range(8):
                                P.mm(pC[:, 0:32], wg[:, k, fc * 128:(fc + 1) * 128], xt[:, k, 512:544], start=(k == 0), stop=(k == 7))
                            for k in range(8):
                                P.mm(pC[:, 32:64], wu[:, k, fc * 128:(fc + 1) * 128], xt[:, k, 512:544], start=(k == 0), stop=(k == 7))
                            P.act(sa[k2][:, 512:544], pC[:, 0:32], AF.Silu)
                            P.tt('dve', h1T[:, f, 512:544], sa[k2][:, 512:544], pC[:, 32:64], ALU.mult)
                wd = WD[e_ % 2]
                for sc in range(5 if has_ctx else 4):
                    np_ = 128 if sc < 4 else 32
                    yo = yout[(e_ * 5 + sc) % 2]
                    s_ = 0 if sc < 4 else 1
                    for half in range(2):
                        pY = self.ps[6 + half]
                        for f in range(16):
                            P.mm(pY[0:np_, :], h1T[:, f, sc * 128:sc * 128 + np_], wd[f // 4][:, f % 4, half * 512:(half + 1) * 512],
                                 start=(f == 0), stop=(f == 15))
                        P.stt(yo[0:np_, half * 512:(half + 1) * 512], pY[0:np_, :], G[0:np_, e_, sc:sc + 1],
                              rows[(s_, 5)][0:np_, half * 512:(half + 1) * 512], ALU.mult, ALU.mult)
                    if sc < 4:
                        P.dma('pool', lambda e, yo=yo, e_=e_, sc=sc: e.indirect_dma_start(
                            out=self.out, out_offset=bass.IndirectOffsetOnAxis(ap=idxs[:, e_, sc:sc + 1], axis=0),
                            in_=yo[:], in_offset=None, compute_op=ALU.add),
                            r=[idxs.name, yo.name], w=[], stream='scat', nbuf=1)
                    else:
                        P.dma('pool', lambda e, yo=yo, e_=e_: e.indirect_dma_start(
                            out=self.hc, out_offset=bass.IndirectOffsetOnAxis(ap=idxc[0:32, e_:e_ + 1], axis=0),
                            in_=yo[0:32, :], in_offset=None, compute_op=ALU.add),
                            r=[idxc.name, yo.name], w=[], stream='scat', nbuf=1)


def bf(a):
    return np.ascontiguousarray(np.asarray(a, np.float32).astype(ml_dtypes.bfloat16))


_CONST = {}


def host_consts():
    if _CONST:
        return _CONST
    c = {}
    c['ident'] = np.eye(128, dtype=np.float32)
    nf = 16
    inv = (10000.0 ** (-np.arange(nf, dtype=np.float32) / nf)).astype(np.float32)
    t = np.arange(S)
    row = (t // 64).astype(np.float32)
    col = (t % 64).astype(np.float32)
    cos = np.ones((128, T), np.float32)
    sin = np.zeros((128, T), np.float32)
    perm = np.zeros((128, 128), np.float32)
    for p in range(128):
        d = p % 64
        pos = row if d < 32 else col
        i = d % 16
        half = (d % 32) // 16
        ang = (pos * inv[i]).astype(np.float32)
        cos[p, :S] = np.cos(ang)
        sn = np.sin(ang)
        sin[p, :S] = -sn if half == 0 else sn
        partner = p + 16 if half == 0 else p - 16
        perm[partner, p] = 1.0
    c['rope_cos'] = cos
    c['rope_sin'] = sin
    c['rope_perm'] = bf(perm)
    bo = np.zeros((128, 128), np.float32)
    bo[:64, :64] = 1.0
    bo[64:, 64:] = 1.0
    c['blockones'] = bf(bo)
    tid = (np.arange(NT)[None, :] * 128 + np.arange(128)[:, None])
    c['tokhl'] = np.ascontiguousarray(np.stack([tid // 64, tid % 64], axis=-1).astype(np.float32))
    mt = np.zeros((128, 4, 5, 128), np.float32)
    Lp = 1024
    for gi, w_ in enumerate((2, 4, 8, 16)):
        Mfull = np.zeros((Lp, Lp), np.float64)
        for t_ in range(Lp):
            lo = max(t_ - w_ // 2, 0)
            hi = min(t_ + w_ // 2, Lp)
            Mfull[t_, lo:hi] = 1.0 / (hi - lo)
            Mfull[t_, t_] -= 1.0
        def blk(ti, si):
            return Mfull[ti * 128:(ti + 1) * 128, si * 128:(si + 1) * 128].T
        mt[:, gi, 0] = blk(3, 3)
        mt[:, gi, 1] = blk(0, 0)
        mt[:, gi, 2] = blk(7, 7)
        mt[:, gi, 3] = blk(3, 2)
        mt[:, gi, 4] = blk(3, 4)
    c['poolmt'] = bf(mt)
    for L, sfx in ((S, ""), (CT, "c")):
        f32 = np.float32
        t = np.linspace(0.0, 1.0, L, dtype=f32)[:, None]
        w = (2.0 * math.pi * np.arange(L, dtype=f32)[:, None] / L).astype(f32)
        f = np.linspace(1e-4, 15, 16, dtype=f32)[None, :]
        z = np.concatenate([t, np.cos(f * w), -np.sin(f * w)], axis=-1).astype(f32)
        c['zT' + sfx] = np.ascontiguousarray(z.T)
        max_decay = math.log(1e-2) / 0.3
        min_decay = math.log(1e-2) / 1.5
        deltas = np.abs(np.linspace(min_decay, max_decay, 512, dtype=f32))
        c['decay' + sfx] = np.exp(-t * deltas[None, :]).astype(f32)
        N = 2 * L
        n = L // 128
        sidx = np.arange(L, dtype=np.int64)
        arg = ((2 * sidx[None, :] + 1) * sidx[:, None]) % (2 * N)
        ang = arg.astype(np.float64) * (math.pi / N)
        for nm, fn, sign in (("C", np.cos, 1.0), ("S", np.sin, 1.0)):
            tf = fn(ang)
            blk = tf.reshape(n, 128, n, 128)
            c['TF' + nm + sfx] = bf(blk.transpose(2, 1, 0, 3))
            ti = tf.T if nm == "C" else -tf.T
            blk = ti.reshape(n, 128, n, 128)
            c['TI' + nm + sfx] = bf(blk.transpose(2, 1, 0, 3))
    _CONST.update(c)
    return _CONST


def host_inputs(I, b):
    c = dict(host_consts())
    m = {}
    m['x'] = np.ascontiguousarray(I['x'][b])
    m['ctx'] = np.ascontiguousarray(I['ctx'][b])
    m['cT'] = np.ascontiguousarray(np.stack([I['c'][b].reshape(8, 128).T, I['c_ctx'].reshape(8, 128).T], axis=-1))
    m['ada_w'] = I['ada_w']
    m['ada_b'] = I['ada_b']
    m['norm_g'] = np.ascontiguousarray(np.stack([I['norm_mix_g'], I['norm_ffn_g']], axis=1))
    m['w_in'] = I['w_in']
    cw = np.concatenate([I['hy_conv_w'], I['hy_conv_b'][:, None, :]], axis=1)
    m['convw'] = np.ascontiguousarray(cw.reshape(2, 4, 12, 128).transpose(0, 3, 2, 1))
    qg = np.tile(I['q_norm_g'], (1, 2))
    kg = np.tile(I['k_norm_g'], (1, 2))
    m['qkg'] = np.ascontiguousarray(np.stack([qg, kg], axis=-1))
    m['router_w'] = I['router_w']
    m['w_out'] = I['w_out']
    m['pool_w'] = I['pool_w']
    m['pool_scale'] = I['pool_scale']
    for k_ in ('hy_f_w1', 'hy_f_w2', 'hy_f_w3', 'hy_f_wout', 'hy_bias'):
        m[k_] = I[k_]
    m['hy_fvec'] = np.ascontiguousarray(np.stack([I['hy_f_freq'], I['hy_f_b1'], I['hy_f_b2'], I['hy_f_b3']], axis=-1))
    m['diff_lambda'] = I['diff_lambda']
    m['subln_g'] = I['subln_g']
    m['w_gate'] = I['exp_w_gate']
    m['w_up'] = I['exp_w_up']
    m['w_down'] = I['exp_w_down']
    m.update(c)
    return m


_NC_CACHE = {}


def kernel(**inputs):
    I = {k: np.asarray(v) for k, v in inputs.items()}
    if 'nc' not in _NC_CACHE:
        b_ = Builder()
        _NC_CACHE['nc'] = b_.build()
        _NC_CACHE['names'] = set(b_.dram)
    nc = _NC_CACHE['nc']
    names = _NC_CACHE['names']
    in_maps = []
    for b in range(8):
        m = host_inputs(I, b)
        in_maps.append({k: v for k, v in m.items() if k in names})
    res = run_bass_kernel_spmd(nc, in_maps, core_ids=list(range(8)))
    return np.stack([np.asarray(res.results[b]["y"]) for b in range(8)], axis=0).astype(np.float32)
```

```python
import math
from contextlib import ExitStack
import numpy as np
import ml_dtypes
import concourse.bass as bass
import concourse.mybir as mybir
from concourse.bass_utils import run_bass_kernel_spmd

F32 = mybir.dt.float32
BF16 = mybir.dt.bfloat16
I32 = mybir.dt.int32
U32 = mybir.dt.uint32
ALU = mybir.AluOpType
AF = mybir.ActivationFunctionType
AX = mybir.AxisListType

D = 1024
S = 4096
CT = 256
T = S + CT
NT = T // 128
NLT = S // 128
DEPTH = 4
NE = 16
FF = 2048
EPS = 1e-6
SEM_CH = 20000


TRACKED = set()


def tokname(ap):
    n = ap.tensor.name
    return n if n in TRACKED else None


class Prog:
    ENGS = ['pe', 'act', 'dve', 'pool', 'sp']

    def __init__(self, nc):
        self.nc = nc
        self.ops = {e: [] for e in self.ENGS}
        self.last_w = {}
        self.reads = {}
        self.seq = {e: 0 for e in self.ENGS}
        self.dseq = {}
        self.latest = {}
        self.needed = set()

    def _deps(self, r, w, eng=None):
        deps = {}
        def add(sig):
            if sig is None:
                return
            k, v = sig
            if deps.get(k, 0) < v:
                deps[k] = v
        for t in r:
            add(self.last_w.get(t))
            if isinstance(t, str) and t.startswith('ps'):
                for k, v in self.reads.get(t, {}).items():
                    if k != eng:
                        add((k, v))
        for t in w:
            add(self.last_w.get(t))
            for k, v in self.reads.get(t, {}).items():
                add((k, v))
        return deps

    def _update(self, r, w, sig):
        k, v = sig
        for t in r:
            d = self.reads.setdefault(t, {})
            if d.get(k, 0) < v:
                d[k] = v
        for t in w:
            self.last_w[t] = sig
            self.reads[t] = {}
        self.latest[k] = v

    def op(self, eng, fn, r=(), w=()):
        self.seq[eng] += 1
        sig = (eng, self.seq[eng])
        deps = self._deps(r, w, eng)
        self._update(r, w, sig)
        for kv in deps.items():
            self.needed.add(kv)
        self.ops[eng].append(dict(fn=fn, deps=deps, sig=sig, kind='c'))

    def dma(self, q, fn, r=(), w=(), stream='d', nbuf=2):
        i = self.dseq.get(stream, 0)
        self.dseq[stream] = i + 1
        key = ('d', stream, i % nbuf)
        val = i // nbuf + 1
        deps = self._deps(r, w)
        if val > 1:
            if deps.get(key, 0) < val - 1:
                deps[key] = val - 1
        sig = (key, val)
        self._update(r, w, sig)
        for kv in deps.items():
            self.needed.add(kv)
        self.ops[q].append(dict(fn=fn, deps=deps, sig=sig, kind='d'))

    def barrier(self):
        deps = dict(self.latest)
        for kv in deps.items():
            self.needed.add(kv)
        for e in self.ENGS:
            self.ops[e].append(dict(fn=None, deps=dict(deps), sig=None, kind='b'))
        self.last_w = {}
        self.reads = {}

    def emit(self, es):
        nc = self.nc
        inc_idx = {}
        nsem = {}
        for e in self.ENGS:
            n = 0
            for o in self.ops[e]:
                if o['kind'] == 'c' and o['sig'] in self.needed:
                    n += 1
                    inc_idx[o['sig']] = n
            nsem[e] = (n + SEM_CH - 1) // SEM_CH
        sems = {}
        for e in self.ENGS:
            sems[e] = [es.enter_context(nc.semaphore(f"s_{e}_{i}")) for i in range(nsem[e])]
        dsems = {}
        for e in self.ENGS:
            for o in self.ops[e]:
                if o['kind'] == 'd':
                    k = o['sig'][0]
                    if k not in dsems:
                        dsems[k] = es.enter_context(nc.semaphore(f"d_{k[1]}_{k[2]}"))
        self.n_sems = sum(nsem.values()) + len(dsems)

        def resolve(k, v):
            if isinstance(k, tuple):
                return dsems[k], 16 * v
            n = inc_idx[(k, v)]
            return sems[k][(n - 1) // SEM_CH], (n - 1) % SEM_CH + 1

        def run(eng_name, eng):
            known = {}
            for o in self.ops[eng_name]:
                for k, v in o['deps'].items():
                    if k == 'pe' and eng_name == 'pe':
                        continue
                    if known.get(k, 0) >= v:
                        continue
                    known[k] = v
                    s, val = resolve(k, v)
                    eng.wait_ge(s, val)
                if o['fn'] is None:
                    continue
                ins = o['fn'](eng)
                if o['kind'] == 'd':
                    s, _ = resolve(*o['sig'])
                    ins.then_inc(s, 16)
                elif o['sig'] in inc_idx:
                    n = inc_idx[o['sig']]
                    ins.then_inc(sems[eng_name][(n - 1) // SEM_CH], 1)

        with nc.Block() as block:
            @block.tensor
            def _(e):
                run('pe', e)

            @block.scalar
            def _(e):
                run('act', e)

            @block.vector
            def _(e):
                run('dve', e)

            @block.gpsimd
            def _(e):
                run('pool', e)

            @block.sync
            def _(e):
                run('sp', e)

    def _rw(self, r, w, ins, outs):
        if r is None:
            r = [tokname(a) for a in ins if a is not None and not isinstance(a, (int, float))]
        if w is None:
            w = [tokname(a) for a in outs]
        r = [t for t in r if t is not None]
        w = [t for t in w if t is not None]
        return r, w

    def mm(self, out, lhsT, rhs, start=True, stop=True, r=None, w=None):
        r, w = self._rw(r, w, [lhsT, rhs], [out])
        self.op('pe', lambda e: e.matmul(out, lhsT, rhs, start=start, stop=stop), r, w)

    def tr(self, out, in_, ident, r=None, w=None):
        r, w = self._rw(r, w, [in_, ident], [out])
        self.op('pe', lambda e: e.transpose(out, in_, ident), r, w)

    def act(self, out, in_, func, bias=None, scale=None, accum_out=None, r=None, w=None):
        ins = [in_]
        if bias is not None and not isinstance(bias, (int, float)):
            ins.append(bias)
        if scale is not None and not isinstance(scale, (int, float)):
            ins.append(scale)
        outs = [out] + ([accum_out] if accum_out is not None else [])
        r, w = self._rw(r, w, ins, outs)
        kw = {}
        if bias is not None:
            kw['bias'] = bias
        if scale is not None:
            kw['scale'] = scale
        if accum_out is not None:
            kw['accum_out'] = accum_out
        self.op('act', lambda e: e.activation(out, in_, func, **kw), r, w)

    def tt(self, eng, out, in0, in1, op, r=None, w=None):
        r, w = self._rw(r, w, [in0, in1], [out])
        self.op(eng, lambda e: e.tensor_tensor(out, in0, in1, op), r, w)

    def ts(self, eng, out, in0, s1, s2, op0, op1=None, accum_out=None, r=None, w=None):
        ins = [in0] + [s for s in (s1, s2) if s is not None and not isinstance(s, (int, float))]
        outs = [out] + ([accum_out] if accum_out is not None else [])
        r, w = self._rw(r, w, ins, outs)
        kw = {}
        if op1 is not None:
            kw['op1'] = op1
        if accum_out is not None:
            kw['accum_out'] = accum_out
        self.op(eng, lambda e: e.tensor_scalar(out, in0, s1, s2, op0, **kw), r, w)

    def stt(self, out, in0, scalar, in1, op0, op1, r=None, w=None):
        ins = [in0, in1] + ([scalar] if not isinstance(scalar, (int, float)) else [])
        r, w = self._rw(r, w, ins, [out])
        self.op('dve', lambda e: e.scalar_tensor_tensor(out, in0, scalar, in1, op0, op1), r, w)

    def copy(self, eng, out, in_, r=None, w=None):
        r, w = self._rw(r, w, [in_], [out])
        if eng == 'act':
            self.op(eng, lambda e: e.copy(out, in_), r, w)
        else:
            self.op(eng, lambda e: e.tensor_copy(out, in_), r, w)

    def memset(self, eng, ap, val, w=None):
        _, w = self._rw([], w, [], [ap])
        self.op(eng, lambda e: e.memset(ap, val), [], w)

    def recip(self, out, in_, r=None, w=None):
        r, w = self._rw(r, w, [in_], [out])
        self.op('dve', lambda e: e.reciprocal(out, in_), r, w)

    def ld(self, q, out, in_, stream, nbuf=2, r=None, w=None, **kw):
        r, w = self._rw(r, w, [in_], [out])
        self.dma(q, lambda e: e.dma_start(out=out, in_=in_, **kw), r, w, stream, nbuf)


class Builder:
    def __init__(self, n_layers=DEPTH, dbg=()):
        self.nc = nc = bass.Bass("TRN2", target_bir_lowering=False)
        self.P = Prog(nc)
        self.n_layers = n_layers
        self.dbg = set(dbg)
        self.dram = {}
        self.es = ExitStack()

    def din(self, name, shape, dt=F32):
        t = self.nc.dram_tensor(name, list(shape), dt, kind="ExternalInput")
        self.dram[name] = t
        return t.ap()

    def dout(self, name, shape, dt=F32):
        t = self.nc.dram_tensor(name, list(shape), dt, kind="ExternalOutput")
        self.dram[name] = t
        return t.ap()

    def dscr(self, name, shape, dt=F32):
        kind = "ExternalOutput" if name in self.dbg else "Internal"
        t = self.nc.dram_tensor(name, list(shape), dt, kind=kind)
        self.dram[name] = t
        return t.ap()

    def sb(self, stack, name, shape, dt=F32):
        self._uid = getattr(self, '_uid', 0) + 1
        name = f"{name}_u{self._uid}"
        TRACKED.add(name)
        return stack.enter_context(self.nc.sbuf_tensor(name, list(shape), dt))

    def build(self):
        nc, P = self.nc, self.P
        self.x = self.din("x", [S, D])
        self.ctx = self.din("ctx", [CT, D])
        self.cT = self.din("cT", [128, 8, 2])
        self.ada_w = self.din("ada_w", [DEPTH, D, 6 * D])
        self.ada_b = self.din("ada_b", [DEPTH, 6 * D])
        self.norm_g = self.din("norm_g", [DEPTH, 2, D])
        self.ident_in = self.din("ident", [128, 128])
        self.w_in = self.din("w_in", [2, D, 3072])
        self.convw = self.din("convw", [2, 128, 12, 4])
        self.qkg = self.din("qkg", [2, 128, 2])
        self.rope_cos = self.din("rope_cos", [128, T])
        self.rope_sin = self.din("rope_sin", [128, T])
        self.rope_perm = self.din("rope_perm", [128, 128], BF16)
        self.blockones_in = self.din("blockones", [128, 128], BF16)
        self.router_w = self.din("router_w", [DEPTH, D, NE])
        self.tokhl_in = self.din("tokhl", [128, NT, 2])
        self.w_gate = self.din("w_gate", [DEPTH, NE, D, FF])
        self.w_up = self.din("w_up", [DEPTH, NE, D, FF])
        self.w_down = self.din("w_down", [DEPTH, NE, FF, D])
        self.diff_lambda = self.din("diff_lambda", [2, 4, 64])
        self.subln_g = self.din("subln_g", [2, 128])
        self.hy_f_w1 = self.din("hy_f_w1", [2, 33, 64])
        self.hy_f_w2 = self.din("hy_f_w2", [2, 64, 64])
        self.hy_f_w3 = self.din("hy_f_w3", [2, 64, 64])
        self.hy_f_wout = self.din("hy_f_wout", [2, 64, 2048])
        self.hy_fvec = self.din("hy_fvec", [2, 64, 4])
        self.hy_bias = self.din("hy_bias", [2, 2, 512])
        self.zT = self.din("zT", [33, S])
        self.zTc = self.din("zTc", [33, CT])
        self.decay = self.din("decay", [S, 512])
        self.decayc = self.din("decayc", [CT, 512])
        for nm in ("TFC", "TFS", "TIC", "TIS"):
            setattr(self, nm, self.din(nm, [S // 128, 128, S // 128, 128], BF16))
            setattr(self, nm + "c", self.din(nm + "c", [CT // 128, 128, CT // 128, 128], BF16))
        self.w_out = self.din("w_out", [2, D, D])
        self.pool_w = self.din("pool_w", [2, 4, 256, 256])
        self.pool_scale = self.din("pool_scale", [2, D])
        self.poolmt = self.din("poolmt", [128, 4, 5, 128], BF16)
        self.out = self.dout("y", [S, D])
        self.hc = self.dscr("hc", [CT, D])
        self.MOD = self.dscr("MOD", [2, DEPTH * 6 * D])

        top = self.es
        self.ps = [top.enter_context(nc.psum_tensor(f"ps{i}", [128, 512], F32)) for i in range(8)]
        for i in range(8):
            TRACKED.add(f"ps{i}")
        self.ident = self.sb(top, "identf", [128, 128], F32)
        self.identb = self.sb(top, "identb", [128, 128], BF16)
        P.ld('sp', self.ident[:], self.ident_in, 'const', 1)
        P.copy('dve', self.identb[:], self.ident[:])
        self.epsc = self.sb(top, "epsc", [128, 1], F32)
        P.memset('dve', self.epsc[:], EPS)

        self.prologue()
        P.barrier()
        for l in range(self.n_layers):
            self.layer(l)
        P.barrier()
        P.emit(self.es)
        self.es.close()
        return nc

    def prologue(self):
        nc, P = self.nc, self.P
        with ExitStack() as st:
            cT = self.sb(st, "cTs", [128, 8, 2])
            sT = self.sb(st, "sTs", [128, 8, 2])
            wt = [self.sb(st, f"adaw{i}", [128, 8, 512]) for i in range(2)]
            bias = [self.sb(st, f"adabs{i}", [2, 6 * D]) for i in range(2)]
            gsb = [self.sb(st, f"gsb{i}", [2, 2 * D]) for i in range(2)]
            mod = [self.sb(st, f"modsb{i}", [2, 6 * D]) for i in range(2)]
            P.ld('sp', cT[:], self.cT, 'const', 1)
            P.act(sT[:], cT[:], AF.Silu)
            n = 0
            for l in range(DEPTH):
                bi, gs, mo = bias[l % 2], gsb[l % 2], mod[l % 2]
                P.ld('sp', bi[:], self.ada_b[l:l + 1, :].partition_broadcast(2), 'pro_b', 2)
                P.ld('sp', gs[:], self.norm_g[l:l + 1].rearrange("o k n -> o (k n)").partition_broadcast(2), 'pro_g', 2)
                for cc in range(12):
                    w = wt[n % 2]
                    P.ld('sp', w[:], self.ada_w[l, :, cc * 512:(cc + 1) * 512].rearrange("(j p) n -> p j n", p=128),
                         'adaw', 2)
                    pst = self.ps[n % 2]
                    for j in range(8):
                        P.mm(pst[0:2, :], sT[:, j, :], w[:, j, :], start=(j == 0), stop=(j == 7))
                    c0 = cc * 512
                    P.tt('dve', mo[:, c0:c0 + 512], pst[0:2, :], bi[:, c0:c0 + 512], ALU.add)
                    n += 1
                for v, k in ((1, 0), (4, 1)):
                    P.stt(mo[:, v * D:(v + 1) * D], mo[:, v * D:(v + 1) * D], 1.0, gs[:, k * D:(k + 1) * D],
                          ALU.add, ALU.mult)
                P.ld('sp', self.MOD[:, l * 6 * D:(l + 1) * 6 * D], mo[:], 'pro_st', 2)

    def modrow(self, l, s, v):
        c0 = l * 6 * D + v * D
        return self.MOD[s:s + 1, c0:c0 + D].partition_broadcast(128)

    def layer(self, l):
        P = self.P
        ctx_mode = ['full', 'full', 'kv', 'none'][l]
        if 'moe_only' in self.dbg:
            self.mixer_passthrough(l, ctx_mode)
        elif l % 2 == 0:
            self.even_proj(l)
            P.barrier()
            if 'no_hyena' not in self.dbg:
                self.hyena_filters(l, S)
                P.barrier()
                self.hyena_conv(l, S, 0)
                P.barrier()
                if ctx_mode == 'full':
                    self.hyena_filters(l, CT)
                    P.barrier()
                    self.hyena_conv(l, CT, S)
                    P.barrier()
            if 'no_attn' not in self.dbg:
                self.even_attn(l, ctx_mode)
            P.barrier()
            self.even_wout(l, ctx_mode)
        else:
            self.odd_pool(l, ctx_mode)
        P.barrier()
        if 'no_moe' in self.dbg:
            return
        self.moe_topk(l, ctx_mode)
        P.barrier()
        self.moe_experts(l, ctx_mode)
        P.barrier()

    def hsrc(self, l, j):
        if j < NLT:
            t = self.x if l == 0 else self.out
            return t[j * 128:(j + 1) * 128, :]
        t = self.ctx if l == 0 else self.hc
        return t[(j - NLT) * 128:(j - NLT + 1) * 128, :]

    def hdst(self, j):
        if j < NLT:
            return self.out[j * 128:(j + 1) * 128, :]
        return self.hc[(j - NLT) * 128:(j - NLT + 1) * 128, :]

    def norm_mod(self, st_bufs, h, a_out, G, Bv, n):
        P = self.P
        junk, ss, sd, rstd, tmp = st_bufs
        k = n % 2
        P.act(junk[k][:], h, AF.Square, accum_out=ss[k][:])
        P.act(sd[k][:], ss[k][:], AF.Sqrt, scale=1.0 / D, bias=self.epsc[:])
        P.recip(rstd[k][:], sd[k][:])
        P.stt(tmp[k][:], h, rstd[k][:], G, ALU.mult, ALU.mult)
        P.tt('dve', a_out, tmp[k][:], Bv, ALU.add)

    def alloc_norm_bufs(self, st):
        junk = [self.sb(st, f"nm_junk{i}", [128, D], BF16) for i in range(2)]
        ss = [self.sb(st, f"nm_ss{i}", [128, 1]) for i in range(2)]
        sd = [self.sb(st, f"nm_sd{i}", [128, 1]) for i in range(2)]
        rstd = [self.sb(st, f"nm_rstd{i}", [128, 1]) for i in range(2)]
        tmp = [self.sb(st, f"nm_tmp{i}", [128, D]) for i in range(2)]
        return junk, ss, sd, rstd, tmp

    def load_modrows(self, st, l, vs, with_ctx, tag):
        P = self.P
        rows = {}
        for s_ in ([0, 1] if with_ctx else [0]):
            for v in vs:
                t = self.sb(st, f"mr_{tag}_{s_}_{v}", [128, D])
                P.ld('sp', t[:], self.modrow(l, s_, v), 'modrow', 1)
                rows[(s_, v)] = t
        return rows

    def even_proj(self, l):
        nc, P = self.nc, self.P
        e = l // 2
        ctx_mode = ['full', 'full', 'kv', 'none'][l]
        ntiles = NT if ctx_mode != 'none' else NLT
        with ExitStack() as st:
            AT = self.sb(st, "AT", [128, 8, T], BF16)
            Win = self.sb(st, "Win", [128, 8, 3072], BF16)
            for g in range(6):
                for j in range(8):
                    P.ld('pool', Win[:, j, g * 512:(g + 1) * 512],
                         self.w_in[e, j * 128:(j + 1) * 128, g * 512:(g + 1) * 512], 'winld', 2,
                         w=[('Win', g)])
            with ExitStack() as st2:
                rows = self.load_modrows(st2, l, [0, 1], ctx_mode != 'none', 'mix')
                nb = self.alloc_norm_bufs(st2)
                hb = [self.sb(st2, f"hb{i}", [128, D]) for i in range(2)]
                ab = [self.sb(st2, f"ab{i}", [128, D], BF16) for i in range(2)]
                for j in range(ntiles):
                    s_ = 0 if j < NLT else 1
                    k = j % 2
                    P.ld('sp', hb[k][:], self.hsrc(l, j), 'hld', 2)
                    self.norm_mod(nb, hb[k][:], ab[k][:], rows[(s_, 1)][:], rows[(s_, 0)][:], j)
                    pst = self.ps[j % 2]
                    pb = pst[:].bitcast(BF16)
                    for c in range(8):
                        P.tr(pb[:, c * 128:(c + 1) * 128], ab[k][:, c * 128:(c + 1) * 128], self.identb[:])
                    src = pb.rearrange("p (c t) -> p c t", c=8)
                    if j % 2 == 0:
                        P.copy('act', AT[:, :, j * 128:(j + 1) * 128], src, w=[('AT', j)])
                    else:
                        P.copy('dve', AT[:, :, j * 128:(j + 1) * 128], src, w=[('AT', j)])
                if 'ATd' in self.dbg:
                    P.ld('sp', self.dscr("ATd", [128, 8, T], BF16), AT[:], 'dbg', 1, r=[('AT', j) for j in range(ntiles)])
            P.barrier()
            self.proj_hyena(l, st, AT, Win, ctx_mode)
            P.barrier()

    def proj_hyena(self, l, st, AT, Win, ctx_mode):
        nc, P = self.nc, self.P
        e = l // 2
        segs = [(0, S)] + ([(S, CT)] if ctx_mode == 'full' else [])
        X2T = self.dscr(f"X2T", [512, T]) if not hasattr(self, 'X2T') else self.X2T
        self.X2T = X2T
        if not hasattr(self, 'X1'):
            self.X1 = self.dscr("X1", [T, 512])
            self.X2 = self.dscr("X2", [T, 512])
            self.VH = self.dscr("VH", [T, 512], BF16)
            self.QT = self.dscr("QT", [4, 128, T], BF16)
            self.KT = self.dscr("KT", [4, 128, T], BF16)
            self.VA = self.dscr("VA", [T, 512], BF16)
        with ExitStack() as st2:
            cw = self.sb(st2, "convw", [128, 12, 4])
            P.ld('sp', cw[:], self.convw[e], 'const', 1)
            PT = [self.sb(st2, f"PT{i}", [128, S + 2]) for i in range(1)]
            UC = [self.sb(st2, f"UC{i}", [128, S]) for i in range(2)]
            UCb = self.sb(st2, "UCb", [128, S], BF16)
            stg = [self.sb(st2, f"stg{i}", [128, 32, 128]) for i in range(1)]
            stgb = [self.sb(st2, f"stgb{i}", [128, 32, 128], BF16) for i in range(1)]
            n = 0
            npsum = 0
            for cc in range(12):
                for (t0, L) in segs:
                    k = n % 2
                    pt, uc = PT[0], UC[k]
                    P.memset('dve', pt[:, 0:1], 0.0)
                    P.memset('dve', pt[:, L + 1:L + 2], 0.0)
                    nb = (L + 511) // 512
                    for tb in range(nb):
                        w_ = min(512, L - tb * 512)
                        pst = self.ps[npsum % 4]
                        npsum += 1
                        c0 = t0 + tb * 512
                        for kk in range(8):
                            P.mm(pst[:, 0:w_], Win[:, kk, cc * 128:(cc + 1) * 128], AT[:, kk, c0:c0 + w_],
                                 start=(kk == 0), stop=(kk == 7),
                                 r=[('Win', cc // 4)] + [('AT', c0 // 128 + i) for i in range(w_ // 128)])
                        if tb % 2 == 0:
                            P.copy('act', pt[:, 1 + tb * 512:1 + tb * 512 + w_], pst[:, 0:w_])
                        else:
                            P.copy('dve', pt[:, 1 + tb * 512:1 + tb * 512 + w_], pst[:, 0:w_])
                    P.ts('dve', uc[:, 0:L], pt[:, 1:L + 1], cw[:, cc, 1:2], cw[:, cc, 3:4], ALU.mult, ALU.add)
                    P.stt(uc[:, 0:L], pt[:, 0:L], cw[:, cc, 0:1], uc[:, 0:L], ALU.mult, ALU.add)
                    P.stt(uc[:, 0:L], pt[:, 2:L + 2], cw[:, cc, 2:3], uc[:, 0:L], ALU.mult, ALU.add)
                    grp = cc // 4
                    ci = cc % 4
                    if grp in (0, 1):
                        Xd = self.X1 if grp == 0 else self.X2
                        sg = stg[0]
                        nt_ = L // 128
                        for q4 in range((nt_ + 3) // 4):
                            pst = self.ps[4 + (npsum % 4)]
                            npsum += 1
                            m4 = min(4, nt_ - q4 * 4)
                            for i in range(m4):
                                tt_ = q4 * 4 + i
                                P.tr(pst[:, i * 128:(i + 1) * 128], uc[:, tt_ * 128:(tt_ + 1) * 128], self.ident[:])
                            P.copy('act', sg[:, q4 * 4:q4 * 4 + m4, :], pst[:, 0:m4 * 128].rearrange("p (a c) -> p a c", a=m4))
                        for a0 in range(0, nt_, 4):
                            a1 = min(nt_, a0 + 4)
                            P.ld('pool', Xd[t0 + a0 * 128:t0 + a1 * 128, ci * 128:(ci + 1) * 128]
                                 .rearrange("(a p) c -> p a c", p=128), sg[:, a0:a1, :], 'x1st', 2)
                    else:
                        sg = stgb[0]
                        nt_ = L // 128
                        P.copy('act', UCb[:, 0:L], uc[:, 0:L])
                        for q8 in range((nt_ + 7) // 8):
                            pst = self.ps[4 + (npsum % 4)]
                            npsum += 1
                            pb = pst[:].bitcast(BF16)
                            m_ = min(8, nt_ - q8 * 8)
                            for i in range(m_):
                                tt_ = q8 * 8 + i
                                P.tr(pb[:, i * 128:(i + 1) * 128], UCb[:, tt_ * 128:(tt_ + 1) * 128], self.identb[:])
                            P.copy('act', sg[:, q8 * 8:q8 * 8 + m_, :],
                                   pb[:, 0:m_ * 128].rearrange("p (a c) -> p a c", a=m_))
                        for a0 in range(0, nt_, 4):
                            a1 = min(nt_, a0 + 4)
                            P.ld('pool', self.VH[t0 + a0 * 128:t0 + a1 * 128, ci * 128:(ci + 1) * 128]
                                 .rearrange("(a p) c -> p a c", p=128), sg[:, a0:a1, :], 'vhst', 2)
                    n += 1
        P.barrier()
        if 'skip_qk' not in self.dbg:
            self.proj_qk(l, AT, Win, ctx_mode)

    def proj_qk(self, l, AT, Win, ctx_mode):
        nc, P = self.nc, self.P
        e = l // 2
        with ExitStack() as st2:
            cosT = self.sb(st2, "cosT", [128, T])
            sinT = self.sb(st2, "sinT", [128, T])
            P.ld('sp', cosT[:], self.rope_cos, 'const', 1)
            P.ld('sp', sinT[:], self.rope_sin, 'const', 1)
            Rm = self.sb(st2, "Rm", [128, 128], BF16)
            bo = self.sb(st2, "blockones", [128, 128], BF16)
            P.ld('sp', Rm[:], self.rope_perm, 'const', 1)
            P.ld('sp', bo[:], self.blockones_in, 'const', 1)
            qkg = self.sb(st2, "qkg", [128, 2])
            P.ld('sp', qkg[:], self.qkg[e], 'const', 1)
            q32 = [self.sb(st2, f"q32_{i}", [128, 512]) for i in range(2)]
            sq = [self.sb(st2, f"sq_{i}", [128, 512], BF16) for i in range(2)]
            sd = [self.sb(st2, f"qsd_{i}", [128, 512]) for i in range(2)]
            rs = [self.sb(st2, f"qrs_{i}", [128, 512]) for i in range(2)]
            qn = [self.sb(st2, f"qn_{i}", [128, 512]) for i in range(2)]
            qnb = [self.sb(st2, f"qnb_{i}", [128, 512], BF16) for i in range(2)]
            t1 = [self.sb(st2, f"qt1_{i}", [128, 512]) for i in range(2)]
            t2 = [self.sb(st2, f"qt2_{i}", [128, 512]) for i in range(2)]
            qo = [self.sb(st2, f"qo_{i}", [128, T], BF16) for i in range(2)]
            n = 0
            for which in ((0, 1) if 'skip_qkloop' not in self.dbg else ()):
                if which == 0:
                    segs = [(0, S)] + ([(S, CT)] if ctx_mode == 'full' else [])
                else:
                    segs = [(0, S)] + ([(S, CT)] if ctx_mode in ('full', 'kv') else [])
                dst = self.QT if which == 0 else self.KT
                for h in range(4):
                    col0 = 1536 + which * 512 + h * 128
                    qout = qo[(which * 4 + h) % 2]
                    tend = 0
                    for (t0, L) in segs:
                        for tb in range((L + 511) // 512):
                            w_ = min(512, L - tb * 512)
                            c0 = t0 + tb * 512
                            k = n % 2
                            pA, pB, pC = self.ps[(n % 2) * 3], self.ps[(n % 2) * 3 + 1], self.ps[(n % 2) * 3 + 2]
                            for kk in range(8):
                                P.mm(pA[:, 0:w_], Win[:, kk, col0:col0 + 128], AT[:, kk, c0:c0 + w_],
                                     start=(kk == 0), stop=(kk == 7),
                                     r=[('Win', col0 // 512)] + [('AT', c0 // 128 + i) for i in range(w_ // 128)])
                            import os
                            lim = int(os.environ.get('QKSTEP', 99))
                            if lim >= 2: P.act(sq[k][:, 0:w_], pA[:, 0:w_], AF.Square)
                            if lim >= 3: P.copy('dve', q32[k][:, 0:w_], pA[:, 0:w_])
                            if lim >= 4: P.mm(pB[:, 0:w_], bo[:], sq[k][:, 0:w_])
                            if lim >= 5: P.act(sd[k][:, 0:w_], pB[:, 0:w_], AF.Sqrt, scale=1.0 / 64, bias=self.epsc[:])
                            if lim >= 6: P.recip(rs[k][:, 0:w_], sd[k][:, 0:w_])
                            if lim >= 7: P.stt(qn[k][:, 0:w_], q32[k][:, 0:w_], qkg[:, which:which + 1], rs[k][:, 0:w_],
                                  ALU.mult, ALU.mult)
                            if lim >= 8: P.copy('act', qnb[k][:, 0:w_], qn[k][:, 0:w_])
                            if lim >= 9: P.mm(pC[:, 0:w_], Rm[:], qnb[k][:, 0:w_])
                            if lim >= 10: P.tt('dve', t1[k][:, 0:w_], qn[k][:, 0:w_], cosT[:, c0:c0 + w_], ALU.mult)
                            if lim >= 11: P.tt('dve', t2[k][:, 0:w_], pC[:, 0:w_], sinT[:, c0:c0 + w_], ALU.mult)
                            if lim >= 12: P.tt('dve', qout[:, c0:c0 + w_], t1[k][:, 0:w_], t2[k][:, 0:w_], ALU.add)
                            n += 1
                        tend = t0 + L
                    if lim >= 13: P.ld('pool', dst[h, :, 0:tend], qout[:, 0:tend], 'qst', 2)
        P.barrier()
        if 'skip_va' in self.dbg:
            return
        with ExitStack() as st2:
            vst = [self.sb(st2, f"vst{i}", [128, 512], BF16) for i in range(2)]
            ntiles = NT if ctx_mode in ('full', 'kv') else NLT
            for j in range(ntiles):
                pst = self.ps[j % 2]
                for kk in range(8):
                    P.mm(pst[:], AT[:, kk, j * 128:(j + 1) * 128], Win[:, kk, 2560:3072],
                         start=(kk == 0), stop=(kk == 7), r=[('Win', 5), ('AT', j)])
                if j % 2 == 0:
                    P.copy('act', vst[j % 2][:], pst[:])
                else:
                    P.copy('dve', vst[j % 2][:], pst[:])
                P.ld('pool', self.VA[j * 128:(j + 1) * 128, :], vst[j % 2][:], 'vast', 2)


    def sin_block(self, B_, out, ps, A, Bc, w_):
        P = self.P
        y, yi, yf, m1, m2 = B_
        P.ts('dve', y[:, 0:w_], ps, A, Bc, ALU.mult, ALU.add)
        P.copy('dve', yi[:, 0:w_], y[:, 0:w_])
        P.copy('dve', yf[:, 0:w_], yi[:, 0:w_])
        P.tt('dve', y[:, 0:w_], y[:, 0:w_], yf[:, 0:w_], ALU.subtract)
        P.ts('dve', m1[:, 0:w_], y[:, 0:w_], 0.5, None, ALU.is_gt)
        P.ts('dve', m2[:, 0:w_], y[:, 0:w_], -0.5, None, ALU.is_lt)
        P.tt('dve', y[:, 0:w_], y[:, 0:w_], m1[:, 0:w_], ALU.subtract)
        P.tt('dve', y[:, 0:w_], y[:, 0:w_], m2[:, 0:w_], ALU.add)
        P.act(out, y[:, 0:w_], AF.Sin, scale=2.0 * math.pi * (1.0 - 1e-6))

    def dft_forward(self, st, L, tabs, srcA, srcB, evac):
        P = self.P
        TC, TS, TFC, TFS = tabs
        nch = L // 128
        for fc in range(nch):
            tc_, ts_ = TC[fc % 2], TS[fc % 2]
            P.ld('sp', tc_[:, 0:nch, :], TFC[fc], 'tabc', 2)
            P.ld('sp', ts_[:, 0:nch, :], TFS[fc], 'tabs', 2)
            pA, pB = self.ps[(fc % 2) * 2], self.ps[(fc % 2) * 2 + 1]
            for sc in range(nch):
                P.mm(pA[:], tc_[:, sc, :], srcA[:, sc, :], start=(sc == 0), stop=(sc == nch - 1))
            for sc in range(nch):
                P.mm(pB[:], ts_[:, sc, :], srcB[:, sc, :], start=(sc == 0), stop=(sc == nch - 1))
            evac(fc, pA, pB)

    def hyena_tables(self, L):
        sfx = "" if L == S else "c"
        return (getattr(self, "TFC" + sfx), getattr(self, "TFS" + sfx), getattr(self, "TIC" + sfx), getattr(self, "TIS" + sfx))

    def hyena_filters(self, l, L):
        P = self.P
        e = l // 2
        sfx = "" if L == S else "c"
        nch = L // 128
        if not hasattr(self, 'KR' + sfx):
            setattr(self, 'KR' + sfx, self.dscr('KR' + sfx, [2, L, 512]))
            setattr(self, 'KI' + sfx, self.dscr('KI' + sfx, [2, L, 512]))
        KR, KI = getattr(self, 'KR' + sfx), getattr(self, 'KI' + sfx)
        zTd = self.zT if L == S else self.zTc
        decd = self.decay if L == S else self.decayc
        TFC, TFS, _, _ = self.hyena_tables(L)
        with ExitStack() as st:
            zT = self.sb(st, "zTs", [33, L])
            P.ld('sp', zT[:], zTd, 'const', 1)
            w1 = self.sb(st, "fw1", [33, 64])
            w2 = self.sb(st, "fw2", [64, 64])
            w3 = self.sb(st, "fw3", [64, 64])
            wo = self.sb(st, "fwo", [64, 2048])
            P.ld('sp', w1[:], self.hy_f_w1[e], 'const', 1)
            P.ld('sp', w2[:], self.hy_f_w2[e], 'const', 1)
            P.ld('sp', w3[:], self.hy_f_w3[e], 'const', 1)
            P.ld('sp', wo[:], self.hy_f_wout[e], 'const', 1)
            fv = self.sb(st, "fvec", [64, 4])
            P.ld('sp', fv[:], self.hy_fvec[e], 'const', 1)
            A = self.sb(st, "fA", [64, 1])
            Bc = self.sb(st, "fBc", [64, 3])
            P.ts('dve', A[:], fv[:, 0:1], 1.0 / (2.0 * math.pi), None, ALU.mult)
            for i in range(3):
                P.tt('dve', Bc[:, i:i + 1], fv[:, i + 1:i + 2], A[:], ALU.mult)
            B_ = (self.sb(st, "sy", [64, 512]), self.sb(st, "syi", [64, 512], I32), self.sb(st, "syf", [64, 512]),
                  self.sb(st, "sm1", [64, 512]), self.sb(st, "sm2", [64, 512]))
            h1 = self.sb(st, "fh1", [64, 512])
            h2 = self.sb(st, "fh2", [64, 512])
            h3T = self.sb(st, "fh3T", [64, L])
            for cb in range((L + 511) // 512):
                w_ = min(512, L - cb * 512)
                c0 = cb * 512
                ps = self.ps[cb % 2]
                P.mm(ps[0:64, 0:w_], w1[:], zT[:, c0:c0 + w_])
                self.sin_block(B_, h1[:, 0:w_], ps[0:64, 0:w_], A[:], Bc[:, 0:1], w_)
                ps2 = self.ps[2 + cb % 2]
                P.mm(ps2[0:64, 0:w_], w2[:], h1[:, 0:w_])
                self.sin_block(B_, h2[:, 0:w_], ps2[0:64, 0:w_], A[:], Bc[:, 1:2], w_)
                ps3 = self.ps[4 + cb % 2]
                P.mm(ps3[0:64, 0:w_], w3[:], h2[:, 0:w_])
                self.sin_block(B_, h3T[:, c0:c0 + w_], ps3[0:64, 0:w_], A[:], Bc[:, 2:3], w_)
            if 'H3T' in self.dbg and L == S:
                P.ld('sp', self.dscr("H3T", [64, L]), h3T[:], 'dbg', 1)
            KS = self.sb(st, "KS", [128, nch, 512], BF16)
            KD = self.sb(st, "KD", [128, nch, 512], BF16)
            TC = [self.sb(st, f"TCk{i}", [128, nch, 128], BF16) for i in range(2)]
            TS = [self.sb(st, f"TSk{i}", [128, nch, 128], BF16) for i in range(2)]
            dec = [self.sb(st, f"dec{i}", [128, 512]) for i in range(2)]
            hf = [self.sb(st, f"hf{i}", [128, 512]) for i in range(2)]
            hb = [self.sb(st, f"hbk{i}", [128, 512]) for i in range(2)]
            brow = self.sb(st, "hybias", [1, 2, 512])
            P.ld('sp', brow[:], self.hy_bias[e:e + 1], 'const', 1)
            kr = [self.sb(st, f"kr{i}", [128, 512]) for i in range(2)]
            ki = [self.sb(st, f"ki{i}", [128, 512]) for i in range(2)]
            sc2 = 2.0 / (2 * L)
            for o in range(2):
                for tt_ in range(nch):
                    k = tt_ % 2
                    P.ld('sp', dec[k][:], decd[tt_ * 128:(tt_ + 1) * 128, :], 'decld', 2)
                    pf, pb_ = self.ps[4 + k * 2], self.ps[5 + k * 2]
                    P.mm(pf[:], h3T[:, tt_ * 128:(tt_ + 1) * 128], wo[:, (o * 2) * 512:(o * 2 + 1) * 512])
                    P.mm(pb_[:], h3T[:, tt_ * 128:(tt_ + 1) * 128], wo[:, (o * 2 + 1) * 512:(o * 2 + 2) * 512])
                    P.tt('dve', hf[k][:], pf[:], dec[k][:], ALU.mult)
                    P.tt('dve', hb[k][:], pb_[:], dec[k][:], ALU.mult)
                    if tt_ == 0:
                        P.tt('dve', hf[k][0:1, :], hf[k][0:1, :], brow[0:1, o, :], ALU.add)
                    P.tt('dve', KS[:, tt_, :], hf[k][:], hb[k][:], ALU.add)
                    P.tt('dve', KD[:, tt_, :], hb[k][:], hf[k][:], ALU.subtract)
                if 'KSd' in self.dbg and L == S and o == 0:
                    P.ld('sp', self.dscr("KSd", [128, nch, 512], BF16), KS[:], 'dbg', 1)

                def evac(fc, pA, pB, o=o):
                    k = fc % 2
                    P.ts('dve', kr[k][:], pA[:], sc2, None, ALU.mult)
                    P.act(ki[k][:], pB[:], AF.Copy, scale=sc2)
                    P.ld('pool', KR[o, fc * 128:(fc + 1) * 128, :], kr[k][:], 'krst', 2)
                    P.ld('pool', KI[o, fc * 128:(fc + 1) * 128, :], ki[k][:], 'kist', 2)
                self.dft_forward(st, L, (TC, TS, TFC, TFS), KS, KD, evac)

    def hyena_conv(self, l, L, t0):
        P = self.P
        sfx = "" if L == S else "c"
        nch = L // 128
        KR, KI = getattr(self, 'KR' + sfx), getattr(self, 'KI' + sfx)
        TFC, TFS, TIC, TIS = self.hyena_tables(L)
        if not hasattr(self, 'MIXT'):
            self.MIXT = self.dscr("MIXT", [D, T], BF16)
        with ExitStack() as st:
            U = [self.sb(st, f"U{i}", [128, nch, 512], BF16) for i in range(2)]
            Yr = self.sb(st, "Yr", [128, nch, 512], BF16)
            Yi = self.sb(st, "Yi", [128, nch, 512], BF16)
            TC = [self.sb(st, f"TCc{i}", [128, nch, 128], BF16) for i in range(2)]
            TS = [self.sb(st, f"TSc{i}", [128, nch, 128], BF16) for i in range(2)]
            kr = [self.sb(st, f"ckr{i}", [128, 512]) for i in range(2)]
            ki = [self.sb(st, f"cki{i}", [128, 512]) for i in range(2)]
            tmp = [self.sb(st, f"ctmp{i}", [128, 512]) for i in range(4)]
            gt = [self.sb(st, f"cgt{i}", [128, 512]) for i in range(2)]
            zb = [self.sb(st, f"czb{i}", [128, 512], BF16) for i in range(2)]
            zT = [self.sb(st, f"czT{i}", [128, 4, 128], BF16) for i in range(2)]
            for a0 in range(0, nch, 8):
                a1 = min(nch, a0 + 8)
                P.ld('sp', U[0][:, a0:a1, :], self.VH[t0 + a0 * 128:t0 + a1 * 128, :].rearrange("(a p) c -> p a c", p=128),
                     'uld', 2, w=[(U[0].name, a0 // 8)])
            for o in range(2):
                Uin, Uout = U[o % 2], U[(o + 1) % 2]
                gate_d = self.X1 if o == 0 else self.X2

                def evac(fc, pA, pB, o=o):
                    k = fc % 2
                    P.ld('sp', kr[k][:], KR[o, fc * 128:(fc + 1) * 128, :], 'krld', 2)
                    P.ld('sp', ki[k][:], KI[o, fc * 128:(fc + 1) * 128, :], 'kild', 2)
                    P.tt('dve', tmp[0][:], pA[:], kr[k][:], ALU.mult)
                    P.tt('dve', tmp[1][:], pB[:], ki[k][:], ALU.mult)
                    P.tt('pool', Yr[:, fc, :], tmp[0][:], tmp[1][:], ALU.add, w=[(Yr.name, fc)])
                    P.tt('dve', tmp[2][:], pA[:], ki[k][:], ALU.mult)
                    P.tt('dve', tmp[3][:], pB[:], kr[k][:], ALU.mult)
                    P.tt('pool', Yi[:, fc, :], tmp[2][:], tmp[3][:], ALU.subtract, w=[(Yi.name, fc)])
                if o == 0:
                    rU = [(Uin.name, a) for a in range((nch + 7) // 8)]
                else:
                    rU = [(Uin.name, 'z', a) for a in range(nch)]
                self._dft_fwd_tok(L, (TC, TS, TFC, TFS), Uin, rU, evac)
                for tc in range(nch):
                    tc_, ts_ = TC[tc % 2], TS[tc % 2]
                    P.ld('sp', tc_[:, 0:nch, :], TIC[tc], 'tabc', 2)
                    P.ld('sp', ts_[:, 0:nch, :], TIS[tc], 'tabs', 2)
                    pY = self.ps[4 + tc % 2]
                    for fk in range(nch):
                        P.mm(pY[:], tc_[:, fk, :], Yr[:, fk, :], start=(fk == 0), stop=False, r=[tc_.name, (Yr.name, fk)])
                    for fk in range(nch):
                        P.mm(pY[:], ts_[:, fk, :], Yi[:, fk, :], start=False, stop=(fk == nch - 1), r=[ts_.name, (Yi.name, fk)])
                    g_ = gt[tc % 2]
                    P.ld('sp', g_[:], gate_d[t0 + tc * 128:t0 + (tc + 1) * 128, :], 'gld', 2)
                    if o == 0:
                        P.tt('dve', Uout[:, tc, :], pY[:], g_[:], ALU.mult, w=[(Uout.name, 'z', tc)])
                    else:
                        z_ = zb[tc % 2]
                        P.tt('dve', z_[:], pY[:], g_[:], ALU.mult)
                        pb = self.ps[6 + tc % 2][:].bitcast(BF16)
                        for c in range(4):
                            P.tr(pb[:, c * 128:(c + 1) * 128], z_[:, c * 128:(c + 1) * 128], self.identb[:])
                        zt_ = zT[tc % 2]
                        P.copy('act', zt_[:], pb[:, 0:512].rearrange("p (c t) -> p c t", c=4))
                        P.ld('pool', self.MIXT[0:512, t0 + tc * 128:t0 + (tc + 1) * 128].rearrange("(c p) t -> p c t", p=128),
                             zt_[:], 'hyst', 2)

    def _dft_fwd_tok(self, L, tabs, src, rsrc, evac):
        P = self.P
        TC, TS, TFC, TFS = tabs
        nch = L // 128
        for fc in range(nch):
            tc_, ts_ = TC[fc % 2], TS[fc % 2]
            P.ld('sp', tc_[:, 0:nch, :], TFC[fc], 'tabc', 2)
            P.ld('sp', ts_[:, 0:nch, :], TFS[fc], 'tabs', 2)
            pA, pB = self.ps[(fc % 2) * 2], self.ps[(fc % 2) * 2 + 1]
            for sc in range(nch):
                P.mm(pA[:], tc_[:, sc, :], src[:, sc, :], start=(sc == 0), stop=(sc == nch - 1), r=[tc_.name] + rsrc)
            for sc in range(nch):
                P.mm(pB[:], ts_[:, sc, :], src[:, sc, :], start=(sc == 0), stop=(sc == nch - 1), r=[ts_.name] + rsrc)
            evac(fc, pA, pB)

    def even_attn(self, l, ctx_mode):
        nc, P = self.nc, self.P
        e = l // 2
        lam_init = 0.8 - 0.6 * math.exp(-0.3 * l)
        if not hasattr(self, 'MIXT'):
            self.MIXT = self.dscr("MIXT", [D, T], BF16)
        with ExitStack() as st:
            ones = self.sb(st, "ones_bf", [128, 128], BF16)
            P.memset('dve', ones[:], 1.0)
            lv = self.sb(st, "lv", [128, 4, 64])
            P.ld('sp', lv[:], self.diff_lambda[e:e + 1].rearrange("o a d -> o (a d)").partition_broadcast(128), 'const', 1)
            pr = self.sb(st, "lvpr", [128, 2, 64])
            s2 = self.sb(st, "lvs", [128, 2])
            P.tt('dve', pr[:, 0, :], lv[:, 0, :], lv[:, 1, :], ALU.mult)
            P.tt('dve', pr[:, 1, :], lv[:, 2, :], lv[:, 3, :], ALU.mult)
            P.op('dve', lambda en: en.tensor_reduce(s2[:], pr[:], AX.X, ALU.add), [pr.name], [s2.name])
            ex = self.sb(st, "lvex", [128, 2])
            P.act(ex[:], s2[:], AF.Exp)
            neglam = self.sb(st, "neglam", [128, 1])
            P.tt('dve', neglam[:], ex[:, 1:2], ex[:, 0:1], ALU.subtract)
            P.ts('dve', neglam[:], neglam[:], -lam_init, None, ALU.add)
            gsc = self.sb(st, "gsc", [128, 1])
            P.ld('sp', gsc[:], self.subln_g[e].rearrange("(p o) -> p o", o=1), 'const', 1)
            P.ts('dve', gsc[:], gsc[:], 1.0 - lam_init, None, ALU.mult)
            K0s = [self.sb(st, f"K0s{i}", [128, T], BF16) for i in range(2)]
            K1s = [self.sb(st, f"K1s{i}", [128, T], BF16) for i in range(2)]
            for i in range(2):
                P.memset('dve', K0s[i][64:128, :], 0.0)
                P.memset('dve', K1s[i][0:64, :], 0.0)
            QTs = [self.sb(st, f"QTs{i}", [128, T], BF16) for i in range(2)]
            Vs = [self.sb(st, f"Vs{i}", [128, NT, 128], BF16) for i in range(2)]
            PT_ = [self.sb(st, f"PTe{i}", [128, 512], BF16) for i in range(4)]
            r_ = [self.sb(st, f"att_r{i}", [128, 512]) for i in range(2)]
            o_ = [self.sb(st, f"att_o{i}", [128, 512]) for i in range(2)]
            oo = self.sb(st, "att_oo", [128, 512])
            sq = self.sb(st, "att_sq", [128, 512], BF16)
            sd = self.sb(st, "att_sd", [128, 512])
            ob = [self.sb(st, f"att_ob{i}", [128, 512], BF16) for i in range(2)]
            pSb = [self.ps[0], self.ps[1], self.ps[7]]
            pO = [self.ps[2], self.ps[3]]
            pD = [self.ps[4], self.ps[5]]
            LOOK = 2
            npt = 0
            nq = 0
            for h in range(4):
                k0_, k1_, qt_, v_ = K0s[h % 2], K1s[h % 2], QTs[h % 2], Vs[h % 2]
                P.ld('sp', k0_[0:64, :], self.KT[h, 0:64, :], 'attk', 2)
                P.ld('sp', k1_[64:128, :], self.KT[h, 64:128, :], 'attk1', 2)
                P.ld('sp', qt_[:], self.QT[h], 'attq', 2)
                for a0 in range(0, NT, 8):
                    a1 = min(NT, a0 + 8)
                    P.ld('sp', v_[:, a0:a1, :], self.VA[a0 * 128:a1 * 128, h * 128:(h + 1) * 128]
                         .rearrange("(a p) c -> p a c", p=128), 'attv', 2, w=[(v_.name, a0 // 8)])
                qblocks = [(qb * 512, 512, list(range(NT))) for qb in range(8)]
                if ctx_mode == 'full':
                    qblocks.append((S, CT, [NLT, NLT + 1]))
                items = []
                for (q0, qw, ktiles) in qblocks:
                    for ki, kt in enumerate(ktiles):
                        for m in range(2):
                            items.append((q0, qw, ki, kt, m, len(ktiles)))

                def emit_qk(it, idx):
                    q0, qw, ki, kt, m, nk = it
                    pS = pSb[idx % 3]
                    km = k0_ if m == 0 else k1_
                    P.mm(pS[:, 0:qw], km[:, kt * 128:(kt + 1) * 128], qt_[:, q0:q0 + qw])

                def emit_pv(it, idx):
                    nonlocal nq
                    q0, qw, ki, kt, m, nk = it
                    pS = pSb[idx % 3]
                    pt = PT_[idx % 4]
                    P.act(pt[:, 0:qw], pS[:, 0:qw], AF.Exp, scale=0.125)
                    P.mm(pO[m][:, 0:qw], v_[:, kt, :], pt[:, 0:qw], start=(ki == 0), stop=(ki == nk - 1),
                         r=[(v_.name, kt // 8), pt.name])
                    P.mm(pD[m][:, 0:qw], ones[:], pt[:, 0:qw], start=(ki == 0), stop=(ki == nk - 1))
                    if ki == nk - 1 and m == 1:
                        for mm_ in range(2):
                            P.recip(r_[mm_][:, 0:qw], pD[mm_][:, 0:qw])
                            P.tt('dve', o_[mm_][:, 0:qw], pO[mm_][:, 0:qw], r_[mm_][:, 0:qw], ALU.mult)
                        P.stt(oo[:, 0:qw], o_[1][:, 0:qw], neglam[:], o_[0][:, 0:qw], ALU.mult, ALU.add)
                        P.act(sq[:, 0:qw], oo[:, 0:qw], AF.Square)
                        pM = self.ps[6]
                        P.mm(pM[:, 0:qw], ones[:], sq[:, 0:qw])
                        P.act(sd[:, 0:qw], pM[:, 0:qw], AF.Sqrt, scale=1.0 / 128, bias=self.epsc[:])
                        P.recip(sd[:, 0:qw], sd[:, 0:qw])
                        obk = ob[nq % 2]
                        nq += 1
                        P.stt(obk[:, 0:qw], oo[:, 0:qw], gsc[:], sd[:, 0:qw], ALU.mult, ALU.mult)
                        P.ld('pool', self.MIXT[512 + h * 128:512 + (h + 1) * 128, q0:q0 + qw], obk[:, 0:qw], 'attst', 2)

                base = npt
                for i in range(min(LOOK, len(items))):
                    emit_qk(items[i], base + i)
                for i in range(len(items)):
                    if i + LOOK < len(items):
                        emit_qk(items[i + LOOK], base + i + LOOK)
                    emit_pv(items[i], base + i)
                npt += len(items)

    def epi_alloc(self, st, l, ctx_mode):
        P = self.P
        E = {}
        E['rows'] = self.load_modrows(st, l, [3, 4], ctx_mode == 'full', 'ffn')
        E['nb'] = self.alloc_norm_bufs(st)
        E['f32'] = [self.sb(st, f"f32_{i}", [128, D]) for i in range(2)]
        E['fb'] = [self.sb(st, f"fb_{i}", [128, D], BF16) for i in range(2)]
        E['fT'] = [self.sb(st, f"fT_{i}", [128, 8, 128]) for i in range(2)]
        E['rw'] = self.sb(st, "rw32", [128, 8, NE])
        P.ld('sp', E['rw'][:], self.router_w[l].rearrange("(k p) e -> p k e", p=128), 'const', 1)
        E['aff'] = self.sb(st, "AFFsb", [128, NT, NE])
        E['mx'] = [self.sb(st, f"rmx{i}", [128, 1]) for i in range(2)]
        E['sm'] = [self.sb(st, f"rsm{i}", [128, 1]) for i in range(2)]
        E['ex'] = [self.sb(st, f"rex{i}", [128, NE]) for i in range(2)]
        if not hasattr(self, 'F'):
            self.F = self.dscr("F", [T, D], BF16)
            self.AFFD = self.dscr("AFFD", [128, NT, NE])
        return E

    def epilogue_tile(self, E, l, j, hnew, n):
        P = self.P
        s_ = 0 if j < NLT else 1
        k = n % 2
        f32, fb, fT = E['f32'][k], E['fb'][k], E['fT'][k]
        self.norm_mod(E['nb'], hnew, f32[:], E['rows'][(s_, 4)][:], E['rows'][(s_, 3)][:], n)
        P.copy('act', fb[:], f32[:])
        P.ld('pool', self.F[j * 128:(j + 1) * 128, :], fb[:], 'fst', 2)
        pa, pb_, pl = self.ps[6], self.ps[7], self.ps[5]
        for c in range(8):
            pp = pa if c < 4 else pb_
            P.tr(pp[:, (c % 4) * 128:(c % 4 + 1) * 128], f32[:, c * 128:(c + 1) * 128], self.ident[:])
        P.copy('act', fT[:, 0:4, :], pa[:].rearrange("p (c t) -> p c t", c=4))
        P.copy('dve', fT[:, 4:8, :], pb_[:].rearrange("p (c t) -> p c t", c=4))
        for c in range(8):
            P.mm(pl[:, 0:NE], fT[:, c, :], E['rw'][:, c, :], start=(c == 0), stop=(c == 7))
        mx, sm, ex = E['mx'][k], E['sm'][k], E['ex'][k]
        P.op('dve', lambda e: e.tensor_reduce(mx[:], pl[:, 0:NE], AX.X, ALU.max, negate=True),
             ['ps5'], [mx.name])
        P.act(ex[:], pl[:, 0:NE], AF.Exp, bias=mx[:], accum_out=sm[:])
        P.recip(sm[:], sm[:])
        P.ts('dve', E['aff'][:, j, :], ex[:], sm[:], None, ALU.mult)

    def even_wout(self, l, ctx_mode):
        P = self.P
        e = l // 2
        ntiles = NT if ctx_mode == 'full' else NLT
        with ExitStack() as st:
            E = self.epi_alloc(st, l, ctx_mode)
            g1 = self.load_modrows(st, l, [2], ctx_mode == 'full', 'g1')
            Wo = self.sb(st, "Wout", [128, 8, D], BF16)
            P.ld('pool', Wo[:], self.w_out[e].rearrange("(k p) n -> p k n", p=128), 'woutld', 1)
            Mt = [self.sb(st, f"Mt{i}", [128, 8, 512], BF16) for i in range(2)]
            hb = [self.sb(st, f"hbw{i}", [128, D]) for i in range(2)]
            hn = [self.sb(st, f"hnw{i}", [128, D]) for i in range(2)]
            for j in range(ntiles):
                s_ = 0 if j < NLT else 1
                if j % 4 == 0:
                    nt4 = min(4, ntiles - j)
                    m_ = Mt[(j // 4) % 2]
                    P.ld('sp', m_[:, :, 0:nt4 * 128], self.MIXT[:, j * 128:(j + nt4) * 128].rearrange("(k p) t -> p k t", p=128),
                         'mixld', 2)
                m_ = Mt[(j // 4) % 2]
                k = j % 2
                P.ld('sp', hb[k][:], self.hsrc(l, j), 'hld', 2)
                for half in range(2):
                    pY = self.ps[(j % 2) * 2 + half]
                    for kk in range(8):
                        P.mm(pY[:], m_[:, kk, (j % 4) * 128:(j % 4 + 1) * 128], Wo[:, kk, half * 512:(half + 1) * 512],
                             start=(kk == 0), stop=(kk == 7))
                    hs = slice(half * 512, (half + 1) * 512)
                    P.tt('dve', hn[k][:, hs], pY[:], g1[(s_, 2)][:, hs], ALU.mult)
                    P.tt('dve', hn[k][:, hs], hn[k][:, hs], hb[k][:, hs], ALU.add)
                P.ld('pool', self.hdst(j), hn[k][:], 'hst', 2)
                self.epilogue_tile(E, l, j, hn[k][:], j)
            P.ld('pool', self.AFFD, E['aff'][:], 'affst', 1)

    def odd_pool(self, l, ctx_mode):
        P = self.P
        o = l // 2
        ntiles = NT if ctx_mode == 'full' else NLT
        with ExitStack() as st:
            E = self.epi_alloc(st, l, ctx_mode)
            A = self.sb(st, "A_tm", [128, ntiles, D], BF16)
            with ExitStack() as st2:
                rows = self.load_modrows(st2, l, [0, 1], ctx_mode == 'full', 'mixp')
                hb = [self.sb(st2, f"hbp{i}", [128, D]) for i in range(2)]
                for j in range(ntiles):
                    s_ = 0 if j < NLT else 1
                    P.ld('sp', hb[j % 2][:], self.hsrc(l, j), 'hld', 2)
                    self.norm_mod(E['nb'], hb[j % 2][:], A[:, j, :], rows[(s_, 1)][:], rows[(s_, 0)][:], j)
            P.barrier()
            g1 = self.load_modrows(st, l, [2], ctx_mode == 'full', 'g1p')
            psr = self.sb(st, "pscale", [128, D])
            P.ld('sp', psr[:], self.pool_scale[o:o + 1, :].partition_broadcast(128), 'const', 1)
            for k_ in g1:
                P.tt('dve', g1[k_][:], g1[k_][:], psr[:], ALU.mult)
            MT = self.sb(st, "poolMT", [128, 4, 5, 128], BF16)
            P.ld('sp', MT[:], self.poolmt, 'const', 1)
            PW = self.sb(st, "poolW", [128, 4, 2, 256], BF16)
            P.ld('pool', PW[:], self.pool_w[o].rearrange("g (c p) d -> p g c d", p=128), 'woutld', 1)
            pT = [self.sb(st, f"poolpT{i}", [128, 8, 128], BF16) for i in range(2)]
            hb = [self.sb(st, f"hbq{i}", [128, D]) for i in range(2)]
            hn = [self.sb(st, f"hnq{i}", [128, D]) for i in range(2)]
            for j in range(ntiles):
                s_ = 0 if j < NLT else 1
                first = j in (0, NLT)
                last = j in (NLT - 1, NT - 1)
                k = j % 2
                P.ld('sp', hb[k][:], self.hsrc(l, j), 'hld', 2)
                pp = [self.ps[(j % 2) * 2], self.ps[(j % 2) * 2 + 1]]
                for cc in range(8):
                    g = cc // 2
                    dst = pp[cc // 4][:, (cc % 4) * 128:(cc % 4 + 1) * 128]
                    cs = slice(cc * 128, (cc + 1) * 128)
                    terms = []
                    if not first:
                        terms.append((A[:, j - 1, cs], MT[:, g, 3, :]))
                    terms.append((A[:, j, cs], MT[:, g, 1 if first else (2 if last else 0), :]))
                    if not last:
                        terms.append((A[:, j + 1, cs], MT[:, g, 4, :]))
                    for ti, (lh, rh) in enumerate(terms):
                        P.mm(dst, lh, rh, start=(ti == 0), stop=(ti == len(terms) - 1))
                pt = pT[k]
                P.copy('act', pt[:, 0:4, :], pp[0][:].rearrange("p (c t) -> p c t", c=4))
                P.copy('dve', pt[:, 4:8, :], pp[1][:].rearrange("p (c t) -> p c t", c=4))
                for half in range(2):
                    pY = self.ps[4 + (j % 2) * 2 + half]
                    for gg in range(2):
                        g = half * 2 + gg
                        for cc in range(2):
                            P.mm(pY[:, gg * 256:(gg + 1) * 256], pt[:, g * 2 + cc, :], PW[:, g, cc, :],
                                 start=(cc == 0), stop=(cc == 1))
                    hs = slice(half * 512, (half + 1) * 512)
                    P.tt('dve', hn[k][:, hs], pY[:], g1[(s_, 2)][:, hs], ALU.mult)
                    P.tt('dve', hn[k][:, hs], hn[k][:, hs], hb[k][:, hs], ALU.add)
                P.ld('pool', self.hdst(j), hn[k][:], 'hst', 2)
                self.epilogue_tile(E, l, j, hn[k][:], j)
            P.ld('pool', self.AFFD, E['aff'][:], 'affst', 1)

    def mixer_passthrough(self, l, ctx_mode):
        P = self.P
        ntiles = NT if ctx_mode == 'full' else NLT
        with ExitStack() as st:
            E = self.epi_alloc(st, l, ctx_mode)
            hb = [self.sb(st, f"hb{i}", [128, D]) for i in range(2)]
            for j in range(ntiles):
                P.ld('sp', hb[j % 2][:], self.hsrc(l, j), 'hld', 2)
                if l == 0:
                    P.ld('sp', self.hdst(j), hb[j % 2][:], 'hst', 2)
                self.epilogue_tile(E, l, j, hb[j % 2][:], j)
            P.ld('pool', self.AFFD, E['aff'][:], 'affst', 1)

    def moe_topk(self, l, ctx_mode):
        P = self.P
        groups = [(0, NLT, 512)] + ([(NLT, 2, 32)] if ctx_mode == 'full' else [])
        if not hasattr(self, 'IDXD'):
            self.IDXD = self.dscr("IDXD", [128, NE, 5], I32)
            self.GD = self.dscr("GD", [128, NE, 5])
        with ExitStack() as st:
            aff = self.sb(st, "aff_tm", [128, NT, NE])
            P.ld('sp', aff[:], self.AFFD, 'const', 1)
            ones = self.sb(st, "ones_scan", [NE, S])
            P.memset('dve', ones[:], 1.0)
            iota = self.sb(st, "iota512", [128, 512])
            P.op('pool', lambda e: e.iota(iota[:], [[1, 512]], base=0, channel_multiplier=0,
                                          allow_small_or_imprecise_dtypes=True), [], [iota.name])
            slot = self.sb(st, "slot_tm", [128, NT, NE])
            rhs = self.sb(st, "ohrhs", [128, NT, NE, 4], BF16)
            tokhl = self.sb(st, "tokhl", [128, NT, 2])
            P.ld('sp', tokhl[:], self.tokhl_in, 'const', 1)
            ahi = self.sb(st, "ahi", [128, NT, NE], BF16)
            alo = self.sb(st, "alo", [128, NT, NE])
            P.copy('dve', ahi[:], aff[:])
            P.tt('dve', alo[:], aff[:], ahi[:], ALU.subtract)
            for j in range(NT):
                P.copy('dve', rhs[:, j, :, 0:2], tokhl[:, j:j + 1, :].to_broadcast([128, NE, 2]))
            P.copy('dve', rhs[:, :, :, 2], ahi[:])
            P.copy('dve', rhs[:, :, :, 3], alo[:])
            with ExitStack() as st2:
                G_ = []
                for gi, (j0, ntl, cap) in enumerate(groups):
                    L = ntl * 128
                    g_ = dict(j0=j0, ntl=ntl, cap=cap, L=L)
                    g_['affT'] = self.sb(st2, f"affT{gi}", [NE, L])
                    g_['junk'] = self.sb(st2, f"tk_junk{gi}", [NE, L])
                    g_['maskT'] = self.sb(st2, f"maskT{gi}", [NE, L])
                    g_['posT'] = self.sb(st2, f"posT{gi}", [NE, L])
                    g_['slotT'] = self.sb(st2, f"slotT{gi}", [NE, L])
                    for n_ in ('lo', 'hi', 'mid', 'cnt', 'pred', 'd', 'd2'):
                        g_[n_] = self.sb(st2, f"tk_{n_}{gi}", [NE, 1])
                    G_.append(g_)
                for g_ in G_:
                    ntl, j0, affT = g_['ntl'], g_['j0'], g_['affT']
                    for q in range((ntl + 3) // 4):
                        pst = self.ps[q % 2]
                        m_ = min(4, ntl - q * 4)
                        for i in range(m_):
                            P.tr(pst[0:NE, i * 128:(i + 1) * 128], aff[:, j0 + q * 4 + i, :], self.ident[:])
                        P.copy('act', affT[:, q * 512:q * 512 + m_ * 128], pst[0:NE, 0:m_ * 128])
                    P.memset('dve', g_['lo'][:], 0.0)
                    P.memset('dve', g_['hi'][:], 1.0)
                for it in range(30):
                    for g_ in G_:
                        P.ts('dve', g_['mid'][:], g_['lo'][:], g_['hi'][:], 0.5, ALU.add, ALU.mult)
                    for g_ in G_:
                        P.ts('dve', g_['junk'][:], g_['affT'][:], g_['mid'][:], None, ALU.is_ge, ALU.add, accum_out=g_['cnt'][:])
                    for g_ in G_:
                        P.ts('dve', g_['pred'][:], g_['cnt'][:], float(g_['cap']), None, ALU.is_ge)
                    for g_ in G_:
                        P.tt('dve', g_['d'][:], g_['mid'][:], g_['lo'][:], ALU.subtract)
                    for g_ in G_:
                        P.tt('dve', g_['d2'][:], g_['hi'][:], g_['mid'][:], ALU.subtract)
                    for g_ in G_:
                        P.stt(g_['lo'][:], g_['d'][:], g_['pred'][:], g_['lo'][:], ALU.mult, ALU.add)
                    for g_ in G_:
                        P.stt(g_['hi'][:], g_['d2'][:], g_['pred'][:], g_['mid'][:], ALU.mult, ALU.add)
                BIG = 8192.0
                for g_ in G_:
                    L, ntl, j0 = g_['L'], g_['ntl'], g_['j0']
                    maskT, posT, slotT = g_['maskT'], g_['posT'], g_['slotT']
                    P.ts('dve', maskT[:], g_['affT'][:], g_['lo'][:], None, ALU.is_ge)
                    P.op('dve', lambda e, posT=posT, maskT=maskT, L=L: e.tensor_tensor_scan(
                        posT[:], ones[:, 0:L], maskT[:], 0.0, ALU.mult, ALU.add),
                        [ones.name, maskT.name], [posT.name])
                    P.stt(slotT[:], posT[:], -1.0 - BIG, maskT[:], ALU.add, ALU.mult)
                    P.ts('dve', slotT[:], slotT[:], BIG, None, ALU.add)
                    for q in range(ntl):
                        pst = self.ps[2 + q % 2]
                        P.tr(pst[:, 0:NE], slotT[:, q * 128:(q + 1) * 128], self.ident[0:NE, 0:NE])
                        P.copy('act', slot[:, j0 + q, :], pst[:, 0:NE])
            P.barrier()
            if 'SLOTD' in self.dbg:
                P.ld('sp', self.dscr("SLOTD", [128, NT, NE]), slot[:], 'dbg', 1)
            with ExitStack() as st2:
                oh = [self.sb(st2, f"oh{i}", [128, NLT, 512], BF16) for i in range(2)]
                ohc = [self.sb(st2, f"ohc{i}", [128, 2, 32], BF16) for i in range(2)]
                res = self.sb(st2, "tk_res", [128, NE, 5, 4])
                P.memset('dve', res[:], 0.0)
                for e_ in range(NE):
                    o = oh[e_ % 2]
                    for j in range(NLT):
                        P.ts('dve', o[:, j, :], iota[:], slot[:, j, e_:e_ + 1], None, ALU.is_equal)
                    pst = self.ps[4 + e_ % 2]
                    for sc in range(4):
                        for j in range(NLT):
                            P.mm(pst[:, sc * 4:(sc + 1) * 4], o[:, j, sc * 128:(sc + 1) * 128], rhs[:, j, e_, :],
                                 start=(j == 0), stop=(j == NLT - 1))
                    P.copy('act', res[:, e_, 0:4, :], pst[:, 0:16].rearrange("p (a b) -> p a b", a=4))
                    if ctx_mode == 'full':
                        oc = ohc[e_ % 2]
                        for jj in range(2):
                            P.ts('dve', oc[:, jj, :], iota[:, 0:32], slot[:, NLT + jj, e_:e_ + 1], None, ALU.is_equal)
                        pst2 = self.ps[6 + e_ % 2]
                        for jj in range(2):
                            P.mm(pst2[0:32, 0:4], oc[:, jj, :], rhs[:, NLT + jj, e_, :], start=(jj == 0), stop=(jj == 1))
                        P.copy('act', res[0:32, e_, 4, :], pst2[0:32, 0:4])
                idxf = self.sb(st2, "idxf", [128, NE, 5])
                idxi = self.sb(st2, "idxi", [128, NE, 5], I32)
                gg = self.sb(st2, "gg", [128, NE, 5])
                P.stt(idxf[:], res[:, :, :, 0], 64.0, res[:, :, :, 1], ALU.mult, ALU.add)
                P.copy('dve', idxi[:], idxf[:])
                P.tt('dve', gg[:], res[:, :, :, 2], res[:, :, :, 3], ALU.add)
                P.ld('sp', self.IDXD, idxi[:], 'tkst', 1)
                P.ld('sp', self.GD, gg[:], 'tkst2', 1)

    def moe_experts(self, l, ctx_mode):
        nc, P = self.nc, self.P
        has_ctx = ctx_mode == 'full'
        NS = 544 if has_ctx else 512
        with ExitStack() as st:
            rows = self.load_modrows(st, l, [5], has_ctx, 'g2')
            idxs = self.sb(st, "idxs", [128, NE, 5], I32)
            idxc = self.sb(st, "idxc", [128, NE], I32)
            G = self.sb(st, "Gs", [128, NE, 5])
            P.ld('sp', idxs[:], self.IDXD, 'const', 1)
            P.ld('sp', G[:], self.GD, 'const', 1)
            if has_ctx:
                P.ts('dve', idxc[:], idxs[:, :, 4], -float(S), None, ALU.add)
            xs = [self.sb(st, f"xs{i}", [128, 5, D], BF16) for i in range(2)]
            xsT = [self.sb(st, f"xsT{i}", [128, 8, 544], BF16) for i in range(2)]
            h1T = self.sb(st, "h1T", [128, 16, 544], BF16)
            sa = [self.sb(st, f"sa{i}", [128, 544], BF16) for i in range(2)]
            WG = [self.sb(st, f"WG{i}", [128, 8, 512], BF16) for i in range(2)]
            WU = [self.sb(st, f"WU{i}", [128, 8, 512], BF16) for i in range(2)]
            WD = [[self.sb(st, f"WD{i}_{f}", [128, 4, D], BF16) for f in range(4)] for i in range(2)]
            yout = [self.sb(st, f"yout{i}", [128, D]) for i in range(2)]

            def gather(e_):
                x_ = xs[e_ % 2]
                for sc in range(4):
                    P.dma('pool', lambda e, x_=x_, sc=sc, e_=e_: e.indirect_dma_start(
                        out=x_[:, sc, :], out_offset=None, in_=self.F,
                        in_offset=bass.IndirectOffsetOnAxis(ap=idxs[:, e_, sc:sc + 1], axis=0)),
                        r=[idxs.name], w=[(x_.name, sc)], stream='gath', nbuf=2)
                if has_ctx:
                    P.dma('pool', lambda e, x_=x_, e_=e_: e.indirect_dma_start(
                        out=x_[0:32, 4, :], out_offset=None, in_=self.F,
                        in_offset=bass.IndirectOffsetOnAxis(ap=idxs[0:32, e_, 4:5], axis=0)),
                        r=[idxs.name], w=[(x_.name, 4)], stream='gath', nbuf=2)

            def load_gu(e_, fg):
                n_ = e_ * 4 + fg
                P.ld('pool', WG[n_ % 2][:], self.w_gate[l, e_, :, fg * 512:(fg + 1) * 512]
                     .rearrange("(k p) n -> p k n", p=128), 'wg', 2)
                P.ld('pool', WU[n_ % 2][:], self.w_up[l, e_, :, fg * 512:(fg + 1) * 512]
                     .rearrange("(k p) n -> p k n", p=128), 'wu', 2)

            def load_d(e_, fg):
                P.ld('pool', WD[e_ % 2][fg][:], self.w_down[l, e_, fg * 512:(fg + 1) * 512, :]
                     .rearrange("(k p) n -> p k n", p=128), 'wd', 8)

            gather(0)
            load_gu(0, 0)
            nmm = 0
            for e_ in range(NE):
                x_, xt = xs[e_ % 2], xsT[e_ % 2]
                for sc in range(5 if has_ctx else 4):
                    pb = self.ps[5][:].bitcast(BF16)
                    if sc < 4:
                        for c in range(8):
                            P.tr(pb[:, c * 128:(c + 1) * 128], x_[:, sc, c * 128:(c + 1) * 128], self.identb[:],
                                 r=[(x_.name, sc), self.identb.name])
                        P.copy('act', xt[:, :, sc * 128:(sc + 1) * 128], pb.rearrange("p (c t) -> p c t", c=8))
                    else:
                        for c in range(8):
                            P.tr(pb[:, c * 32:(c + 1) * 32], x_[0:32, 4, c * 128:(c + 1) * 128], self.identb[0:32, 0:32],
                                 r=[(x_.name, 4), self.identb.name])
                        P.copy('act', xt[:, :, 512:544], pb[:, 0:256].rearrange("p (c t) -> p c t", c=8))
                if e_ + 1 < NE:
                    gather(e_ + 1)
                for fg in range(4):
                    if fg < 3:
                        load_gu(e_, fg + 1)
                    elif e_ + 1 < NE:
                        load_gu(e_ + 1, 0)
                    load_d(e_, fg)
                    n_ = e_ * 4 + fg
                    wg, wu = WG[n_ % 2], WU[n_ % 2]
                    for fc in range(4):
                        f = fg * 4 + fc
                        pA, pU = self.ps[nmm % 2], self.ps[2 + nmm % 2]
                        pC = self.ps[4]
                        k2 = nmm % 2
                        nmm += 1
                        for k in range(8):
                            P.mm(pA[:, 0:512], wg[:, k, fc * 128:(fc + 1) * 128], xt[:, k, 0:512], start=(k == 0), stop=(k == 7))
                        for k in range(8):
                            P.mm(pU[:, 0:512], wu[:, k, fc * 128:(fc + 1) * 128], xt[:, k, 0:512], start=(k == 0), stop=(k == 7))
                        P.act(sa[k2][:, 0:512], pA[:, 0:512], AF.Silu)
                        P.tt('dve', h1T[:, f, 0:512], sa[k2][:, 0:512], pU[:, 0:512], ALU.mult)
                        if has_ctx:
                            for k in range(8):
                                P.mm(pC[:, 0:32], wg[:, k, fc * 128:(fc + 1) * 128], xt[:, k, 512:544], start=(k == 0), stop=(k == 7))
                            for k in range(8):
                                P.mm(pC[:, 32:64], wu[:, k, fc * 128:(fc + 1) * 128], xt[:, k, 512:544], start=(k == 0), stop=(k == 7))
                            P.act(sa[k2][:, 512:544], pC[:, 0:32], AF.Silu)
                            P.tt('dve', h1T[:, f, 512:544], sa[k2][:, 512:544], pC[:, 32:64], ALU.mult)
                wd = WD[e_ % 2]
                for sc in range(5 if has_ctx else 4):
                    np_ = 128 if sc < 4 else 32
                    yo = yout[(e_ * 5 + sc) % 2]
                    s_ = 0 if sc < 4 else 1
                    for half in range(2):
                        pY = self.ps[6 + half]
                        for f in range(16):
                            P.mm(pY[0:np_, :], h1T[:, f, sc * 128:sc * 128 + np_], wd[f // 4][:, f % 4, half * 512:(half + 1) * 512],
                                 start=(f == 0), stop=(f == 15))
                        P.stt(yo[0:np_, half * 512:(half + 1) * 512], pY[0:np_, :], G[0:np_, e_, sc:sc + 1],
                              rows[(s_, 5)][0:np_, half * 512:(half + 1) * 512], ALU.mult, ALU.mult)
                    if sc < 4:
                        P.dma('pool', lambda e, yo=yo, e_=e_, sc=sc: e.indirect_dma_start(
                            out=self.out, out_offset=bass.IndirectOffsetOnAxis(ap=idxs[:, e_, sc:sc + 1], axis=0),
                            in_=yo[:], in_offset=None, compute_op=ALU.add),
                            r=[idxs.name, yo.name], w=[], stream='scat', nbuf=1)
                    else:
                        P.dma('pool', lambda e, yo=yo, e_=e_: e.indirect_dma_start(
                            out=self.hc, out_offset=bass.IndirectOffsetOnAxis(ap=idxc[0:32, e_:e_ + 1], axis=0),
                            in_=yo[0:32, :], in_offset=None, compute_op=ALU.add),
                            r=[idxc.name, yo.name], w=[], stream='scat', nbuf=1)


def bf(a):
    return np.ascontiguousarray(np.asarray(a, np.float32).astype(ml_dtypes.bfloat16))


_CONST = {}


def host_consts():
    if _CONST:
        return _CONST
    c = {}
    c['ident'] = np.eye(128, dtype=np.float32)
    nf = 16
    inv = (10000.0 ** (-np.arange(nf, dtype=np.float32) / nf)).astype(np.float32)
    t = np.arange(S)
    row = (t // 64).astype(np.float32)
    col = (t % 64).astype(np.float32)
    cos = np.ones((128, T), np.float32)
    sin = np.zeros((128, T), np.float32)
    perm = np.zeros((128, 128), np.float32)
    for p in range(128):
        d = p % 64
        pos = row if d < 32 else col
        i = d % 16
        half = (d % 32) // 16
        ang = (pos * inv[i]).astype(np.float32)
        cos[p, :S] = np.cos(ang)
        sn = np.sin(ang)
        sin[p, :S] = -sn if half == 0 else sn
        partner = p + 16 if half == 0 else p - 16
        perm[partner, p] = 1.0
    c['rope_cos'] = cos
    c['rope_sin'] = sin
    c['rope_perm'] = bf(perm)
    bo = np.zeros((128, 128), np.float32)
    bo[:64, :64] = 1.0
    bo[64:, 64:] = 1.0
    c['blockones'] = bf(bo)
    tid = (np.arange(NT)[None, :] * 128 + np.arange(128)[:, None])
    c['tokhl'] = np.ascontiguousarray(np.stack([tid // 64, tid % 64], axis=-1).astype(np.float32))
    mt = np.zeros((128, 4, 5, 128), np.float32)
    Lp = 1024
    for gi, w_ in enumerate((2, 4, 8, 16)):
        Mfull = np.zeros((Lp, Lp), np.float64)
        for t_ in range(Lp):
            lo = max(t_ - w_ // 2, 0)
            hi = min(t_ + w_ // 2, Lp)
            Mfull[t_, lo:hi] = 1.0 / (hi - lo)
            Mfull[t_, t_] -= 1.0
        def blk(ti, si):
            return Mfull[ti * 128:(ti + 1) * 128, si * 128:(si + 1) * 128].T
        mt[:, gi, 0] = blk(3, 3)
        mt[:, gi, 1] = blk(0, 0)
        mt[:, gi, 2] = blk(7, 7)
        mt[:, gi, 3] = blk(3, 2)
        mt[:, gi, 4] = blk(3, 4)
    c['poolmt'] = bf(mt)
    for L, sfx in ((S, ""), (CT, "c")):
        f32 = np.float32
        t = np.linspace(0.0, 1.0, L, dtype=f32)[:, None]
        w = (2.0 * math.pi * np.arange(L, dtype=f32)[:, None] / L).astype(f32)
        f = np.linspace(1e-4, 15, 16, dtype=f32)[None, :]
        z = np.concatenate([t, np.cos(f * w), -np.sin(f * w)], axis=-1).astype(f32)
        c['zT' + sfx] = np.ascontiguousarray(z.T)
        max_decay = math.log(1e-2) / 0.3
        min_decay = math.log(1e-2) / 1.5
        deltas = np.abs(np.linspace(min_decay, max_decay, 512, dtype=f32))
        c['decay' + sfx] = np.exp(-t * deltas[None, :]).astype(f32)
        N = 2 * L
        n = L // 128
        sidx = np.arange(L, dtype=np.int64)
        arg = ((2 * sidx[None, :] + 1) * sidx[:, None]) % (2 * N)
        ang = arg.astype(np.float64) * (math.pi / N)
        for nm, fn, sign in (("C", np.cos, 1.0), ("S", np.sin, 1.0)):
            tf = fn(ang)
            blk = tf.reshape(n, 128, n, 128)
            c['TF' + nm + sfx] = bf(blk.transpose(2, 1, 0, 3))
            ti = tf.T if nm == "C" else -tf.T
            blk = ti.reshape(n, 128, n, 128)
            c['TI' + nm + sfx] = bf(blk.transpose(2, 1, 0, 3))
    _CONST.update(c)
    return _CONST


def host_inputs(I, b):
    c = dict(host_consts())
    m = {}
    m['x'] = np.ascontiguousarray(I['x'][b])
    m['ctx'] = np.ascontiguousarray(I['ctx'][b])
    m['cT'] = np.ascontiguousarray(np.stack([I['c'][b].reshape(8, 128).T, I['c_ctx'].reshape(8, 128).T], axis=-1))
    m['ada_w'] = I['ada_w']
    m['ada_b'] = I['ada_b']
    m['norm_g'] = np.ascontiguousarray(np.stack([I['norm_mix_g'], I['norm_ffn_g']], axis=1))
    m['w_in'] = I['w_in']
    cw = np.concatenate([I['hy_conv_w'], I['hy_conv_b'][:, None, :]], axis=1)
    m['convw'] = np.ascontiguousarray(cw.reshape(2, 4, 12, 128).transpose(0, 3, 2, 1))
    qg = np.tile(I['q_norm_g'], (1, 2))
    kg = np.tile(I['k_norm_g'], (1, 2))
    m['qkg'] = np.ascontiguousarray(np.stack([qg, kg], axis=-1))
    m['router_w'] = I['router_w']
    m['w_out'] = I['w_out']
    m['pool_w'] = I['pool_w']
    m['pool_scale'] = I['pool_scale']
    for k_ in ('hy_f_w1', 'hy_f_w2', 'hy_f_w3', 'hy_f_wout', 'hy_bias'):
        m[k_] = I[k_]
    m['hy_fvec'] = np.ascontiguousarray(np.stack([I['hy_f_freq'], I['hy_f_b1'], I['hy_f_b2'], I['hy_f_b3']], axis=-1))
    m['diff_lambda'] = I['diff_lambda']
    m['subln_g'] = I['subln_g']
    m['w_gate'] = I['exp_w_gate']
    m['w_up'] = I['exp_w_up']
    m['w_down'] = I['exp_w_down']
    m.update(c)
    return m


_NC_CACHE = {}


def kernel(**inputs):
    I = {k: np.asarray(v) for k, v in inputs.items()}
    if 'nc' not in _NC_CACHE:
        b_ = Builder()
        _NC_CACHE['nc'] = b_.build()
        _NC_CACHE['names'] = set(b_.dram)
    nc = _NC_CACHE['nc']
    names = _NC_CACHE['names']
    in_maps = []
    for b in range(8):
        m = host_inputs(I, b)
        in_maps.append({k: v for k, v in m.items() if k in names})
    res = run_bass_kernel_spmd(nc, in_maps, core_ids=list(range(8)))
    return np.stack([np.asarray(res.results[b]["y"]) for b in range(8)], axis=0).astype(np.float32)
```

```python
import math
from contextlib import ExitStack
import numpy as np
import ml_dtypes
import concourse.bass as bass
import concourse.mybir as mybir
from concourse.bass_utils import run_bass_kernel_spmd

F32 = mybir.dt.float32
BF16 = mybir.dt.bfloat16
I32 = mybir.dt.int32
F16 = mybir.dt.float16
U32 = mybir.dt.uint32
ALU = mybir.AluOpType
AF = mybir.ActivationFunctionType
AX = mybir.AxisListType

D = 1024
S = 4096
CT = 256
T = S + CT
NT = T // 128
NLT = S // 128
DEPTH = 4
NE = 16
FF = 2048
EPS = 1e-6
SEM_CH = 20000


TRACKED = set()


def tokname(ap):
    n = ap.tensor.name
    return n if n in TRACKED else None


class Prog:
    ENGS = ['pe', 'act', 'dve', 'pool', 'sp']

    def __init__(self, nc):
        self.nc = nc
        self.ops = {e: [] for e in self.ENGS}
        self.last_w = {}
        self.reads = {}
        self.seq = {e: 0 for e in self.ENGS}
        self.dseq = {}
        self.latest = {}
        self.needed = set()

    def _deps(self, r, w, eng=None):
        deps = {}
        def add(sig):
            if sig is None:
                return
            k, v = sig
            if deps.get(k, 0) < v:
                deps[k] = v
        for t in r:
            add(self.last_w.get(t))
            if isinstance(t, str) and t.startswith('ps'):
                for k, v in self.reads.get(t, {}).items():
                    if k != eng:
                        add((k, v))
        for t in w:
            add(self.last_w.get(t))
            for k, v in self.reads.get(t, {}).items():
                add((k, v))
        return deps

    def _update(self, r, w, sig):
        k, v = sig
        for t in r:
            d = self.reads.setdefault(t, {})
            if d.get(k, 0) < v:
                d[k] = v
        for t in w:
            self.last_w[t] = sig
            self.reads[t] = {}
        self.latest[k] = v

    def op(self, eng, fn, r=(), w=()):
        self.seq[eng] += 1
        sig = (eng, self.seq[eng])
        deps = self._deps(r, w, eng)
        self._update(r, w, sig)
        for kv in deps.items():
            self.needed.add(kv)
        self.ops[eng].append(dict(fn=fn, deps=deps, sig=sig, kind='c'))

    def dma(self, q, fn, r=(), w=(), stream='d', nbuf=2):
        i = self.dseq.get(stream, 0)
        self.dseq[stream] = i + 1
        key = ('d', stream, i % nbuf)
        val = i // nbuf + 1
        deps = self._deps(r, w)
        if val > 1:
            if deps.get(key, 0) < val - 1:
                deps[key] = val - 1
        sig = (key, val)
        self._update(r, w, sig)
        for kv in deps.items():
            self.needed.add(kv)
        self.ops[q].append(dict(fn=fn, deps=deps, sig=sig, kind='d'))

    def barrier(self):
        deps = dict(self.latest)
        for kv in deps.items():
            self.needed.add(kv)
        for e in self.ENGS:
            self.ops[e].append(dict(fn=None, deps=dict(deps), sig=None, kind='b'))
        self.last_w = {}
        self.reads = {}

    def emit(self, es):
        nc = self.nc
        inc_idx = {}
        nsem = {}
        for e in self.ENGS:
            n = 0
            for o in self.ops[e]:
                if o['kind'] == 'c' and o['sig'] in self.needed:
                    n += 1
                    inc_idx[o['sig']] = n
            nsem[e] = (n + SEM_CH - 1) // SEM_CH
        sems = {}
        for e in self.ENGS:
            sems[e] = [es.enter_context(nc.semaphore(f"s_{e}_{i}")) for i in range(nsem[e])]
        dsems = {}
        for e in self.ENGS:
            for o in self.ops[e]:
                if o['kind'] == 'd':
                    k = o['sig'][0]
                    if k not in dsems:
                        dsems[k] = es.enter_context(nc.semaphore(f"d_{k[1]}_{k[2]}"))
        self.n_sems = sum(nsem.values()) + len(dsems)

        def resolve(k, v):
            if isinstance(k, tuple):
                return dsems[k], 16 * v
            n = inc_idx[(k, v)]
            return sems[k][(n - 1) // SEM_CH], (n - 1) % SEM_CH + 1

        def run(eng_name, eng):
            known = {}
            for o in self.ops[eng_name]:
                for k, v in o['deps'].items():
                    if k == 'pe' and eng_name == 'pe':
                        continue
                    if known.get(k, 0) >= v:
                        continue
                    known[k] = v
                    s, val = resolve(k, v)
                    eng.wait_ge(s, val)
                if o['fn'] is None:
                    continue
                ins = o['fn'](eng)
                if o['kind'] == 'd':
                    s, _ = resolve(*o['sig'])
                    ins.then_inc(s, 16)
                elif o['sig'] in inc_idx:
                    n = inc_idx[o['sig']]
                    ins.then_inc(sems[eng_name][(n - 1) // SEM_CH], 1)

        with nc.Block() as block:
            @block.tensor
            def _(e):
                run('pe', e)

            @block.scalar
            def _(e):
                run('act', e)

            @block.vector
            def _(e):
                run('dve', e)

            @block.gpsimd
            def _(e):
                run('pool', e)

            @block.sync
            def _(e):
                run('sp', e)

    def _rw(self, r, w, ins, outs):
        if r is None:
            r = [tokname(a) for a in ins if a is not None and not isinstance(a, (int, float))]
        if w is None:
            w = [tokname(a) for a in outs]
        r = [t for t in r if t is not None]
        w = [t for t in w if t is not None]
        return r, w

    def mm(self, out, lhsT, rhs, start=True, stop=True, r=None, w=None):
        r, w = self._rw(r, w, [lhsT, rhs], [out])
        self.op('pe', lambda e: e.matmul(out, lhsT, rhs, start=start, stop=stop), r, w)

    def tr(self, out, in_, ident, r=None, w=None):
        r, w = self._rw(r, w, [in_, ident], [out])
        self.op('pe', lambda e: e.transpose(out, in_, ident), r, w)

    def act(self, out, in_, func, bias=None, scale=None, accum_out=None, r=None, w=None):
        ins = [in_]
        if bias is not None and not isinstance(bias, (int, float)):
            ins.append(bias)
        if scale is not None and not isinstance(scale, (int, float)):
            ins.append(scale)
        outs = [out] + ([accum_out] if accum_out is not None else [])
        r, w = self._rw(r, w, ins, outs)
        kw = {}
        if bias is not None:
            kw['bias'] = bias
        if scale is not None:
            kw['scale'] = scale
        if accum_out is not None:
            kw['accum_out'] = accum_out
        self.op('act', lambda e: e.activation(out, in_, func, **kw), r, w)

    def tt(self, eng, out, in0, in1, op, r=None, w=None):
        r, w = self._rw(r, w, [in0, in1], [out])
        self.op(eng, lambda e: e.tensor_tensor(out, in0, in1, op), r, w)

    def ts(self, eng, out, in0, s1, s2, op0, op1=None, accum_out=None, r=None, w=None):
        ins = [in0] + [s for s in (s1, s2) if s is not None and not isinstance(s, (int, float))]
        outs = [out] + ([accum_out] if accum_out is not None else [])
        r, w = self._rw(r, w, ins, outs)
        kw = {}
        if op1 is not None:
            kw['op1'] = op1
        if accum_out is not None:
            kw['accum_out'] = accum_out
        self.op(eng, lambda e: e.tensor_scalar(out, in0, s1, s2, op0, **kw), r, w)

    def stt(self, out, in0, scalar, in1, op0, op1, r=None, w=None):
        ins = [in0, in1] + ([scalar] if not isinstance(scalar, (int, float)) else [])
        r, w = self._rw(r, w, ins, [out])
        self.op('dve', lambda e: e.scalar_tensor_tensor(out, in0, scalar, in1, op0, op1), r, w)

    def copy(self, eng, out, in_, r=None, w=None):
        r, w = self._rw(r, w, [in_], [out])
        if eng == 'act':
            self.op(eng, lambda e: e.copy(out, in_), r, w)
        else:
            self.op(eng, lambda e: e.tensor_copy(out, in_), r, w)

    def memset(self, eng, ap, val, w=None):
        _, w = self._rw([], w, [], [ap])
        self.op(eng, lambda e: e.memset(ap, val), [], w)

    def recip(self, out, in_, r=None, w=None):
        r, w = self._rw(r, w, [in_], [out])
        self.op('dve', lambda e: e.reciprocal(out, in_), r, w)

    def ld(self, q, out, in_, stream, nbuf=2, r=None, w=None, **kw):
        r, w = self._rw(r, w, [in_], [out])
        self.dma(q, lambda e: e.dma_start(out=out, in_=in_, **kw), r, w, stream, nbuf)


class Builder:
    def __init__(self, n_layers=DEPTH, dbg=()):
        self.nc = nc = bass.Bass("TRN2", target_bir_lowering=False)
        self.P = Prog(nc)
        self.n_layers = n_layers
        self.dbg = set(dbg)
        self.dram = {}
        self.es = ExitStack()

    def din(self, name, shape, dt=F32):
        t = self.nc.dram_tensor(name, list(shape), dt, kind="ExternalInput")
        self.dram[name] = t
        return t.ap()

    def dout(self, name, shape, dt=F32):
        t = self.nc.dram_tensor(name, list(shape), dt, kind="ExternalOutput")
        self.dram[name] = t
        return t.ap()

    def dscr(self, name, shape, dt=F32):
        kind = "ExternalOutput" if name in self.dbg else "Internal"
        t = self.nc.dram_tensor(name, list(shape), dt, kind=kind)
        self.dram[name] = t
        return t.ap()

    def sb(self, stack, name, shape, dt=F32):
        self._uid = getattr(self, '_uid', 0) + 1
        name = f"{name}_u{self._uid}"
        TRACKED.add(name)
        return stack.enter_context(self.nc.sbuf_tensor(name, list(shape), dt))

    def build(self):
        nc, P = self.nc, self.P
        self.x = self.din("x", [S, D])
        self.ctx = self.din("ctx", [CT, D])
        self.cT = self.din("cT", [128, 8, 2])
        self.ada_w = self.din("ada_w", [DEPTH, D, 6 * D])
        self.ada_b = self.din("ada_b", [DEPTH, 6 * D])
        self.norm_g = self.din("norm_g", [DEPTH, 2, D])
        self.ident_in = self.din("ident", [128, 128])
        self.w_in = self.din("w_in", [2, D, 3072])
        self.convw = self.din("convw", [2, 128, 12, 4])
        self.qkg = self.din("qkg", [2, 128, 2])
        self.rope_cos = self.din("rope_cos", [128, T])
        self.rope_sin = self.din("rope_sin", [128, T])
        self.rope_perm = self.din("rope_perm", [128, 128], BF16)
        self.blockones_in = self.din("blockones", [128, 128], BF16)
        self.router_w = self.din("router_w", [DEPTH, D, NE])
        self.tokhl_in = self.din("tokhl", [128, NT, 2])
        self.w_gate = self.din("w_gate", [DEPTH, NE, D, FF])
        self.w_up = self.din("w_up", [DEPTH, NE, D, FF])
        self.w_down = self.din("w_down", [DEPTH, NE, FF, D])
        self.diff_lambda = self.din("diff_lambda", [2, 4, 64])
        self.subln_g = self.din("subln_g", [2, 128])
        self.hy_f_w1 = self.din("hy_f_w1", [2, 33, 64])
        self.hy_f_w2 = self.din("hy_f_w2", [2, 64, 64])
        self.hy_f_w3 = self.din("hy_f_w3", [2, 64, 64])
        self.hy_f_wout = self.din("hy_f_wout", [2, 64, 2048])
        self.hy_fvec = self.din("hy_fvec", [2, 64, 4])
        self.hy_bias = self.din("hy_bias", [2, 2, 512])
        self.zT = self.din("zT", [33, S])
        self.zTc = self.din("zTc", [33, CT])
        self.decay = self.din("decay", [S, 512])
        self.decayc = self.din("decayc", [CT, 512])
        for nm in ("TFC", "TFS", "TIC", "TIS"):
            setattr(self, nm, self.din(nm, [S // 128, 128, S // 128, 128], BF16))
            setattr(self, nm + "c", self.din(nm + "c", [CT // 128, 128, CT // 128, 128], BF16))
        self.w_out = self.din("w_out", [2, D, D])
        self.pool_w = self.din("pool_w", [2, 4, 256, 256])
        self.pool_scale = self.din("pool_scale", [2, D])
        self.poolmt = self.din("poolmt", [128, 4, 5, 128], BF16)
        self.out = self.dout("y", [S, D])
        self.hc = self.dscr("hc", [CT, D])
        self.MOD = self.dscr("MOD", [2, DEPTH * 6 * D])

        top = self.es
        self.ps = [top.enter_context(nc.psum_tensor(f"ps{i}", [128, 512], F32)) for i in range(8)]
        for i in range(8):
            TRACKED.add(f"ps{i}")
        self.ident = self.sb(top, "identf", [128, 128], F32)
        self.identb = self.sb(top, "identb", [128, 128], BF16)
        P.ld('sp', self.ident[:], self.ident_in, 'const', 1)
        P.copy('dve', self.identb[:], self.ident[:])
        self.epsc = self.sb(top, "epsc", [128, 1], F32)
        P.memset('dve', self.epsc[:], EPS)

        self.prologue()
        P.barrier()
        for l in range(self.n_layers):
            self.layer(l)
        P.barrier()
        P.emit(self.es)
        self.es.close()
        return nc

    def prologue(self):
        nc, P = self.nc, self.P
        with ExitStack() as st:
            cT = self.sb(st, "cTs", [128, 8, 2])
            sT = self.sb(st, "sTs", [128, 8, 2])
            wt = [self.sb(st, f"adaw{i}", [128, 8, 512]) for i in range(2)]
            bias = [self.sb(st, f"adabs{i}", [2, 6 * D]) for i in range(2)]
            gsb = [self.sb(st, f"gsb{i}", [2, 2 * D]) for i in range(2)]
            mod = [self.sb(st, f"modsb{i}", [2, 6 * D]) for i in range(2)]
            P.ld('sp', cT[:], self.cT, 'const', 1)
            P.act(sT[:], cT[:], AF.Silu)
            n = 0
            for l in range(DEPTH):
                bi, gs, mo = bias[l % 2], gsb[l % 2], mod[l % 2]
                P.ld('sp', bi[:], self.ada_b[l:l + 1, :].partition_broadcast(2), 'pro_b', 2)
                P.ld('sp', gs[:], self.norm_g[l:l + 1].rearrange("o k n -> o (k n)").partition_broadcast(2), 'pro_g', 2)
                for cc in range(12):
                    w = wt[n % 2]
                    P.ld('sp', w[:], self.ada_w[l, :, cc * 512:(cc + 1) * 512].rearrange("(j p) n -> p j n", p=128),
                         'adaw', 2)
                    pst = self.ps[n % 2]
                    for j in range(8):
                        P.mm(pst[0:2, :], sT[:, j, :], w[:, j, :], start=(j == 0), stop=(j == 7))
                    c0 = cc * 512
                    P.tt('dve', mo[:, c0:c0 + 512], pst[0:2, :], bi[:, c0:c0 + 512], ALU.add)
                    n += 1
                for v, k in ((1, 0), (4, 1)):
                    P.stt(mo[:, v * D:(v + 1) * D], mo[:, v * D:(v + 1) * D], 1.0, gs[:, k * D:(k + 1) * D],
                          ALU.add, ALU.mult)
                P.ld('sp', self.MOD[:, l * 6 * D:(l + 1) * 6 * D], mo[:], 'pro_st', 2)

    def modrow(self, l, s, v):
        c0 = l * 6 * D + v * D
        return self.MOD[s:s + 1, c0:c0 + D].partition_broadcast(128)

    def layer(self, l):
        P = self.P
        ctx_mode = ['full', 'full', 'kv', 'none'][l]
        if 'moe_only' in self.dbg:
            self.mixer_passthrough(l, ctx_mode)
        elif l % 2 == 0:
            self.even_proj(l)
            P.barrier()
            if 'no_hyena' not in self.dbg:
                self.hyena_filters(l, S)
                P.barrier()
                self.hyena_conv(l, S, 0)
                P.barrier()
                if ctx_mode == 'full':
                    self.hyena_filters(l, CT)
                    P.barrier()
                    self.hyena_conv(l, CT, S)
                    P.barrier()
            if 'no_attn' not in self.dbg:
                self.even_attn(l, ctx_mode)
            P.barrier()
            self.even_wout(l, ctx_mode)
        else:
            self.odd_pool(l, ctx_mode)
        P.barrier()
        if 'no_moe' in self.dbg:
            return
        self.moe_topk(l, ctx_mode)
        P.barrier()
        self.moe_experts(l, ctx_mode)
        P.barrier()

    def hsrc(self, l, j):
        if j < NLT:
            t = self.x if l == 0 else self.out
            return t[j * 128:(j + 1) * 128, :]
        t = self.ctx if l == 0 else self.hc
        return t[(j - NLT) * 128:(j - NLT + 1) * 128, :]

    def hdst(self, j):
        if j < NLT:
            return self.out[j * 128:(j + 1) * 128, :]
        return self.hc[(j - NLT) * 128:(j - NLT + 1) * 128, :]

    def norm_mod(self, st_bufs, h, a_out, G, Bv, n):
        P = self.P
        junk, ss, sd, rstd, tmp = st_bufs
        k = n % 2
        P.act(junk[k][:], h, AF.Square, accum_out=ss[k][:])
        P.act(sd[k][:], ss[k][:], AF.Sqrt, scale=1.0 / D, bias=self.epsc[:])
        P.recip(rstd[k][:], sd[k][:])
        P.stt(tmp[k][:], h, rstd[k][:], G, ALU.mult, ALU.mult)
        P.tt('dve', a_out, tmp[k][:], Bv, ALU.add)

    def alloc_norm_bufs(self, st):
        junk = [self.sb(st, f"nm_junk{i}", [128, D], BF16) for i in range(2)]
        ss = [self.sb(st, f"nm_ss{i}", [128, 1]) for i in range(2)]
        sd = [self.sb(st, f"nm_sd{i}", [128, 1]) for i in range(2)]
        rstd = [self.sb(st, f"nm_rstd{i}", [128, 1]) for i in range(2)]
        tmp = [self.sb(st, f"nm_tmp{i}", [128, D]) for i in range(2)]
        return junk, ss, sd, rstd, tmp

    def load_modrows(self, st, l, vs, with_ctx, tag):
        P = self.P
        rows = {}
        for s_ in ([0, 1] if with_ctx else [0]):
            for v in vs:
                t = self.sb(st, f"mr_{tag}_{s_}_{v}", [128, D])
                P.ld('sp', t[:], self.modrow(l, s_, v), 'modrow', 1)
                rows[(s_, v)] = t
        return rows

    def even_proj(self, l):
        nc, P = self.nc, self.P
        e = l // 2
        ctx_mode = ['full', 'full', 'kv', 'none'][l]
        ntiles = NT if ctx_mode != 'none' else NLT
        with ExitStack() as st:
            AT = self.sb(st, "AT", [128, 8, T], BF16)
            Win = self.sb(st, "Win", [128, 8, 3072], BF16)
            for g in range(6):
                for j in range(8):
                    P.ld('pool', Win[:, j, g * 512:(g + 1) * 512],
                         self.w_in[e, j * 128:(j + 1) * 128, g * 512:(g + 1) * 512], 'winld', 2,
                         w=[('Win', g)])
            with ExitStack() as st2:
                rows = self.load_modrows(st2, l, [0, 1], ctx_mode != 'none', 'mix')
                nb = self.alloc_norm_bufs(st2)
                hb = [self.sb(st2, f"hb{i}", [128, D]) for i in range(2)]
                ab = [self.sb(st2, f"ab{i}", [128, D], BF16) for i in range(2)]
                for j in range(ntiles):
                    s_ = 0 if j < NLT else 1
                    k = j % 2
                    P.ld('sp', hb[k][:], self.hsrc(l, j), 'hld', 2)
                    self.norm_mod(nb, hb[k][:], ab[k][:], rows[(s_, 1)][:], rows[(s_, 0)][:], j)
                    pst = self.ps[j % 2]
                    pb = pst[:].bitcast(BF16)
                    for c in range(8):
                        P.tr(pb[:, c * 128:(c + 1) * 128], ab[k][:, c * 128:(c + 1) * 128], self.identb[:])
                    src = pb.rearrange("p (c t) -> p c t", c=8)
                    if j % 2 == 0:
                        P.copy('act', AT[:, :, j * 128:(j + 1) * 128], src, w=[('AT', j)])
                    else:
                        P.copy('dve', AT[:, :, j * 128:(j + 1) * 128], src, w=[('AT', j)])
                if 'ATd' in self.dbg:
                    P.ld('sp', self.dscr("ATd", [128, 8, T], BF16), AT[:], 'dbg', 1, r=[('AT', j) for j in range(ntiles)])
            P.barrier()
            self.proj_hyena(l, st, AT, Win, ctx_mode)
            P.barrier()

    def proj_hyena(self, l, st, AT, Win, ctx_mode):
        nc, P = self.nc, self.P
        e = l // 2
        segs = [(0, S)] + ([(S, CT)] if ctx_mode == 'full' else [])
        X2T = self.dscr(f"X2T", [512, T]) if not hasattr(self, 'X2T') else self.X2T
        self.X2T = X2T
        if not hasattr(self, 'X1'):
            self.X1 = self.dscr("X1", [T, 512])
            self.X2 = self.dscr("X2", [T, 512])
            self.VH = self.dscr("VH", [T, 512], BF16)
            self.QT = self.dscr("QT", [4, 128, T], BF16)
            self.KT = self.dscr("KT", [4, 128, T], BF16)
            self.VA = self.dscr("VA", [T, 512], BF16)
        with ExitStack() as st2:
            cw = self.sb(st2, "convw", [128, 12, 4])
            P.ld('sp', cw[:], self.convw[e], 'const', 1)
            PT = [self.sb(st2, f"PT{i}", [128, S + 2]) for i in range(1)]
            UC = [self.sb(st2, f"UC{i}", [128, S]) for i in range(2)]
            UCb = self.sb(st2, "UCb", [128, S], BF16)
            stg = [self.sb(st2, f"stg{i}", [128, 32, 128]) for i in range(1)]
            stgb = [self.sb(st2, f"stgb{i}", [128, 32, 128], BF16) for i in range(1)]
            n = 0
            npsum = 0
            for cc in range(12):
                for (t0, L) in segs:
                    k = n % 2
                    pt, uc = PT[0], UC[k]
                    P.memset('dve', pt[:, 0:1], 0.0)
                    P.memset('dve', pt[:, L + 1:L + 2], 0.0)
                    nb = (L + 511) // 512
                    for tb in range(nb):
                        w_ = min(512, L - tb * 512)
                        pst = self.ps[npsum % 4]
                        npsum += 1
                        c0 = t0 + tb * 512
                        for kk in range(8):
                            P.mm(pst[:, 0:w_], Win[:, kk, cc * 128:(cc + 1) * 128], AT[:, kk, c0:c0 + w_],
                                 start=(kk == 0), stop=(kk == 7),
                                 r=[('Win', cc // 4)] + [('AT', c0 // 128 + i) for i in range(w_ // 128)])
                        if tb % 2 == 0:
                            P.copy('act', pt[:, 1 + tb * 512:1 + tb * 512 + w_], pst[:, 0:w_])
                        else:
                            P.copy('dve', pt[:, 1 + tb * 512:1 + tb * 512 + w_], pst[:, 0:w_])
                    P.ts('dve', uc[:, 0:L], pt[:, 1:L + 1], cw[:, cc, 1:2], cw[:, cc, 3:4], ALU.mult, ALU.add)
                    P.stt(uc[:, 0:L], pt[:, 0:L], cw[:, cc, 0:1], uc[:, 0:L], ALU.mult, ALU.add)
                    P.stt(uc[:, 0:L], pt[:, 2:L + 2], cw[:, cc, 2:3], uc[:, 0:L], ALU.mult, ALU.add)
                    grp = cc // 4
                    ci = cc % 4
                    if grp in (0, 1):
                        Xd = self.X1 if grp == 0 else self.X2
                        sg = stg[0]
                        nt_ = L // 128
                        for q4 in range((nt_ + 3) // 4):
                            pst = self.ps[4 + (npsum % 4)]
                            npsum += 1
                            m4 = min(4, nt_ - q4 * 4)
                            for i in range(m4):
                                tt_ = q4 * 4 + i
                                P.tr(pst[:, i * 128:(i + 1) * 128], uc[:, tt_ * 128:(tt_ + 1) * 128], self.ident[:])
                            P.copy('act', sg[:, q4 * 4:q4 * 4 + m4, :], pst[:, 0:m4 * 128].rearrange("p (a c) -> p a c", a=m4))
                        for a0 in range(0, nt_, 4):
                            a1 = min(nt_, a0 + 4)
                            P.ld('pool', Xd[t0 + a0 * 128:t0 + a1 * 128, ci * 128:(ci + 1) * 128]
                                 .rearrange("(a p) c -> p a c", p=128), sg[:, a0:a1, :], 'x1st', 2)
                    else:
                        sg = stgb[0]
                        nt_ = L // 128
                        P.copy('act', UCb[:, 0:L], uc[:, 0:L])
                        for q8 in range((nt_ + 7) // 8):
                            pst = self.ps[4 + (npsum % 4)]
                            npsum += 1
                            pb = pst[:].bitcast(BF16)
                            m_ = min(8, nt_ - q8 * 8)
                            for i in range(m_):
                                tt_ = q8 * 8 + i
                                P.tr(pb[:, i * 128:(i + 1) * 128], UCb[:, tt_ * 128:(tt_ + 1) * 128], self.identb[:])
                            P.copy('act', sg[:, q8 * 8:q8 * 8 + m_, :],
                                   pb[:, 0:m_ * 128].rearrange("p (a c) -> p a c", a=m_))
                        for a0 in range(0, nt_, 4):
                            a1 = min(nt_, a0 + 4)
                            P.ld('pool', self.VH[t0 + a0 * 128:t0 + a1 * 128, ci * 128:(ci + 1) * 128]
                                 .rearrange("(a p) c -> p a c", p=128), sg[:, a0:a1, :], 'vhst', 2)
                    n += 1
        P.barrier()
        if 'skip_qk' not in self.dbg:
            self.proj_qk(l, AT, Win, ctx_mode)

    def proj_qk(self, l, AT, Win, ctx_mode):
        nc, P = self.nc, self.P
        e = l // 2
        with ExitStack() as st2:
            cosT = self.sb(st2, "cosT", [128, T])
            sinT = self.sb(st2, "sinT", [128, T])
            P.ld('sp', cosT[:], self.rope_cos, 'const', 1)
            P.ld('sp', sinT[:], self.rope_sin, 'const', 1)
            Rm = self.sb(st2, "Rm", [128, 128], BF16)
            bo = self.sb(st2, "blockones", [128, 128], BF16)
            P.ld('sp', Rm[:], self.rope_perm, 'const', 1)
            P.ld('sp', bo[:], self.blockones_in, 'const', 1)
            qkg = self.sb(st2, "qkg", [128, 2])
            P.ld('sp', qkg[:], self.qkg[e], 'const', 1)
            q32 = [self.sb(st2, f"q32_{i}", [128, 512]) for i in range(2)]
            sq = [self.sb(st2, f"sq_{i}", [128, 512], BF16) for i in range(2)]
            sd = [self.sb(st2, f"qsd_{i}", [128, 512]) for i in range(2)]
            rs = [self.sb(st2, f"qrs_{i}", [128, 512]) for i in range(2)]
            qn = [self.sb(st2, f"qn_{i}", [128, 512]) for i in range(2)]
            qnb = [self.sb(st2, f"qnb_{i}", [128, 512], BF16) for i in range(2)]
            t1 = [self.sb(st2, f"qt1_{i}", [128, 512]) for i in range(2)]
            t2 = [self.sb(st2, f"qt2_{i}", [128, 512]) for i in range(2)]
            qo = [self.sb(st2, f"qo_{i}", [128, T], BF16) for i in range(2)]
            n = 0
            for which in ((0, 1) if 'skip_qkloop' not in self.dbg else ()):
                if which == 0:
                    segs = [(0, S)] + ([(S, CT)] if ctx_mode == 'full' else [])
                else:
                    segs = [(0, S)] + ([(S, CT)] if ctx_mode in ('full', 'kv') else [])
                dst = self.QT if which == 0 else self.KT
                for h in range(4):
                    col0 = 1536 + which * 512 + h * 128
                    qout = qo[(which * 4 + h) % 2]
                    tend = 0
                    for (t0, L) in segs:
                        for tb in range((L + 511) // 512):
                            w_ = min(512, L - tb * 512)
                            c0 = t0 + tb * 512
                            k = n % 2
                            pA, pB, pC = self.ps[(n % 2) * 3], self.ps[(n % 2) * 3 + 1], self.ps[(n % 2) * 3 + 2]
                            for kk in range(8):
                                P.mm(pA[:, 0:w_], Win[:, kk, col0:col0 + 128], AT[:, kk, c0:c0 + w_],
                                     start=(kk == 0), stop=(kk == 7),
                                     r=[('Win', col0 // 512)] + [('AT', c0 // 128 + i) for i in range(w_ // 128)])
                            import os
                            lim = int(os.environ.get('QKSTEP', 99))
                            if lim >= 2: P.act(sq[k][:, 0:w_], pA[:, 0:w_], AF.Square)
                            if lim >= 3: P.copy('dve', q32[k][:, 0:w_], pA[:, 0:w_])
                            if lim >= 4: P.mm(pB[:, 0:w_], bo[:], sq[k][:, 0:w_])
                            if lim >= 5: P.act(sd[k][:, 0:w_], pB[:, 0:w_], AF.Sqrt, scale=1.0 / 64, bias=self.epsc[:])
                            if lim >= 6: P.recip(rs[k][:, 0:w_], sd[k][:, 0:w_])
                            if lim >= 7: P.stt(qn[k][:, 0:w_], q32[k][:, 0:w_], qkg[:, which:which + 1], rs[k][:, 0:w_],
                                  ALU.mult, ALU.mult)
                            if lim >= 8: P.copy('act', qnb[k][:, 0:w_], qn[k][:, 0:w_])
                            if lim >= 9: P.mm(pC[:, 0:w_], Rm[:], qnb[k][:, 0:w_])
                            if lim >= 10: P.tt('dve', t1[k][:, 0:w_], qn[k][:, 0:w_], cosT[:, c0:c0 + w_], ALU.mult)
                            if lim >= 11: P.tt('dve', t2[k][:, 0:w_], pC[:, 0:w_], sinT[:, c0:c0 + w_], ALU.mult)
                            if lim >= 12: P.tt('dve', qout[:, c0:c0 + w_], t1[k][:, 0:w_], t2[k][:, 0:w_], ALU.add)
                            n += 1
                        tend = t0 + L
                    if lim >= 13: P.ld('pool', dst[h, :, 0:tend], qout[:, 0:tend], 'qst', 2)
        P.barrier()
        if 'skip_va' in self.dbg:
            return
        with ExitStack() as st2:
            vst = [self.sb(st2, f"vst{i}", [128, 512], BF16) for i in range(2)]
            ntiles = NT if ctx_mode in ('full', 'kv') else NLT
            for j in range(ntiles):
                pst = self.ps[j % 2]
                for kk in range(8):
                    P.mm(pst[:], AT[:, kk, j * 128:(j + 1) * 128], Win[:, kk, 2560:3072],
                         start=(kk == 0), stop=(kk == 7), r=[('Win', 5), ('AT', j)])
                if j % 2 == 0:
                    P.copy('act', vst[j % 2][:], pst[:])
                else:
                    P.copy('dve', vst[j % 2][:], pst[:])
                P.ld('pool', self.VA[j * 128:(j + 1) * 128, :], vst[j % 2][:], 'vast', 2)


    def sin_block(self, B_, out, ps, A, Bc, w_):
        P = self.P
        y, yi, yf, m1, m2 = B_
        P.ts('dve', y[:, 0:w_], ps, A, Bc, ALU.mult, ALU.add)
        P.copy('dve', yi[:, 0:w_], y[:, 0:w_])
        P.copy('dve', yf[:, 0:w_], yi[:, 0:w_])
        P.tt('dve', y[:, 0:w_], y[:, 0:w_], yf[:, 0:w_], ALU.subtract)
        P.ts('dve', m1[:, 0:w_], y[:, 0:w_], 0.5, None, ALU.is_gt)
        P.ts('dve', m2[:, 0:w_], y[:, 0:w_], -0.5, None, ALU.is_lt)
        P.tt('dve', y[:, 0:w_], y[:, 0:w_], m1[:, 0:w_], ALU.subtract)
        P.tt('dve', y[:, 0:w_], y[:, 0:w_], m2[:, 0:w_], ALU.add)
        P.act(out, y[:, 0:w_], AF.Sin, scale=2.0 * math.pi * (1.0 - 1e-6))

    def dft_forward(self, st, L, tabs, srcA, srcB, evac):
        P = self.P
        TC, TS, TFC, TFS = tabs
        nch = L // 128
        for fc in range(nch):
            tc_, ts_ = TC[fc % 2], TS[fc % 2]
            P.ld('sp', tc_[:, 0:nch, :], TFC[fc], 'tabc', 2)
            P.ld('sp', ts_[:, 0:nch, :], TFS[fc], 'tabs', 2)
            pA, pB = self.ps[(fc % 2) * 2], self.ps[(fc % 2) * 2 + 1]
            for sc in range(nch):
                P.mm(pA[:], tc_[:, sc, :], srcA[:, sc, :], start=(sc == 0), stop=(sc == nch - 1))
            for sc in range(nch):
                P.mm(pB[:], ts_[:, sc, :], srcB[:, sc, :], start=(sc == 0), stop=(sc == nch - 1))
            evac(fc, pA, pB)

    def hyena_tables(self, L):
        sfx = "" if L == S else "c"
        return (getattr(self, "TFC" + sfx), getattr(self, "TFS" + sfx), getattr(self, "TIC" + sfx), getattr(self, "TIS" + sfx))

    def hyena_filters(self, l, L):
        P = self.P
        e = l // 2
        sfx = "" if L == S else "c"
        nch = L // 128
        if not hasattr(self, 'KR' + sfx):
            setattr(self, 'KR' + sfx, self.dscr('KR' + sfx, [2, L, 512]))
            setattr(self, 'KI' + sfx, self.dscr('KI' + sfx, [2, L, 512]))
        KR, KI = getattr(self, 'KR' + sfx), getattr(self, 'KI' + sfx)
        zTd = self.zT if L == S else self.zTc
        decd = self.decay if L == S else self.decayc
        TFC, TFS, _, _ = self.hyena_tables(L)
        with ExitStack() as st:
            zT = self.sb(st, "zTs", [33, L])
            P.ld('sp', zT[:], zTd, 'const', 1)
            w1 = self.sb(st, "fw1", [33, 64])
            w2 = self.sb(st, "fw2", [64, 64])
            w3 = self.sb(st, "fw3", [64, 64])
            wo = self.sb(st, "fwo", [64, 2048])
            P.ld('sp', w1[:], self.hy_f_w1[e], 'const', 1)
            P.ld('sp', w2[:], self.hy_f_w2[e], 'const', 1)
            P.ld('sp', w3[:], self.hy_f_w3[e], 'const', 1)
            P.ld('sp', wo[:], self.hy_f_wout[e], 'const', 1)
            fv = self.sb(st, "fvec", [64, 4])
            P.ld('sp', fv[:], self.hy_fvec[e], 'const', 1)
            A = self.sb(st, "fA", [64, 1])
            Bc = self.sb(st, "fBc", [64, 3])
            P.ts('dve', A[:], fv[:, 0:1], 1.0 / (2.0 * math.pi), None, ALU.mult)
            for i in range(3):
                P.tt('dve', Bc[:, i:i + 1], fv[:, i + 1:i + 2], A[:], ALU.mult)
            B_ = (self.sb(st, "sy", [64, 512]), self.sb(st, "syi", [64, 512], I32), self.sb(st, "syf", [64, 512]),
                  self.sb(st, "sm1", [64, 512]), self.sb(st, "sm2", [64, 512]))
            h1 = self.sb(st, "fh1", [64, 512])
            h2 = self.sb(st, "fh2", [64, 512])
            h3T = self.sb(st, "fh3T", [64, L])
            for cb in range((L + 511) // 512):
                w_ = min(512, L - cb * 512)
                c0 = cb * 512
                ps = self.ps[cb % 2]
                P.mm(ps[0:64, 0:w_], w1[:], zT[:, c0:c0 + w_])
                self.sin_block(B_, h1[:, 0:w_], ps[0:64, 0:w_], A[:], Bc[:, 0:1], w_)
                ps2 = self.ps[2 + cb % 2]
                P.mm(ps2[0:64, 0:w_], w2[:], h1[:, 0:w_])
                self.sin_block(B_, h2[:, 0:w_], ps2[0:64, 0:w_], A[:], Bc[:, 1:2], w_)
                ps3 = self.ps[4 + cb % 2]
                P.mm(ps3[0:64, 0:w_], w3[:], h2[:, 0:w_])
                self.sin_block(B_, h3T[:, c0:c0 + w_], ps3[0:64, 0:w_], A[:], Bc[:, 2:3], w_)
            if 'H3T' in self.dbg and L == S:
                P.ld('sp', self.dscr("H3T", [64, L]), h3T[:], 'dbg', 1)
            KS = self.sb(st, "KS", [128, nch, 512], BF16)
            KD = self.sb(st, "KD", [128, nch, 512], BF16)
            TC = [self.sb(st, f"TCk{i}", [128, nch, 128], BF16) for i in range(2)]
            TS = [self.sb(st, f"TSk{i}", [128, nch, 128], BF16) for i in range(2)]
            dec = [self.sb(st, f"dec{i}", [128, 512]) for i in range(2)]
            hf = [self.sb(st, f"hf{i}", [128, 512]) for i in range(2)]
            hb = [self.sb(st, f"hbk{i}", [128, 512]) for i in range(2)]
            brow = self.sb(st, "hybias", [1, 2, 512])
            P.ld('sp', brow[:], self.hy_bias[e:e + 1], 'const', 1)
            kr = [self.sb(st, f"kr{i}", [128, 512]) for i in range(2)]
            ki = [self.sb(st, f"ki{i}", [128, 512]) for i in range(2)]
            sc2 = 2.0 / (2 * L)
            for o in range(2):
                for tt_ in range(nch):
                    k = tt_ % 2
                    P.ld('sp', dec[k][:], decd[tt_ * 128:(tt_ + 1) * 128, :], 'decld', 2)
                    pf, pb_ = self.ps[4 + k * 2], self.ps[5 + k * 2]
                    P.mm(pf[:], h3T[:, tt_ * 128:(tt_ + 1) * 128], wo[:, (o * 2) * 512:(o * 2 + 1) * 512])
                    P.mm(pb_[:], h3T[:, tt_ * 128:(tt_ + 1) * 128], wo[:, (o * 2 + 1) * 512:(o * 2 + 2) * 512])
                    P.tt('dve', hf[k][:], pf[:], dec[k][:], ALU.mult)
                    P.tt('dve', hb[k][:], pb_[:], dec[k][:], ALU.mult)
                    if tt_ == 0:
                        P.tt('dve', hf[k][0:1, :], hf[k][0:1, :], brow[0:1, o, :], ALU.add)
                    P.tt('dve', KS[:, tt_, :], hf[k][:], hb[k][:], ALU.add)
                    P.tt('dve', KD[:, tt_, :], hb[k][:], hf[k][:], ALU.subtract)
                if 'KSd' in self.dbg and L == S and o == 0:
                    P.ld('sp', self.dscr("KSd", [128, nch, 512], BF16), KS[:], 'dbg', 1)

                def evac(fc, pA, pB, o=o):
                    k = fc % 2
                    P.ts('dve', kr[k][:], pA[:], sc2, None, ALU.mult)
                    P.act(ki[k][:], pB[:], AF.Copy, scale=sc2)
                    P.ld('pool', KR[o, fc * 128:(fc + 1) * 128, :], kr[k][:], 'krst', 2)
                    P.ld('pool', KI[o, fc * 128:(fc + 1) * 128, :], ki[k][:], 'kist', 2)
                self.dft_forward(st, L, (TC, TS, TFC, TFS), KS, KD, evac)

    def hyena_conv(self, l, L, t0):
        P = self.P
        sfx = "" if L == S else "c"
        nch = L // 128
        KR, KI = getattr(self, 'KR' + sfx), getattr(self, 'KI' + sfx)
        TFC, TFS, TIC, TIS = self.hyena_tables(L)
        if not hasattr(self, 'MIXT'):
            self.MIXT = self.dscr("MIXT", [D, T], BF16)
        with ExitStack() as st:
            U = [self.sb(st, f"U{i}", [128, nch, 512], BF16) for i in range(2)]
            Yr = self.sb(st, "Yr", [128, nch, 512], BF16)
            Yi = self.sb(st, "Yi", [128, nch, 512], BF16)
            TC = [self.sb(st, f"TCc{i}", [128, nch, 128], BF16) for i in range(2)]
            TS = [self.sb(st, f"TSc{i}", [128, nch, 128], BF16) for i in range(2)]
            kr = [self.sb(st, f"ckr{i}", [128, 512]) for i in range(2)]
            ki = [self.sb(st, f"cki{i}", [128, 512]) for i in range(2)]
            tmp = [self.sb(st, f"ctmp{i}", [128, 512]) for i in range(4)]
            gt = [self.sb(st, f"cgt{i}", [128, 512]) for i in range(2)]
            zb = [self.sb(st, f"czb{i}", [128, 512], BF16) for i in range(2)]
            zT = [self.sb(st, f"czT{i}", [128, 4, 128], BF16) for i in range(2)]
            for a0 in range(0, nch, 8):
                a1 = min(nch, a0 + 8)
                P.ld('sp', U[0][:, a0:a1, :], self.VH[t0 + a0 * 128:t0 + a1 * 128, :].rearrange("(a p) c -> p a c", p=128),
                     'uld', 2, w=[(U[0].name, a0 // 8)])
            for o in range(2):
                Uin, Uout = U[o % 2], U[(o + 1) % 2]
                gate_d = self.X1 if o == 0 else self.X2

                def evac(fc, pA, pB, o=o):
                    k = fc % 2
                    P.ld('sp', kr[k][:], KR[o, fc * 128:(fc + 1) * 128, :], 'krld', 2)
                    P.ld('sp', ki[k][:], KI[o, fc * 128:(fc + 1) * 128, :], 'kild', 2)
                    P.tt('dve', tmp[0][:], pA[:], kr[k][:], ALU.mult)
                    P.tt('dve', tmp[1][:], pB[:], ki[k][:], ALU.mult)
                    P.tt('pool', Yr[:, fc, :], tmp[0][:], tmp[1][:], ALU.add, w=[(Yr.name, fc)])
                    P.tt('dve', tmp[2][:], pA[:], ki[k][:], ALU.mult)
                    P.tt('dve', tmp[3][:], pB[:], kr[k][:], ALU.mult)
                    P.tt('pool', Yi[:, fc, :], tmp[2][:], tmp[3][:], ALU.subtract, w=[(Yi.name, fc)])
                if o == 0:
                    rU = [(Uin.name, a) for a in range((nch + 7) // 8)]
                else:
                    rU = [(Uin.name, 'z', a) for a in range(nch)]
                self._dft_fwd_tok(L, (TC, TS, TFC, TFS), Uin, rU, evac)
                for tc in range(nch):
                    tc_, ts_ = TC[tc % 2], TS[tc % 2]
                    P.ld('sp', tc_[:, 0:nch, :], TIC[tc], 'tabc', 2)
                    P.ld('sp', ts_[:, 0:nch, :], TIS[tc], 'tabs', 2)
                    pY = self.ps[4 + tc % 2]
                    for fk in range(nch):
                        P.mm(pY[:], tc_[:, fk, :], Yr[:, fk, :], start=(fk == 0), stop=False, r=[tc_.name, (Yr.name, fk)])
                    for fk in range(nch):
                        P.mm(pY[:], ts_[:, fk, :], Yi[:, fk, :], start=False, stop=(fk == nch - 1), r=[ts_.name, (Yi.name, fk)])
                    g_ = gt[tc % 2]
                    P.ld('sp', g_[:], gate_d[t0 + tc * 128:t0 + (tc + 1) * 128, :], 'gld', 2)
                    if o == 0:
                        P.tt('dve', Uout[:, tc, :], pY[:], g_[:], ALU.mult, w=[(Uout.name, 'z', tc)])
                    else:
                        z_ = zb[tc % 2]
                        P.tt('dve', z_[:], pY[:], g_[:], ALU.mult)
                        pb = self.ps[6 + tc % 2][:].bitcast(BF16)
                        for c in range(4):
                            P.tr(pb[:, c * 128:(c + 1) * 128], z_[:, c * 128:(c + 1) * 128], self.identb[:])
                        zt_ = zT[tc % 2]
                        P.copy('act', zt_[:], pb[:, 0:512].rearrange("p (c t) -> p c t", c=4))
                        P.ld('pool', self.MIXT[0:512, t0 + tc * 128:t0 + (tc + 1) * 128].rearrange("(c p) t -> p c t", p=128),
                             zt_[:], 'hyst', 2)

    def _dft_fwd_tok(self, L, tabs, src, rsrc, evac):
        P = self.P
        TC, TS, TFC, TFS = tabs
        nch = L // 128
        for fc in range(nch):
            tc_, ts_ = TC[fc % 2], TS[fc % 2]
            P.ld('sp', tc_[:, 0:nch, :], TFC[fc], 'tabc', 2)
            P.ld('sp', ts_[:, 0:nch, :], TFS[fc], 'tabs', 2)
            pA, pB = self.ps[(fc % 2) * 2], self.ps[(fc % 2) * 2 + 1]
            for sc in range(nch):
                P.mm(pA[:], tc_[:, sc, :], src[:, sc, :], start=(sc == 0), stop=(sc == nch - 1), r=[tc_.name] + rsrc)
            for sc in range(nch):
                P.mm(pB[:], ts_[:, sc, :], src[:, sc, :], start=(sc == 0), stop=(sc == nch - 1), r=[ts_.name] + rsrc)
            evac(fc, pA, pB)

    def even_attn(self, l, ctx_mode):
        nc, P = self.nc, self.P
        e = l // 2
        lam_init = 0.8 - 0.6 * math.exp(-0.3 * l)
        if not hasattr(self, 'MIXT'):
            self.MIXT = self.dscr("MIXT", [D, T], BF16)
        with ExitStack() as st:
            ones = self.sb(st, "ones_bf", [128, 128], BF16)
            P.memset('dve', ones[:], 1.0)
            lv = self.sb(st, "lv", [128, 4, 64])
            P.ld('sp', lv[:], self.diff_lambda[e:e + 1].rearrange("o a d -> o (a d)").partition_broadcast(128), 'const', 1)
            pr = self.sb(st, "lvpr", [128, 2, 64])
            s2 = self.sb(st, "lvs", [128, 2])
            P.tt('dve', pr[:, 0, :], lv[:, 0, :], lv[:, 1, :], ALU.mult)
            P.tt('dve', pr[:, 1, :], lv[:, 2, :], lv[:, 3, :], ALU.mult)
            P.op('dve', lambda en: en.tensor_reduce(s2[:], pr[:], AX.X, ALU.add), [pr.name], [s2.name])
            ex = self.sb(st, "lvex", [128, 2])
            P.act(ex[:], s2[:], AF.Exp)
            neglam = self.sb(st, "neglam", [128, 1])
            P.tt('dve', neglam[:], ex[:, 1:2], ex[:, 0:1], ALU.subtract)
            P.ts('dve', neglam[:], neglam[:], -lam_init, None, ALU.add)
            gsc = self.sb(st, "gsc", [128, 1])
            P.ld('sp', gsc[:], self.subln_g[e].rearrange("(p o) -> p o", o=1), 'const', 1)
            P.ts('dve', gsc[:], gsc[:], 1.0 - lam_init, None, ALU.mult)
            K0s = [self.sb(st, f"K0s{i}", [128, T], BF16) for i in range(2)]
            K1s = [self.sb(st, f"K1s{i}", [128, T], BF16) for i in range(2)]
            for i in range(2):
                P.memset('dve', K0s[i][64:128, :], 0.0)
                P.memset('dve', K1s[i][0:64, :], 0.0)
            QTs = [self.sb(st, f"QTs{i}", [128, T], BF16) for i in range(2)]
            Vs = [self.sb(st, f"Vs{i}", [128, NT, 128], BF16) for i in range(2)]
            PT_ = [self.sb(st, f"PTe{i}", [128, 512], BF16) for i in range(4)]
            r_ = [self.sb(st, f"att_r{i}", [128, 512]) for i in range(2)]
            o_ = [self.sb(st, f"att_o{i}", [128, 512]) for i in range(2)]
            oo = self.sb(st, "att_oo", [128, 512])
            sq = self.sb(st, "att_sq", [128, 512], BF16)
            sd = self.sb(st, "att_sd", [128, 512])
            ob = [self.sb(st, f"att_ob{i}", [128, 512], BF16) for i in range(2)]
            ones32 = self.sb(st, "ones_f32", [128, 128])
            P.memset('dve', ones32[:], 1.0)
            accD = [[self.sb(st, f"accD{i}_{m}", [128, 512]) for m in range(2)] for i in range(2)]
            accP = [[self.sb(st, f"accP{i}_{m}", [128, 512]) for m in range(2)] for i in range(2)]
            pSb = [self.ps[0], self.ps[1], self.ps[7]]
            pO = [self.ps[2], self.ps[3]]
            pD = [self.ps[4], self.ps[5]]
            LOOK = 2
            npt = 0
            nq = 0
            for h in range(4):
                k0_, k1_, qt_, v_ = K0s[h % 2], K1s[h % 2], QTs[h % 2], Vs[h % 2]
                P.ld('sp', k0_[0:64, :], self.KT[h, 0:64, :], 'attk', 2)
                P.ld('sp', k1_[64:128, :], self.KT[h, 64:128, :], 'attk1', 2)
                P.ld('sp', qt_[:], self.QT[h], 'attq', 2)
                for a0 in range(0, NT, 8):
                    a1 = min(NT, a0 + 8)
                    P.ld('sp', v_[:, a0:a1, :], self.VA[a0 * 128:a1 * 128, h * 128:(h + 1) * 128]
                         .rearrange("(a p) c -> p a c", p=128), 'attv', 2, w=[(v_.name, a0 // 8)])
                qblocks = [(qb * 512, 512, list(range(NT))) for qb in range(8)]
                if ctx_mode == 'full':
                    qblocks.append((S, CT, [NLT, NLT + 1]))
                items = []
                for (q0, qw, ktiles) in qblocks:
                    for ki, kt in enumerate(ktiles):
                        for m in range(2):
                            items.append((q0, qw, ki, kt, m, len(ktiles)))

                def emit_qk(it, idx):
                    q0, qw, ki, kt, m, nk = it
                    pS = pSb[idx % 3]
                    km = k0_ if m == 0 else k1_
                    P.mm(pS[:, 0:qw], km[:, kt * 128:(kt + 1) * 128], qt_[:, q0:q0 + qw])

                def emit_pv(it, idx):
                    nonlocal nq
                    q0, qw, ki, kt, m, nk = it
                    pS = pSb[idx % 3]
                    pt = PT_[idx % 4]
                    P.act(pt[:, 0:qw], pS[:, 0:qw], AF.Exp, scale=0.125)
                    P.mm(pO[m][:, 0:qw], v_[:, kt, :], pt[:, 0:qw], start=(ki == 0), stop=(ki == nk - 1),
                         r=[(v_.name, kt // 8), pt.name])
                    qpar = (q0 // 512) % 2
                    if ki % 3 == 2:
                        acc, eng = accP[qpar][m], 'pool'
                        first = (ki == 2)
                    else:
                        acc, eng = accD[qpar][m], 'dve'
                        first = (ki == 0)
                    if first:
                        P.copy(eng, acc[:, 0:qw], pt[:, 0:qw])
                    else:
                        P.tt(eng, acc[:, 0:qw], acc[:, 0:qw], pt[:, 0:qw], ALU.add)
                    if ki == nk - 1 and m == 1:
                        for mm_ in range(2):
                            usep = nk > 2
                            P.mm(pD[mm_][:, 0:qw], ones32[:], accD[qpar][mm_][:, 0:qw], start=True, stop=not usep)
                            if usep:
                                P.mm(pD[mm_][:, 0:qw], ones32[:], accP[qpar][mm_][:, 0:qw], start=False, stop=True)
                        for mm_ in range(2):
                            P.recip(r_[mm_][:, 0:qw], pD[mm_][:, 0:qw])
                            P.tt('dve', o_[mm_][:, 0:qw], pO[mm_][:, 0:qw], r_[mm_][:, 0:qw], ALU.mult)
                        P.stt(oo[:, 0:qw], o_[1][:, 0:qw], neglam[:], o_[0][:, 0:qw], ALU.mult, ALU.add)
                        P.act(sq[:, 0:qw], oo[:, 0:qw], AF.Square)
                        pM = self.ps[6]
                        P.mm(pM[:, 0:qw], ones[:], sq[:, 0:qw])
                        P.act(sd[:, 0:qw], pM[:, 0:qw], AF.Sqrt, scale=1.0 / 128, bias=self.epsc[:])
                        P.recip(sd[:, 0:qw], sd[:, 0:qw])
                        obk = ob[nq % 2]
                        nq += 1
                        P.stt(obk[:, 0:qw], oo[:, 0:qw], gsc[:], sd[:, 0:qw], ALU.mult, ALU.mult)
                        P.ld('pool', self.MIXT[512 + h * 128:512 + (h + 1) * 128, q0:q0 + qw], obk[:, 0:qw], 'attst', 2)

                base = npt
                for i in range(min(LOOK, len(items))):
                    emit_qk(items[i], base + i)
                for i in range(len(items)):
                    if i + LOOK < len(items):
                        emit_qk(items[i + LOOK], base + i + LOOK)
                    emit_pv(items[i], base + i)
                npt += len(items)

    def epi_alloc(self, st, l, ctx_mode):
        P = self.P
        E = {}
        E['rows'] = self.load_modrows(st, l, [3, 4], ctx_mode == 'full', 'ffn')
        E['nb'] = self.alloc_norm_bufs(st)
        E['f32'] = [self.sb(st, f"f32_{i}", [128, D]) for i in range(2)]
        E['fb'] = [self.sb(st, f"fb_{i}", [128, D], BF16) for i in range(2)]
        E['fT'] = [self.sb(st, f"fT_{i}", [128, 8, 128]) for i in range(2)]
        E['rw'] = self.sb(st, "rw32", [128, 8, NE])
        P.ld('sp', E['rw'][:], self.router_w[l].rearrange("(k p) e -> p k e", p=128), 'const', 1)
        E['aff'] = self.sb(st, "AFFsb", [128, NT, NE])
        E['mx'] = [self.sb(st, f"rmx{i}", [128, 1]) for i in range(2)]
        E['sm'] = [self.sb(st, f"rsm{i}", [128, 1]) for i in range(2)]
        E['ex'] = [self.sb(st, f"rex{i}", [128, NE]) for i in range(2)]
        if not hasattr(self, 'F'):
            self.F = self.dscr("F", [T, D], BF16)
            self.AFFD = self.dscr("AFFD", [128, NT, NE])
        return E

    def epilogue_tile(self, E, l, j, hnew, n):
        P = self.P
        s_ = 0 if j < NLT else 1
        k = n % 2
        f32, fb, fT = E['f32'][k], E['fb'][k], E['fT'][k]
        self.norm_mod(E['nb'], hnew, f32[:], E['rows'][(s_, 4)][:], E['rows'][(s_, 3)][:], n)
        P.copy('act', fb[:], f32[:])
        P.ld('pool', self.F[j * 128:(j + 1) * 128, :], fb[:], 'fst', 2)
        pa, pb_, pl = self.ps[6], self.ps[7], self.ps[5]
        for c in range(8):
            pp = pa if c < 4 else pb_
            P.tr(pp[:, (c % 4) * 128:(c % 4 + 1) * 128], f32[:, c * 128:(c + 1) * 128], self.ident[:])
        P.copy('act', fT[:, 0:4, :], pa[:].rearrange("p (c t) -> p c t", c=4))
        P.copy('dve', fT[:, 4:8, :], pb_[:].rearrange("p (c t) -> p c t", c=4))
        for c in range(8):
            P.mm(pl[:, 0:NE], fT[:, c, :], E['rw'][:, c, :], start=(c == 0), stop=(c == 7))
        mx, sm, ex = E['mx'][k], E['sm'][k], E['ex'][k]
        P.op('dve', lambda e: e.tensor_reduce(mx[:], pl[:, 0:NE], AX.X, ALU.max, negate=True),
             ['ps5'], [mx.name])
        P.act(ex[:], pl[:, 0:NE], AF.Exp, bias=mx[:], accum_out=sm[:])
        P.recip(sm[:], sm[:])
        P.ts('dve', E['aff'][:, j, :], ex[:], sm[:], None, ALU.mult)

    def even_wout(self, l, ctx_mode):
        P = self.P
        e = l // 2
        ntiles = NT if ctx_mode == 'full' else NLT
        with ExitStack() as st:
            E = self.epi_alloc(st, l, ctx_mode)
            g1 = self.load_modrows(st, l, [2], ctx_mode == 'full', 'g1')
            Wo = self.sb(st, "Wout", [128, 8, D], BF16)
            P.ld('pool', Wo[:], self.w_out[e].rearrange("(k p) n -> p k n", p=128), 'woutld', 1)
            Mt = [self.sb(st, f"Mt{i}", [128, 8, 512], BF16) for i in range(2)]
            hb = [self.sb(st, f"hbw{i}", [128, D]) for i in range(2)]
            hn = [self.sb(st, f"hnw{i}", [128, D]) for i in range(2)]
            for j in range(ntiles):
                s_ = 0 if j < NLT else 1
                if j % 4 == 0:
                    nt4 = min(4, ntiles - j)
                    m_ = Mt[(j // 4) % 2]
                    P.ld('sp', m_[:, :, 0:nt4 * 128], self.MIXT[:, j * 128:(j + nt4) * 128].rearrange("(k p) t -> p k t", p=128),
                         'mixld', 2)
                m_ = Mt[(j // 4) % 2]
                k = j % 2
                P.ld('sp', hb[k][:], self.hsrc(l, j), 'hld', 2)
                for half in range(2):
                    pY = self.ps[(j % 2) * 2 + half]
                    for kk in range(8):
                        P.mm(pY[:], m_[:, kk, (j % 4) * 128:(j % 4 + 1) * 128], Wo[:, kk, half * 512:(half + 1) * 512],
                             start=(kk == 0), stop=(kk == 7))
                    hs = slice(half * 512, (half + 1) * 512)
                    P.tt('dve', hn[k][:, hs], pY[:], g1[(s_, 2)][:, hs], ALU.mult)
                    P.tt('dve', hn[k][:, hs], hn[k][:, hs], hb[k][:, hs], ALU.add)
                P.ld('pool', self.hdst(j), hn[k][:], 'hst', 2)
                self.epilogue_tile(E, l, j, hn[k][:], j)
            P.ld('pool', self.AFFD, E['aff'][:], 'affst', 1)

    def odd_pool(self, l, ctx_mode):
        P = self.P
        o = l // 2
        ntiles = NT if ctx_mode == 'full' else NLT
        with ExitStack() as st:
            E = self.epi_alloc(st, l, ctx_mode)
            A = self.sb(st, "A_tm", [128, ntiles, D], BF16)
            with ExitStack() as st2:
                rows = self.load_modrows(st2, l, [0, 1], ctx_mode == 'full', 'mixp')
                hb = [self.sb(st2, f"hbp{i}", [128, D]) for i in range(2)]
                for j in range(ntiles):
                    s_ = 0 if j < NLT else 1
                    P.ld('sp', hb[j % 2][:], self.hsrc(l, j), 'hld', 2)
                    self.norm_mod(E['nb'], hb[j % 2][:], A[:, j, :], rows[(s_, 1)][:], rows[(s_, 0)][:], j)
            P.barrier()
            g1 = self.load_modrows(st, l, [2], ctx_mode == 'full', 'g1p')
            psr = self.sb(st, "pscale", [128, D])
            P.ld('sp', psr[:], self.pool_scale[o:o + 1, :].partition_broadcast(128), 'const', 1)
            for k_ in g1:
                P.tt('dve', g1[k_][:], g1[k_][:], psr[:], ALU.mult)
            MT = self.sb(st, "poolMT", [128, 4, 5, 128], BF16)
            P.ld('sp', MT[:], self.poolmt, 'const', 1)
            PW = self.sb(st, "poolW", [128, 4, 2, 256], BF16)
            P.ld('pool', PW[:], self.pool_w[o].rearrange("g (c p) d -> p g c d", p=128), 'woutld', 1)
            pT = [self.sb(st, f"poolpT{i}", [128, 8, 128], BF16) for i in range(2)]
            hb = [self.sb(st, f"hbq{i}", [128, D]) for i in range(2)]
            hn = [self.sb(st, f"hnq{i}", [128, D]) for i in range(2)]
            for j in range(ntiles):
                s_ = 0 if j < NLT else 1
                first = j in (0, NLT)
                last = j in (NLT - 1, NT - 1)
                k = j % 2
                P.ld('sp', hb[k][:], self.hsrc(l, j), 'hld', 2)
                pp = [self.ps[(j % 2) * 2], self.ps[(j % 2) * 2 + 1]]
                for cc in range(8):
                    g = cc // 2
                    dst = pp[cc // 4][:, (cc % 4) * 128:(cc % 4 + 1) * 128]
                    cs = slice(cc * 128, (cc + 1) * 128)
                    terms = []
                    if not first:
                        terms.append((A[:, j - 1, cs], MT[:, g, 3, :]))
                    terms.append((A[:, j, cs], MT[:, g, 1 if first else (2 if last else 0), :]))
                    if not last:
                        terms.append((A[:, j + 1, cs], MT[:, g, 4, :]))
                    for ti, (lh, rh) in enumerate(terms):
                        P.mm(dst, lh, rh, start=(ti == 0), stop=(ti == len(terms) - 1))
                pt = pT[k]
                P.copy('act', pt[:, 0:4, :], pp[0][:].rearrange("p (c t) -> p c t", c=4))
                P.copy('dve', pt[:, 4:8, :], pp[1][:].rearrange("p (c t) -> p c t", c=4))
                for half in range(2):
                    pY = self.ps[4 + (j % 2) * 2 + half]
                    for gg in range(2):
                        g = half * 2 + gg
                        for cc in range(2):
                            P.mm(pY[:, gg * 256:(gg + 1) * 256], pt[:, g * 2 + cc, :], PW[:, g, cc, :],
                                 start=(cc == 0), stop=(cc == 1))
                    hs = slice(half * 512, (half + 1) * 512)
                    P.tt('dve', hn[k][:, hs], pY[:], g1[(s_, 2)][:, hs], ALU.mult)
                    P.tt('dve', hn[k][:, hs], hn[k][:, hs], hb[k][:, hs], ALU.add)
                P.ld('pool', self.hdst(j), hn[k][:], 'hst', 2)
                self.epilogue_tile(E, l, j, hn[k][:], j)
            P.ld('pool', self.AFFD, E['aff'][:], 'affst', 1)

    def mixer_passthrough(self, l, ctx_mode):
        P = self.P
        ntiles = NT if ctx_mode == 'full' else NLT
        with ExitStack() as st:
            E = self.epi_alloc(st, l, ctx_mode)
            hb = [self.sb(st, f"hb{i}", [128, D]) for i in range(2)]
            for j in range(ntiles):
                P.ld('sp', hb[j % 2][:], self.hsrc(l, j), 'hld', 2)
                if l == 0:
                    P.ld('sp', self.hdst(j), hb[j % 2][:], 'hst', 2)
                self.epilogue_tile(E, l, j, hb[j % 2][:], j)
            P.ld('pool', self.AFFD, E['aff'][:], 'affst', 1)

    def moe_topk(self, l, ctx_mode):
        P = self.P
        groups = [(0, NLT, 512)] + ([(NLT, 2, 32)] if ctx_mode == 'full' else [])
        if not hasattr(self, 'IDXD'):
            self.IDXD = self.dscr("IDXD", [128, NE, 5], I32)
            self.GD = self.dscr("GD", [128, NE, 5])
        with ExitStack() as st:
            aff = self.sb(st, "aff_tm", [128, NT, NE])
            P.ld('sp', aff[:], self.AFFD, 'const', 1)
            ones = self.sb(st, "ones_scan", [NE, S])
            P.memset('dve', ones[:], 1.0)
            iota32 = self.sb(st, "iota512f", [128, 512])
            P.op('pool', lambda e: e.iota(iota32[:], [[1, 512]], base=0, channel_multiplier=0,
                                          allow_small_or_imprecise_dtypes=True), [], [iota32.name])
            iota = self.sb(st, "iota512", [128, 512], F16)
            P.copy('dve', iota[:], iota32[:])
            slot16 = self.sb(st, "slot_tm16", [128, NT, NE], F16)
            slot = self.sb(st, "slot_tm", [128, NT, NE])
            rhs = self.sb(st, "ohrhs", [128, NT, NE, 4], BF16)
            tokhl = self.sb(st, "tokhl", [128, NT, 2])
            P.ld('sp', tokhl[:], self.tokhl_in, 'const', 1)
            ahi = self.sb(st, "ahi", [128, NT, NE], BF16)
            alo = self.sb(st, "alo", [128, NT, NE])
            P.copy('dve', ahi[:], aff[:])
            P.tt('dve', alo[:], aff[:], ahi[:], ALU.subtract)
            for j in range(NT):
                P.copy('dve', rhs[:, j, :, 0:2], tokhl[:, j:j + 1, :].to_broadcast([128, NE, 2]))
            P.copy('dve', rhs[:, :, :, 2], ahi[:])
            P.copy('dve', rhs[:, :, :, 3], alo[:])
            with ExitStack() as st2:
                G_ = []
                for gi, (j0, ntl, cap) in enumerate(groups):
                    L = ntl * 128
                    g_ = dict(j0=j0, ntl=ntl, cap=cap, L=L)
                    g_['affT'] = self.sb(st2, f"affT{gi}", [NE, L])
                    g_['junk'] = self.sb(st2, f"tk_junk{gi}", [NE, L])
                    g_['maskT'] = self.sb(st2, f"maskT{gi}", [NE, L])
                    g_['posT'] = self.sb(st2, f"posT{gi}", [NE, L])
                    g_['slotT'] = self.sb(st2, f"slotT{gi}", [NE, L])
                    for n_ in ('lo', 'hi', 'mid', 'cnt', 'pred', 'd', 'd2'):
                        g_[n_] = self.sb(st2, f"tk_{n_}{gi}", [NE, 1])
                    G_.append(g_)
                for g_ in G_:
                    ntl, j0, affT = g_['ntl'], g_['j0'], g_['affT']
                    for q in range((ntl + 3) // 4):
                        pst = self.ps[q % 2]
                        m_ = min(4, ntl - q * 4)
                        for i in range(m_):
                            P.tr(pst[0:NE, i * 128:(i + 1) * 128], aff[:, j0 + q * 4 + i, :], self.ident[:])
                        P.copy('act', affT[:, q * 512:q * 512 + m_ * 128], pst[0:NE, 0:m_ * 128])
                    P.memset('dve', g_['lo'][:], 0.0)
                    P.memset('dve', g_['hi'][:], 1.0)
                for it in range(30):
                    for g_ in G_:
                        P.ts('dve', g_['mid'][:], g_['lo'][:], g_['hi'][:], 0.5, ALU.add, ALU.mult)
                    for g_ in G_:
                        P.ts('dve', g_['junk'][:], g_['affT'][:], g_['mid'][:], None, ALU.is_ge, ALU.add, accum_out=g_['cnt'][:])
                    for g_ in G_:
                        P.ts('dve', g_['pred'][:], g_['cnt'][:], float(g_['cap']), None, ALU.is_ge)
                    for g_ in G_:
                        P.tt('dve', g_['d'][:], g_['mid'][:], g_['lo'][:], ALU.subtract)
                    for g_ in G_:
                        P.tt('dve', g_['d2'][:], g_['hi'][:], g_['mid'][:], ALU.subtract)
                    for g_ in G_:
                        P.stt(g_['lo'][:], g_['d'][:], g_['pred'][:], g_['lo'][:], ALU.mult, ALU.add)
                    for g_ in G_:
                        P.stt(g_['hi'][:], g_['d2'][:], g_['pred'][:], g_['mid'][:], ALU.mult, ALU.add)
                BIG = 2048.0
                for g_ in G_:
                    L, ntl, j0 = g_['L'], g_['ntl'], g_['j0']
                    maskT, posT, slotT = g_['maskT'], g_['posT'], g_['slotT']
                    P.ts('dve', maskT[:], g_['affT'][:], g_['lo'][:], None, ALU.is_ge)
                    P.op('dve', lambda e, posT=posT, maskT=maskT, L=L: e.tensor_tensor_scan(
                        posT[:], ones[:, 0:L], maskT[:], 0.0, ALU.mult, ALU.add),
                        [ones.name, maskT.name], [posT.name])
                    P.stt(slotT[:], posT[:], -1.0 - BIG, maskT[:], ALU.add, ALU.mult)
                    P.ts('dve', slotT[:], slotT[:], BIG, None, ALU.add)
                    for q in range(ntl):
                        pst = self.ps[2 + q % 2]
                        P.tr(pst[:, 0:NE], slotT[:, q * 128:(q + 1) * 128], self.ident[0:NE, 0:NE])
                        P.copy('act', slot[:, j0 + q, :], pst[:, 0:NE])
            P.barrier()
            P.copy('dve', slot16[:], slot[:])
            if 'SLOTD' in self.dbg:
                P.ld('sp', self.dscr("SLOTD", [128, NT, NE]), slot[:], 'dbg', 1)
            with ExitStack() as st2:
                oh = [self.sb(st2, f"oh{i}", [128, NLT, 512], BF16) for i in range(2)]
                ohc = [self.sb(st2, f"ohc{i}", [128, 2, 32], BF16) for i in range(2)]
                res = self.sb(st2, "tk_res", [128, NE, 5, 4])
                P.memset('dve', res[:], 0.0)
                for e_ in range(NE):
                    o = oh[e_ % 2]
                    for j in range(NLT):
                        P.ts('dve', o[:, j, :], iota[:], slot16[:, j, e_:e_ + 1], None, ALU.is_equal)
                    pst = self.ps[4 + e_ % 2]
                    for sc in range(4):
                        for j in range(NLT):
                            P.mm(pst[:, sc * 4:(sc + 1) * 4], o[:, j, sc * 128:(sc + 1) * 128], rhs[:, j, e_, :],
                                 start=(j == 0), stop=(j == NLT - 1))
                    P.copy('act', res[:, e_, 0:4, :], pst[:, 0:16].rearrange("p (a b) -> p a b", a=4))
                    if ctx_mode == 'full':
                        oc = ohc[e_ % 2]
                        for jj in range(2):
                            P.ts('dve', oc[:, jj, :], iota[:, 0:32], slot16[:, NLT + jj, e_:e_ + 1], None, ALU.is_equal)
                        pst2 = self.ps[6 + e_ % 2]
                        for jj in range(2):
                            P.mm(pst2[0:32, 0:4], oc[:, jj, :], rhs[:, NLT + jj, e_, :], start=(jj == 0), stop=(jj == 1))
                        P.copy('act', res[0:32, e_, 4, :], pst2[0:32, 0:4])
                idxf = self.sb(st2, "idxf", [128, NE, 5])
                idxi = self.sb(st2, "idxi", [128, NE, 5], I32)
                gg = self.sb(st2, "gg", [128, NE, 5])
                P.stt(idxf[:], res[:, :, :, 0], 64.0, res[:, :, :, 1], ALU.mult, ALU.add)
                P.copy('dve', idxi[:], idxf[:])
                P.tt('dve', gg[:], res[:, :, :, 2], res[:, :, :, 3], ALU.add)
                P.ld('sp', self.IDXD, idxi[:], 'tkst', 1)
                P.ld('sp', self.GD, gg[:], 'tkst2', 1)

    def moe_experts(self, l, ctx_mode):
        nc, P = self.nc, self.P
        has_ctx = ctx_mode == 'full'
        NS = 544 if has_ctx else 512
        with ExitStack() as st:
            rows = self.load_modrows(st, l, [5], has_ctx, 'g2')
            idxs = self.sb(st, "idxs", [128, NE, 5], I32)
            idxc = self.sb(st, "idxc", [128, NE], I32)
            G = self.sb(st, "Gs", [128, NE, 5])
            P.ld('sp', idxs[:], self.IDXD, 'const', 1)
            P.ld('sp', G[:], self.GD, 'const', 1)
            if has_ctx:
                P.ts('dve', idxc[:], idxs[:, :, 4], -float(S), None, ALU.add)
            xs = [self.sb(st, f"xs{i}", [128, 5, D], BF16) for i in range(2)]
            xsT = [self.sb(st, f"xsT{i}", [128, 8, 544], BF16) for i in range(2)]
            h1T = self.sb(st, "h1T", [128, 16, 544], BF16)
            sa = [self.sb(st, f"sa{i}", [128, 544], BF16) for i in range(2)]
            WG = [self.sb(st, f"WG{i}", [128, 8, 512], BF16) for i in range(2)]
            WU = [self.sb(st, f"WU{i}", [128, 8, 512], BF16) for i in range(2)]
            WD = [[self.sb(st, f"WD{i}_{f}", [128, 4, D], BF16) for f in range(4)] for i in range(2)]
            yout = [self.sb(st, f"yout{i}", [128, D]) for i in range(2)]

            def gather(e_):
                x_ = xs[e_ % 2]
                for sc in range(4):
                    P.dma('pool', lambda e, x_=x_, sc=sc, e_=e_: e.indirect_dma_start(
                        out=x_[:, sc, :], out_offset=None, in_=self.F,
                        in_offset=bass.IndirectOffsetOnAxis(ap=idxs[:, e_, sc:sc + 1], axis=0)),
                        r=[idxs.name], w=[(x_.name, sc)], stream='gath', nbuf=2)
                if has_ctx:
                    P.dma('pool', lambda e, x_=x_, e_=e_: e.indirect_dma_start(
                        out=x_[0:32, 4, :], out_offset=None, in_=self.F,
                        in_offset=bass.IndirectOffsetOnAxis(ap=idxs[0:32, e_, 4:5], axis=0)),
                        r=[idxs.name], w=[(x_.name, 4)], stream='gath', nbuf=2)

            def load_gu(e_, fg):
                n_ = e_ * 4 + fg
                P.ld('pool', WG[n_ % 2][:], self.w_gate[l, e_, :, fg * 512:(fg + 1) * 512]
                     .rearrange("(k p) n -> p k n", p=128), 'wg', 2)
                P.ld('pool', WU[n_ % 2][:], self.w_up[l, e_, :, fg * 512:(fg + 1) * 512]
                     .rearrange("(k p) n -> p k n", p=128), 'wu', 2)

            def load_d(e_, fg):
                P.ld('pool', WD[e_ % 2][fg][:], self.w_down[l, e_, fg * 512:(fg + 1) * 512, :]
                     .rearrange("(k p) n -> p k n", p=128), 'wd', 8)

            def transposes(e_):
                x_, xt = xs[e_ % 2], xsT[e_ % 2]
                for sc in range(5 if has_ctx else 4):
                    pb = self.ps[5][:].bitcast(BF16)
                    if sc < 4:
                        for c in range(8):
                            P.tr(pb[:, c * 128:(c + 1) * 128], x_[:, sc, c * 128:(c + 1) * 128], self.identb[:],
                                 r=[(x_.name, sc), self.identb.name])
                        P.copy('act', xt[:, :, sc * 128:(sc + 1) * 128], pb.rearrange("p (c t) -> p c t", c=8))
                    else:
                        for c in range(8):
                            P.tr(pb[:, c * 32:(c + 1) * 32], x_[0:32, 4, c * 128:(c + 1) * 128], self.identb[0:32, 0:32],
                                 r=[(x_.name, 4), self.identb.name])
                        P.copy('act', xt[:, :, 512:544], pb[:, 0:256].rearrange("p (c t) -> p c t", c=8))

            gather(0)
            load_gu(0, 0)
            gather(1)
            transposes(0)
            nmm = 0
            for e_ in range(NE):
                x_, xt = xs[e_ % 2], xsT[e_ % 2]
                for fg in range(4):
                    if fg < 3:
                        load_gu(e_, fg + 1)
                    elif e_ + 1 < NE:
                        load_gu(e_ + 1, 0)
                    load_d(e_, fg)
                    n_ = e_ * 4 + fg
                    wg, wu = WG[n_ % 2], WU[n_ % 2]
                    for fc in range(4):
                        f = fg * 4 + fc
                        pA, pU = self.ps[nmm % 2], self.ps[2 + nmm % 2]
                        pC = self.ps[4]
                        k2 = nmm % 2
                        nmm += 1
                        for k in range(8):
                            P.mm(pA[:, 0:512], wg[:, k, fc * 128:(fc + 1) * 128], xt[:, k, 0:512], start=(k == 0), stop=(k == 7))
                        for k in range(8):
                            P.mm(pU[:, 0:512], wu[:, k, fc * 128:(fc + 1) * 128], xt[:, k, 0:512], start=(k == 0), stop=(k == 7))
                        P.act(sa[k2][:, 0:512], pA[:, 0:512], AF.Silu)
                        P.tt('dve', h1T[:, f, 0:512], sa[k2][:, 0:512], pU[:, 0:512], ALU.mult)
                        if has_ctx:
                            for k in range(8):
                                P.mm(pC[:, 0:32], wg[:, k, fc * 128:(fc + 1) * 128], xt[:, k, 512:544], start=(k == 0), stop=(k == 7))
                            for k in range(8):
                                P.mm(pC[:, 32:64], wu[:, k, fc * 128:(fc + 1) * 128], xt[:, k, 512:544], start=(k == 0), stop=(k == 7))
                            P.act(sa[k2][:, 512:544], pC[:, 0:32], AF.Silu)
                            P.tt('dve', h1T[:, f, 512:544], sa[k2][:, 512:544], pC[:, 32:64], ALU.mult)
                if e_ + 1 < NE:
                    transposes(e_ + 1)
                if e_ + 2 < NE:
                    gather(e_ + 2)
                wd = WD[e_ % 2]
                for sc in range(5 if has_ctx else 4):
                    np_ = 128 if sc < 4 else 32
                    yo = yout[(e_ * 5 + sc) % 2]
                    s_ = 0 if sc < 4 else 1
                    for half in range(2):
                        pY = self.ps[6 + half]
                        for f in range(16):
                            P.mm(pY[0:np_, :], h1T[:, f, sc * 128:sc * 128 + np_], wd[f // 4][:, f % 4, half * 512:(half + 1) * 512],
                                 start=(f == 0), stop=(f == 15))
                        P.stt(yo[0:np_, half * 512:(half + 1) * 512], pY[0:np_, :], G[0:np_, e_, sc:sc + 1],
                              rows[(s_, 5)][0:np_, half * 512:(half + 1) * 512], ALU.mult, ALU.mult)
                    if sc < 4:
                        P.dma('pool', lambda e, yo=yo, e_=e_, sc=sc: e.indirect_dma_start(
                            out=self.out, out_offset=bass.IndirectOffsetOnAxis(ap=idxs[:, e_, sc:sc + 1], axis=0),
                            in_=yo[:], in_offset=None, compute_op=ALU.add),
                            r=[idxs.name, yo.name], w=[], stream='scat', nbuf=1)
                    else:
                        P.dma('pool', lambda e, yo=yo, e_=e_: e.indirect_dma_start(
                            out=self.hc, out_offset=bass.IndirectOffsetOnAxis(ap=idxc[0:32, e_:e_ + 1], axis=0),
                            in_=yo[0:32, :], in_offset=None, compute_op=ALU.add),
                            r=[idxc.name, yo.name], w=[], stream='scat', nbuf=1)


def bf(a):
    return np.ascontiguousarray(np.asarray(a, np.float32).astype(ml_dtypes.bfloat16))


_CONST = {}


def host_consts():
    if _CONST:
        return _CONST
    c = {}
    c['ident'] = np.eye(128, dtype=np.float32)
    nf = 16
    inv = (10000.0 ** (-np.arange(nf, dtype=np.float32) / nf)).astype(np.float32)
    t = np.arange(S)
    row = (t // 64).astype(np.float32)
    col = (t % 64).astype(np.float32)
    cos = np.ones((128, T), np.float32)
    sin = np.zeros((128, T), np.float32)
    perm = np.zeros((128, 128), np.float32)
    for p in range(128):
        d = p % 64
        pos = row if d < 32 else col
        i = d % 16
        half = (d % 32) // 16
        ang = (pos * inv[i]).astype(np.float32)
        cos[p, :S] = np.cos(ang)
        sn = np.sin(ang)
        sin[p, :S] = -sn if half == 0 else sn
        partner = p + 16 if half == 0 else p - 16
        perm[partner, p] = 1.0
    c['rope_cos'] = cos
    c['rope_sin'] = sin
    c['rope_perm'] = bf(perm)
    bo = np.zeros((128, 128), np.float32)
    bo[:64, :64] = 1.0
    bo[64:, 64:] = 1.0
    c['blockones'] = bf(bo)
    tid = (np.arange(NT)[None, :] * 128 + np.arange(128)[:, None])
    c['tokhl'] = np.ascontiguousarray(np.stack([tid // 64, tid % 64], axis=-1).astype(np.float32))
    mt = np.zeros((128, 4, 5, 128), np.float32)
    Lp = 1024
    for gi, w_ in enumerate((2, 4, 8, 16)):
        Mfull = np.zeros((Lp, Lp), np.float64)
        for t_ in range(Lp):
            lo = max(t_ - w_ // 2, 0)
            hi = min(t_ + w_ // 2, Lp)
            Mfull[t_, lo:hi] = 1.0 / (hi - lo)
            Mfull[t_, t_] -= 1.0
        def blk(ti, si):
            return Mfull[ti * 128:(ti + 1) * 128, si * 128:(si + 1) * 128].T
        mt[:, gi, 0] = blk(3, 3)
        mt[:, gi, 1] = blk(0, 0)
        mt[:, gi, 2] = blk(7, 7)
        mt[:, gi, 3] = blk(3, 2)
        mt[:, gi, 4] = blk(3, 4)
    c['poolmt'] = bf(mt)
    for L, sfx in ((S, ""), (CT, "c")):
        f32 = np.float32
        t = np.linspace(0.0, 1.0, L, dtype=f32)[:, None]
        w = (2.0 * math.pi * np.arange(L, dtype=f32)[:, None] / L).astype(f32)
        f = np.linspace(1e-4, 15, 16, dtype=f32)[None, :]
        z = np.concatenate([t, np.cos(f * w), -np.sin(f * w)], axis=-1).astype(f32)
        c['zT' + sfx] = np.ascontiguousarray(z.T)
        max_decay = math.log(1e-2) / 0.3
        min_decay = math.log(1e-2) / 1.5
        deltas = np.abs(np.linspace(min_decay, max_decay, 512, dtype=f32))
        c['decay' + sfx] = np.exp(-t * deltas[None, :]).astype(f32)
        N = 2 * L
        n = L // 128
        sidx = np.arange(L, dtype=np.int64)
        arg = ((2 * sidx[None, :] + 1) * sidx[:, None]) % (2 * N)
        ang = arg.astype(np.float64) * (math.pi / N)
        for nm, fn, sign in (("C", np.cos, 1.0), ("S", np.sin, 1.0)):
            tf = fn(ang)
            blk = tf.reshape(n, 128, n, 128)
            c['TF' + nm + sfx] = bf(blk.transpose(2, 1, 0, 3))
            ti = tf.T if nm == "C" else -tf.T
            blk = ti.reshape(n, 128, n, 128)
            c['TI' + nm + sfx] = bf(blk.transpose(2, 1, 0, 3))
    _CONST.update(c)
    return _CONST


def host_inputs(I, b):
    c = dict(host_consts())
    m = {}
    m['x'] = np.ascontiguousarray(I['x'][b])
    m['ctx'] = np.ascontiguousarray(I['ctx'][b])
    m['cT'] = np.ascontiguousarray(np.stack([I['c'][b].reshape(8, 128).T, I['c_ctx'].reshape(8, 128).T], axis=-1))
    m['ada_w'] = I['ada_w']
    m['ada_b'] = I['ada_b']
    m['norm_g'] = np.ascontiguousarray(np.stack([I['norm_mix_g'], I['norm_ffn_g']], axis=1))
    m['w_in'] = I['w_in']
    cw = np.concatenate([I['hy_conv_w'], I['hy_conv_b'][:, None, :]], axis=1)
    m['convw'] = np.ascontiguousarray(cw.reshape(2, 4, 12, 128).transpose(0, 3, 2, 1))
    qg = np.tile(I['q_norm_g'], (1, 2))
    kg = np.tile(I['k_norm_g'], (1, 2))
    m['qkg'] = np.ascontiguousarray(np.stack([qg, kg], axis=-1))
    m['router_w'] = I['router_w']
    m['w_out'] = I['w_out']
    m['pool_w'] = I['pool_w']
    m['pool_scale'] = I['pool_scale']
    for k_ in ('hy_f_w1', 'hy_f_w2', 'hy_f_w3', 'hy_f_wout', 'hy_bias'):
        m[k_] = I[k_]
    m['hy_fvec'] = np.ascontiguousarray(np.stack([I['hy_f_freq'], I['hy_f_b1'], I['hy_f_b2'], I['hy_f_b3']], axis=-1))
    m['diff_lambda'] = I['diff_lambda']
    m['subln_g'] = I['subln_g']
    m['w_gate'] = I['exp_w_gate']
    m['w_up'] = I['exp_w_up']
    m['w_down'] = I['exp_w_down']
    m.update(c)
    return m


_NC_CACHE = {}


def kernel(**inputs):
    I = {k: np.asarray(v) for k, v in inputs.items()}
    if 'nc' not in _NC_CACHE:
        b_ = Builder()
        _NC_CACHE['nc'] = b_.build()
        _NC_CACHE['names'] = set(b_.dram)
    nc = _NC_CACHE['nc']
    names = _NC_CACHE['names']
    in_maps = []
    for b in range(8):
        m = host_inputs(I, b)
        in_maps.append({k: v for k, v in m.items() if k in names})
    res = run_bass_kernel_spmd(nc, in_maps, core_ids=list(range(8)))
    return np.stack([np.asarray(res.results[b]["y"]) for b in range(8)], axis=0).astype(np.float32)
```

```python
import math
from contextlib import ExitStack
import numpy as np
import ml_dtypes
import concourse.bass as bass
import concourse.mybir as mybir
from concourse.bass_utils import run_bass_kernel_spmd

F32 = mybir.dt.float32
BF16 = mybir.dt.bfloat16
I32 = mybir.dt.int32
F16 = mybir.dt.float16
U32 = mybir.dt.uint32
ALU = mybir.AluOpType
AF = mybir.ActivationFunctionType
AX = mybir.AxisListType

D = 1024
S = 4096
CT = 256
T = S + CT
NT = T // 128
NLT = S // 128
DEPTH = 4
NE = 16
FF = 2048
EPS = 1e-6
SEM_CH = 20000


TRACKED = set()


def tokname(ap):
    n = ap.tensor.name
    return n if n in TRACKED else None


class Prog:
    ENGS = ['pe', 'act', 'dve', 'pool', 'sp']

    def __init__(self, nc):
        self.nc = nc
        self.ops = {e: [] for e in self.ENGS}
        self.last_w = {}
        self.reads = {}
        self.seq = {e: 0 for e in self.ENGS}
        self.dseq = {}
        self.latest = {}
        self.needed = set()
        self.bigflag = set()

    def _deps(self, r, w, eng=None):
        deps = {}
        def add(sig):
            if sig is None:
                return
            k, v = sig
            if deps.get(k, 0) < v:
                deps[k] = v
        for t in r:
            add(self.last_w.get(t))
            if isinstance(t, str) and t.startswith('ps'):
                for k, v in self.reads.get(t, {}).items():
                    if k != eng:
                        add((k, v))
        for t in w:
            add(self.last_w.get(t))
            for k, v in self.reads.get(t, {}).items():
                add((k, v))
        return deps

    def _update(self, r, w, sig):
        k, v = sig
        for t in r:
            d = self.reads.setdefault(t, {})
            if d.get(k, 0) < v:
                d[k] = v
        for t in w:
            self.last_w[t] = sig
            self.reads[t] = {}
        self.latest[k] = v

    def op(self, eng, fn, r=(), w=(), big=False):
        self.seq[eng] += 1
        sig = (eng, self.seq[eng])
        if big:
            self.bigflag.add(sig)
        deps = self._deps(r, w, eng)
        self._update(r, w, sig)
        for kv in deps.items():
            self.needed.add(kv)
        self.ops[eng].append(dict(fn=fn, deps=deps, sig=sig, kind='c'))

    def dma(self, q, fn, r=(), w=(), stream='d', nbuf=2):
        i = self.dseq.get(stream, 0)
        self.dseq[stream] = i + 1
        key = ('d', stream, i % nbuf)
        val = i // nbuf + 1
        deps = self._deps(r, w)
        if val > 1:
            if deps.get(key, 0) < val - 1:
                deps[key] = val - 1
        sig = (key, val)
        self._update(r, w, sig)
        for kv in deps.items():
            self.needed.add(kv)
        self.ops[q].append(dict(fn=fn, deps=deps, sig=sig, kind='d'))

    def barrier(self):
        deps = dict(self.latest)
        for kv in deps.items():
            self.needed.add(kv)
        for e in self.ENGS:
            self.ops[e].append(dict(fn=None, deps=dict(deps), sig=None, kind='b'))
        self.last_w = {}
        self.reads = {}

    def emit(self, es):
        nc = self.nc
        inc_idx = {}
        nsem = {}
        for e in self.ENGS:
            n = 0
            for o in self.ops[e]:
                if o['kind'] == 'c' and o['sig'] in self.needed:
                    n += 1
                    inc_idx[o['sig']] = n
            nsem[e] = (n + SEM_CH - 1) // SEM_CH
        sems = {}
        for e in self.ENGS:
            sems[e] = [es.enter_context(nc.semaphore(f"s_{e}_{i}")) for i in range(nsem[e])]
        dsems = {}
        for e in self.ENGS:
            for o in self.ops[e]:
                if o['kind'] == 'd':
                    k = o['sig'][0]
                    if k not in dsems:
                        dsems[k] = es.enter_context(nc.semaphore(f"d_{k[1]}_{k[2]}"))
        self.n_sems = sum(nsem.values()) + len(dsems)

        def resolve(k, v):
            if isinstance(k, tuple):
                return dsems[k], 16 * v
            n = inc_idx[(k, v)]
            return sems[k][(n - 1) // SEM_CH], (n - 1) % SEM_CH + 1

        def run(eng_name, eng):
            known = {}
            for o in self.ops[eng_name]:
                for k, v in o['deps'].items():
                    if k == 'pe' and eng_name == 'pe':
                        continue
                    if k == eng_name and (k, v) in self.bigflag:
                        continue
                    if known.get(k, 0) >= v:
                        continue
                    known[k] = v
                    s, val = resolve(k, v)
                    eng.wait_ge(s, val)
                if o['fn'] is None:
                    continue
                ins = o['fn'](eng)
                if o['kind'] == 'd':
                    s, _ = resolve(*o['sig'])
                    ins.then_inc(s, 16)
                elif o['sig'] in inc_idx:
                    n = inc_idx[o['sig']]
                    ins.then_inc(sems[eng_name][(n - 1) // SEM_CH], 1)

        with nc.Block() as block:
            @block.tensor
            def _(e):
                run('pe', e)

            @block.scalar
            def _(e):
                run('act', e)

            @block.vector
            def _(e):
                run('dve', e)

            @block.gpsimd
            def _(e):
                run('pool', e)

            @block.sync
            def _(e):
                run('sp', e)

    def _rw(self, r, w, ins, outs):
        if r is None:
            r = [tokname(a) for a in ins if a is not None and not isinstance(a, (int, float))]
        if w is None:
            w = [tokname(a) for a in outs]
        r = [t for t in r if t is not None]
        w = [t for t in w if t is not None]
        return r, w

    @staticmethod
    def _big(out, eng, accum_out=None):
        if eng != 'dve' or accum_out is not None:
            return False
        n = 1
        for d in out.shape[1:]:
            n *= int(d)
        return n >= 256

    def mm(self, out, lhsT, rhs, start=True, stop=True, r=None, w=None):
        r, w = self._rw(r, w, [lhsT, rhs], [out])
        self.op('pe', lambda e: e.matmul(out, lhsT, rhs, start=start, stop=stop), r, w)

    def tr(self, out, in_, ident, r=None, w=None):
        r, w = self._rw(r, w, [in_, ident], [out])
        self.op('pe', lambda e: e.transpose(out, in_, ident), r, w)

    def act(self, out, in_, func, bias=None, scale=None, accum_out=None, r=None, w=None):
        ins = [in_]
        if bias is not None and not isinstance(bias, (int, float)):
            ins.append(bias)
        if scale is not None and not isinstance(scale, (int, float)):
            ins.append(scale)
        outs = [out] + ([accum_out] if accum_out is not None else [])
        r, w = self._rw(r, w, ins, outs)
        kw = {}
        if bias is not None:
            kw['bias'] = bias
        if scale is not None:
            kw['scale'] = scale
        if accum_out is not None:
            kw['accum_out'] = accum_out
        self.op('act', lambda e: e.activation(out, in_, func, **kw), r, w)

    def tt(self, eng, out, in0, in1, op, r=None, w=None, bigchain=False):
        r, w = self._rw(r, w, [in0, in1], [out])
        self.op(eng, lambda e: e.tensor_tensor(out, in0, in1, op), r, w, big=(self._big(out, eng) or bigchain))

    def ts(self, eng, out, in0, s1, s2, op0, op1=None, accum_out=None, r=None, w=None):
        ins = [in0] + [s for s in (s1, s2) if s is not None and not isinstance(s, (int, float))]
        outs = [out] + ([accum_out] if accum_out is not None else [])
        r, w = self._rw(r, w, ins, outs)
        kw = {}
        if op1 is not None:
            kw['op1'] = op1
        if accum_out is not None:
            kw['accum_out'] = accum_out
        self.op(eng, lambda e: e.tensor_scalar(out, in0, s1, s2, op0, **kw), r, w, big=self._big(out, eng, accum_out))

    def stt(self, out, in0, scalar, in1, op0, op1, r=None, w=None):
        ins = [in0, in1] + ([scalar] if not isinstance(scalar, (int, float)) else [])
        r, w = self._rw(r, w, ins, [out])
        self.op('dve', lambda e: e.scalar_tensor_tensor(out, in0, scalar, in1, op0, op1), r, w, big=self._big(out, 'dve'))

    def copy(self, eng, out, in_, r=None, w=None, bigchain=False):
        r, w = self._rw(r, w, [in_], [out])
        if eng == 'act':
            self.op(eng, lambda e: e.copy(out, in_), r, w)
        else:
            self.op(eng, lambda e: e.tensor_copy(out, in_), r, w, big=(self._big(out, eng) or bigchain))

    def memset(self, eng, ap, val, w=None):
        _, w = self._rw([], w, [], [ap])
        self.op(eng, lambda e: e.memset(ap, val), [], w)

    def recip(self, out, in_, r=None, w=None):
        r, w = self._rw(r, w, [in_], [out])
        self.op('dve', lambda e: e.reciprocal(out, in_), r, w, big=self._big(out, 'dve'))

    def ld(self, q, out, in_, stream, nbuf=2, r=None, w=None, **kw):
        r, w = self._rw(r, w, [in_], [out])
        self.dma(q, lambda e: e.dma_start(out=out, in_=in_, **kw), r, w, stream, nbuf)


class Builder:
    def __init__(self, n_layers=DEPTH, dbg=()):
        self.nc = nc = bass.Bass("TRN2", target_bir_lowering=False)
        self.P = Prog(nc)
        self.n_layers = n_layers
        self.dbg = set(dbg)
        self.dram = {}
        self.es = ExitStack()

    def din(self, name, shape, dt=F32):
        t = self.nc.dram_tensor(name, list(shape), dt, kind="ExternalInput")
        self.dram[name] = t
        return t.ap()

    def dout(self, name, shape, dt=F32):
        t = self.nc.dram_tensor(name, list(shape), dt, kind="ExternalOutput")
        self.dram[name] = t
        return t.ap()

    def dscr(self, name, shape, dt=F32):
        kind = "ExternalOutput" if name in self.dbg else "Internal"
        t = self.nc.dram_tensor(name, list(shape), dt, kind=kind)
        self.dram[name] = t
        return t.ap()

    def sb(self, stack, name, shape, dt=F32):
        self._uid = getattr(self, '_uid', 0) + 1
        name = f"{name}_u{self._uid}"
        TRACKED.add(name)
        return stack.enter_context(self.nc.sbuf_tensor(name, list(shape), dt))

    def build(self):
        nc, P = self.nc, self.P
        self.x = self.din("x", [S, D])
        self.ctx = self.din("ctx", [CT, D])
        self.cT = self.din("cT", [128, 8, 2])
        self.ada_w = self.din("ada_w", [DEPTH, D, 6 * D])
        self.ada_b = self.din("ada_b", [DEPTH, 6 * D])
        self.norm_g = self.din("norm_g", [DEPTH, 2, D])
        self.ident_in = self.din("ident", [128, 128])
        self.w_in = self.din("w_in", [2, D, 3072])
        self.convw = self.din("convw", [2, 128, 12, 4])
        self.qkg = self.din("qkg", [2, 128, 2])
        self.rope_cos = self.din("rope_cos", [128, T])
        self.rope_sin = self.din("rope_sin", [128, T])
        self.rope_perm = self.din("rope_perm", [128, 128], BF16)
        self.blockones_in = self.din("blockones", [128, 128], BF16)
        self.router_w = self.din("router_w", [DEPTH, D, NE])
        self.tokhl_in = self.din("tokhl", [128, NT, 2])
        self.w_gate = self.din("w_gate", [DEPTH, NE, D, FF])
        self.w_up = self.din("w_up", [DEPTH, NE, D, FF])
        self.w_down = self.din("w_down", [DEPTH, NE, FF, D])
        self.diff_lambda = self.din("diff_lambda", [2, 4, 64])
        self.subln_g = self.din("subln_g", [2, 128])
        self.hy_f_w1 = self.din("hy_f_w1", [2, 33, 64])
        self.hy_f_w2 = self.din("hy_f_w2", [2, 64, 64])
        self.hy_f_w3 = self.din("hy_f_w3", [2, 64, 64])
        self.hy_f_wout = self.din("hy_f_wout", [2, 64, 2048])
        self.hy_fvec = self.din("hy_fvec", [2, 64, 4])
        self.hy_bias = self.din("hy_bias", [2, 2, 512])
        self.zT = self.din("zT", [33, S])
        self.zTc = self.din("zTc", [33, CT])
        self.decay = self.din("decay", [S, 512])
        self.decayc = self.din("decayc", [CT, 512])
        for nm in ("TFC", "TFS", "TIC", "TIS"):
            setattr(self, nm, self.din(nm, [S // 128, 128, S // 128, 128], BF16))
            setattr(self, nm + "c", self.din(nm + "c", [CT // 128, 128, CT // 128, 128], BF16))
        self.w_out = self.din("w_out", [2, D, D])
        self.pool_w = self.din("pool_w", [2, 4, 256, 256])
        self.pool_scale = self.din("pool_scale", [2, D])
        self.poolmt = self.din("poolmt", [128, 4, 5, 128], BF16)
        self.out = self.dout("y", [S, D])
        self.hc = self.dscr("hc", [CT, D])
        self.MOD = self.dscr("MOD", [2, DEPTH * 6 * D])

        top = self.es
        self.ps = [top.enter_context(nc.psum_tensor(f"ps{i}", [128, 512], F32)) for i in range(8)]
        for i in range(8):
            TRACKED.add(f"ps{i}")
        self.ident = self.sb(top, "identf", [128, 128], F32)
        self.identb = self.sb(top, "identb", [128, 128], BF16)
        P.ld('sp', self.ident[:], self.ident_in, 'const', 1)
        P.copy('dve', self.identb[:], self.ident[:])
        self.epsc = self.sb(top, "epsc", [128, 1], F32)
        P.memset('dve', self.epsc[:], EPS)

        self.prologue()
        P.barrier()
        for l in range(self.n_layers):
            self.layer(l)
        P.barrier()
        P.emit(self.es)
        self.es.close()
        return nc

    def prologue(self):
        nc, P = self.nc, self.P
        with ExitStack() as st:
            cT = self.sb(st, "cTs", [128, 8, 2])
            sT = self.sb(st, "sTs", [128, 8, 2])
            wt = [self.sb(st, f"adaw{i}", [128, 8, 512]) for i in range(2)]
            bias = [self.sb(st, f"adabs{i}", [2, 6 * D]) for i in range(2)]
            gsb = [self.sb(st, f"gsb{i}", [2, 2 * D]) for i in range(2)]
            mod = [self.sb(st, f"modsb{i}", [2, 6 * D]) for i in range(2)]
            P.ld('sp', cT[:], self.cT, 'const', 1)
            P.act(sT[:], cT[:], AF.Silu)
            n = 0
            for l in range(DEPTH):
                bi, gs, mo = bias[l % 2], gsb[l % 2], mod[l % 2]
                P.ld('sp', bi[:], self.ada_b[l:l + 1, :].partition_broadcast(2), 'pro_b', 2)
                P.ld('sp', gs[:], self.norm_g[l:l + 1].rearrange("o k n -> o (k n)").partition_broadcast(2), 'pro_g', 2)
                for cc in range(12):
                    w = wt[n % 2]
                    P.ld('sp', w[:], self.ada_w[l, :, cc * 512:(cc + 1) * 512].rearrange("(j p) n -> p j n", p=128),
                         'adaw', 2)
                    pst = self.ps[n % 2]
                    for j in range(8):
                        P.mm(pst[0:2, :], sT[:, j, :], w[:, j, :], start=(j == 0), stop=(j == 7))
                    c0 = cc * 512
                    P.tt('dve', mo[:, c0:c0 + 512], pst[0:2, :], bi[:, c0:c0 + 512], ALU.add)
                    n += 1
                for v, k in ((1, 0), (4, 1)):
                    P.stt(mo[:, v * D:(v + 1) * D], mo[:, v * D:(v + 1) * D], 1.0, gs[:, k * D:(k + 1) * D],
                          ALU.add, ALU.mult)
                P.ld('sp', self.MOD[:, l * 6 * D:(l + 1) * 6 * D], mo[:], 'pro_st', 2)

    def modrow(self, l, s, v):
        c0 = l * 6 * D + v * D
        return self.MOD[s:s + 1, c0:c0 + D].partition_broadcast(128)

    def layer(self, l):
        P = self.P
        ctx_mode = ['full', 'full', 'kv', 'none'][l]
        if 'moe_only' in self.dbg:
            self.mixer_passthrough(l, ctx_mode)
        elif l % 2 == 0:
            self.even_proj(l)
            P.barrier()
            if 'no_hyena' not in self.dbg:
                self.hyena_filters(l, S)
                P.barrier()
                self.hyena_conv(l, S, 0)
                P.barrier()
                if ctx_mode == 'full':
                    self.hyena_filters(l, CT)
                    P.barrier()
                    self.hyena_conv(l, CT, S)
                    P.barrier()
            if 'no_attn' not in self.dbg:
                self.even_attn(l, ctx_mode)
            P.barrier()
            self.even_wout(l, ctx_mode)
        else:
            self.odd_pool(l, ctx_mode)
        P.barrier()
        if 'no_moe' in self.dbg:
            return
        self.moe_topk(l, ctx_mode)
        P.barrier()
        self.moe_experts(l, ctx_mode)
        P.barrier()

    def hsrc(self, l, j):
        if j < NLT:
            t = self.x if l == 0 else self.out
            return t[j * 128:(j + 1) * 128, :]
        t = self.ctx if l == 0 else self.hc
        return t[(j - NLT) * 128:(j - NLT + 1) * 128, :]

    def hdst(self, j):
        if j < NLT:
            return self.out[j * 128:(j + 1) * 128, :]
        return self.hc[(j - NLT) * 128:(j - NLT + 1) * 128, :]

    def norm_mod(self, st_bufs, h, a_out, G, Bv, n):
        P = self.P
        junk, ss, sd, rstd, tmp = st_bufs
        k = n % 2
        P.act(junk[k][:], h, AF.Square, accum_out=ss[k][:])
        P.act(sd[k][:], ss[k][:], AF.Sqrt, scale=1.0 / D, bias=self.epsc[:])
        P.recip(rstd[k][:], sd[k][:])
        P.stt(tmp[k][:], h, rstd[k][:], G, ALU.mult, ALU.mult)
        P.tt('dve', a_out, tmp[k][:], Bv, ALU.add)

    def alloc_norm_bufs(self, st):
        junk = [self.sb(st, f"nm_junk{i}", [128, D], BF16) for i in range(2)]
        ss = [self.sb(st, f"nm_ss{i}", [128, 1]) for i in range(2)]
        sd = [self.sb(st, f"nm_sd{i}", [128, 1]) for i in range(2)]
        rstd = [self.sb(st, f"nm_rstd{i}", [128, 1]) for i in range(2)]
        tmp = [self.sb(st, f"nm_tmp{i}", [128, D]) for i in range(2)]
        return junk, ss, sd, rstd, tmp

    def load_modrows(self, st, l, vs, with_ctx, tag):
        P = self.P
        rows = {}
        for s_ in ([0, 1] if with_ctx else [0]):
            for v in vs:
                t = self.sb(st, f"mr_{tag}_{s_}_{v}", [128, D])
                P.ld('sp', t[:], self.modrow(l, s_, v), 'modrow', 1)
                rows[(s_, v)] = t
        return rows

    def even_proj(self, l):
        nc, P = self.nc, self.P
        e = l // 2
        ctx_mode = ['full', 'full', 'kv', 'none'][l]
        ntiles = NT if ctx_mode != 'none' else NLT
        with ExitStack() as st:
            AT = self.sb(st, "AT", [128, 8, T], BF16)
            Win = self.sb(st, "Win", [128, 8, 3072], BF16)
            for g in range(6):
                for j in range(8):
                    P.ld('pool', Win[:, j, g * 512:(g + 1) * 512],
                         self.w_in[e, j * 128:(j + 1) * 128, g * 512:(g + 1) * 512], 'winld', 2,
                         w=[('Win', g)])
            with ExitStack() as st2:
                rows = self.load_modrows(st2, l, [0, 1], ctx_mode != 'none', 'mix')
                nb = self.alloc_norm_bufs(st2)
                hb = [self.sb(st2, f"hb{i}", [128, D]) for i in range(2)]
                ab = [self.sb(st2, f"ab{i}", [128, D], BF16) for i in range(2)]
                for j in range(ntiles):
                    s_ = 0 if j < NLT else 1
                    k = j % 2
                    P.ld('sp', hb[k][:], self.hsrc(l, j), 'hld', 2)
                    self.norm_mod(nb, hb[k][:], ab[k][:], rows[(s_, 1)][:], rows[(s_, 0)][:], j)
                    pst = self.ps[j % 2]
                    pb = pst[:].bitcast(BF16)
                    for c in range(8):
                        P.tr(pb[:, c * 128:(c + 1) * 128], ab[k][:, c * 128:(c + 1) * 128], self.identb[:])
                    src = pb.rearrange("p (c t) -> p c t", c=8)
                    if j % 2 == 0:
                        P.copy('act', AT[:, :, j * 128:(j + 1) * 128], src, w=[('AT', j)])
                    else:
                        P.copy('dve', AT[:, :, j * 128:(j + 1) * 128], src, w=[('AT', j)])
                if 'ATd' in self.dbg:
                    P.ld('sp', self.dscr("ATd", [128, 8, T], BF16), AT[:], 'dbg', 1, r=[('AT', j) for j in range(ntiles)])
            P.barrier()
            self.proj_hyena(l, st, AT, Win, ctx_mode)
            P.barrier()

    def proj_hyena(self, l, st, AT, Win, ctx_mode):
        nc, P = self.nc, self.P
        e = l // 2
        segs = [(0, S)] + ([(S, CT)] if ctx_mode == 'full' else [])
        X2T = self.dscr(f"X2T", [512, T]) if not hasattr(self, 'X2T') else self.X2T
        self.X2T = X2T
        if not hasattr(self, 'X1'):
            self.X1 = self.dscr("X1", [T, 512])
            self.X2 = self.dscr("X2", [T, 512])
            self.VH = self.dscr("VH", [T, 512], BF16)
            self.QT = self.dscr("QT", [4, 128, T], BF16)
            self.KT = self.dscr("KT", [4, 128, T], BF16)
            self.VA = self.dscr("VA", [T, 512], BF16)
        with ExitStack() as st2:
            cw = self.sb(st2, "convw", [128, 12, 4])
            P.ld('sp', cw[:], self.convw[e], 'const', 1)
            PT = [self.sb(st2, f"PT{i}", [128, S + 2]) for i in range(1)]
            UC = [self.sb(st2, f"UC{i}", [128, S]) for i in range(2)]
            UCb = self.sb(st2, "UCb", [128, S], BF16)
            stg = [self.sb(st2, f"stg{i}", [128, 32, 128]) for i in range(1)]
            stgb = [self.sb(st2, f"stgb{i}", [128, 32, 128], BF16) for i in range(1)]
            n = 0
            npsum = 0
            for cc in range(12):
                for (t0, L) in segs:
                    k = n % 2
                    pt, uc = PT[0], UC[k]
                    P.memset('dve', pt[:, 0:1], 0.0)
                    P.memset('dve', pt[:, L + 1:L + 2], 0.0)
                    nb = (L + 511) // 512
                    for tb in range(nb):
                        w_ = min(512, L - tb * 512)
                        pst = self.ps[npsum % 4]
                        npsum += 1
                        c0 = t0 + tb * 512
                        for kk in range(8):
                            P.mm(pst[:, 0:w_], Win[:, kk, cc * 128:(cc + 1) * 128], AT[:, kk, c0:c0 + w_],
                                 start=(kk == 0), stop=(kk == 7),
                                 r=[('Win', cc // 4)] + [('AT', c0 // 128 + i) for i in range(w_ // 128)])
                        if tb % 2 == 0:
                            P.copy('act', pt[:, 1 + tb * 512:1 + tb * 512 + w_], pst[:, 0:w_])
                        else:
                            P.copy('dve', pt[:, 1 + tb * 512:1 + tb * 512 + w_], pst[:, 0:w_])
                    P.ts('dve', uc[:, 0:L], pt[:, 1:L + 1], cw[:, cc, 1:2], cw[:, cc, 3:4], ALU.mult, ALU.add)
                    P.stt(uc[:, 0:L], pt[:, 0:L], cw[:, cc, 0:1], uc[:, 0:L], ALU.mult, ALU.add)
                    P.stt(uc[:, 0:L], pt[:, 2:L + 2], cw[:, cc, 2:3], uc[:, 0:L], ALU.mult, ALU.add)
                    grp = cc // 4
                    ci = cc % 4
                    if grp in (0, 1):
                        Xd = self.X1 if grp == 0 else self.X2
                        sg = stg[0]
                        nt_ = L // 128
                        for q4 in range((nt_ + 3) // 4):
                            pst = self.ps[4 + (npsum % 4)]
                            npsum += 1
                            m4 = min(4, nt_ - q4 * 4)
                            for i in range(m4):
                                tt_ = q4 * 4 + i
                                P.tr(pst[:, i * 128:(i + 1) * 128], uc[:, tt_ * 128:(tt_ + 1) * 128], self.ident[:])
                            P.copy('act', sg[:, q4 * 4:q4 * 4 + m4, :], pst[:, 0:m4 * 128].rearrange("p (a c) -> p a c", a=m4))
                        for a0 in range(0, nt_, 4):
                            a1 = min(nt_, a0 + 4)
                            P.ld('pool', Xd[t0 + a0 * 128:t0 + a1 * 128, ci * 128:(ci + 1) * 128]
                                 .rearrange("(a p) c -> p a c", p=128), sg[:, a0:a1, :], 'x1st', 2)
                    else:
                        sg = stgb[0]
                        nt_ = L // 128
                        P.copy('act', UCb[:, 0:L], uc[:, 0:L])
                        for q8 in range((nt_ + 7) // 8):
                            pst = self.ps[4 + (npsum % 4)]
                            npsum += 1
                            pb = pst[:].bitcast(BF16)
                            m_ = min(8, nt_ - q8 * 8)
                            for i in range(m_):
                                tt_ = q8 * 8 + i
                                P.tr(pb[:, i * 128:(i + 1) * 128], UCb[:, tt_ * 128:(tt_ + 1) * 128], self.identb[:])
                            P.copy('act', sg[:, q8 * 8:q8 * 8 + m_, :],
                                   pb[:, 0:m_ * 128].rearrange("p (a c) -> p a c", a=m_))
                        for a0 in range(0, nt_, 4):
                            a1 = min(nt_, a0 + 4)
                            P.ld('pool', self.VH[t0 + a0 * 128:t0 + a1 * 128, ci * 128:(ci + 1) * 128]
                                 .rearrange("(a p) c -> p a c", p=128), sg[:, a0:a1, :], 'vhst', 2)
                    n += 1
        P.barrier()
        if 'skip_qk' not in self.dbg:
            self.proj_qk(l, AT, Win, ctx_mode)

    def proj_qk(self, l, AT, Win, ctx_mode):
        nc, P = self.nc, self.P
        e = l // 2
        with ExitStack() as st2:
            cosT = self.sb(st2, "cosT", [128, T])
            sinT = self.sb(st2, "sinT", [128, T])
            P.ld('sp', cosT[:], self.rope_cos, 'const', 1)
            P.ld('sp', sinT[:], self.rope_sin, 'const', 1)
            Rm = self.sb(st2, "Rm", [128, 128], BF16)
            bo = self.sb(st2, "blockones", [128, 128], BF16)
            P.ld('sp', Rm[:], self.rope_perm, 'const', 1)
            P.ld('sp', bo[:], self.blockones_in, 'const', 1)
            qkg = self.sb(st2, "qkg", [128, 2])
            P.ld('sp', qkg[:], self.qkg[e], 'const', 1)
            q32 = [self.sb(st2, f"q32_{i}", [128, 512]) for i in range(2)]
            sq = [self.sb(st2, f"sq_{i}", [128, 512], BF16) for i in range(2)]
            sd = [self.sb(st2, f"qsd_{i}", [128, 512]) for i in range(2)]
            rs = [self.sb(st2, f"qrs_{i}", [128, 512]) for i in range(2)]
            qn = [self.sb(st2, f"qn_{i}", [128, 512]) for i in range(2)]
            qnb = [self.sb(st2, f"qnb_{i}", [128, 512], BF16) for i in range(2)]
            t1 = [self.sb(st2, f"qt1_{i}", [128, 512]) for i in range(2)]
            t2 = [self.sb(st2, f"qt2_{i}", [128, 512]) for i in range(2)]
            qo = [self.sb(st2, f"qo_{i}", [128, T], BF16) for i in range(2)]
            n = 0
            for which in ((0, 1) if 'skip_qkloop' not in self.dbg else ()):
                if which == 0:
                    segs = [(0, S)] + ([(S, CT)] if ctx_mode == 'full' else [])
                else:
                    segs = [(0, S)] + ([(S, CT)] if ctx_mode in ('full', 'kv') else [])
                dst = self.QT if which == 0 else self.KT
                for h in range(4):
                    col0 = 1536 + which * 512 + h * 128
                    qout = qo[(which * 4 + h) % 2]
                    tend = 0
                    for (t0, L) in segs:
                        for tb in range((L + 511) // 512):
                            w_ = min(512, L - tb * 512)
                            c0 = t0 + tb * 512
                            k = n % 2
                            pA, pB, pC = self.ps[(n % 2) * 3], self.ps[(n % 2) * 3 + 1], self.ps[(n % 2) * 3 + 2]
                            for kk in range(8):
                                P.mm(pA[:, 0:w_], Win[:, kk, col0:col0 + 128], AT[:, kk, c0:c0 + w_],
                                     start=(kk == 0), stop=(kk == 7),
                                     r=[('Win', col0 // 512)] + [('AT', c0 // 128 + i) for i in range(w_ // 128)])
                            import os
                            lim = int(os.environ.get('QKSTEP', 99))
                            if lim >= 2: P.act(sq[k][:, 0:w_], pA[:, 0:w_], AF.Square)
                            if lim >= 3: P.copy('dve', q32[k][:, 0:w_], pA[:, 0:w_])
                            if lim >= 4: P.mm(pB[:, 0:w_], bo[:], sq[k][:, 0:w_])
                            if lim >= 5: P.act(sd[k][:, 0:w_], pB[:, 0:w_], AF.Sqrt, scale=1.0 / 64, bias=self.epsc[:])
                            if lim >= 6: P.recip(rs[k][:, 0:w_], sd[k][:, 0:w_])
                            if lim >= 7: P.stt(qn[k][:, 0:w_], q32[k][:, 0:w_], qkg[:, which:which + 1], rs[k][:, 0:w_],
                                  ALU.mult, ALU.mult)
                            if lim >= 8: P.copy('act', qnb[k][:, 0:w_], qn[k][:, 0:w_])
                            if lim >= 9: P.mm(pC[:, 0:w_], Rm[:], qnb[k][:, 0:w_])
                            if lim >= 10: P.tt('dve', t1[k][:, 0:w_], qn[k][:, 0:w_], cosT[:, c0:c0 + w_], ALU.mult)
                            if lim >= 11: P.tt('dve', t2[k][:, 0:w_], pC[:, 0:w_], sinT[:, c0:c0 + w_], ALU.mult)
                            if lim >= 12: P.tt('dve', qout[:, c0:c0 + w_], t1[k][:, 0:w_], t2[k][:, 0:w_], ALU.add)
                            n += 1
                        tend = t0 + L
                    if lim >= 13: P.ld('pool', dst[h, :, 0:tend], qout[:, 0:tend], 'qst', 2)
        P.barrier()
        if 'skip_va' in self.dbg:
            return
        with ExitStack() as st2:
            vst = [self.sb(st2, f"vst{i}", [128, 512], BF16) for i in range(2)]
            ntiles = NT if ctx_mode in ('full', 'kv') else NLT
            for j in range(ntiles):
                pst = self.ps[j % 2]
                for kk in range(8):
                    P.mm(pst[:], AT[:, kk, j * 128:(j + 1) * 128], Win[:, kk, 2560:3072],
                         start=(kk == 0), stop=(kk == 7), r=[('Win', 5), ('AT', j)])
                if j % 2 == 0:
                    P.copy('act', vst[j % 2][:], pst[:])
                else:
                    P.copy('dve', vst[j % 2][:], pst[:])
                P.ld('pool', self.VA[j * 128:(j + 1) * 128, :], vst[j % 2][:], 'vast', 2)


    def sin_block(self, B_, out, ps, A, Bc, w_):
        P = self.P
        y, yi, yf, m1, m2 = B_
        P.ts('dve', y[:, 0:w_], ps, A, Bc, ALU.mult, ALU.add)
        P.copy('dve', yi[:, 0:w_], y[:, 0:w_])
        P.copy('dve', yf[:, 0:w_], yi[:, 0:w_])
        P.tt('dve', y[:, 0:w_], y[:, 0:w_], yf[:, 0:w_], ALU.subtract)
        P.ts('dve', m1[:, 0:w_], y[:, 0:w_], 0.5, None, ALU.is_gt)
        P.ts('dve', m2[:, 0:w_], y[:, 0:w_], -0.5, None, ALU.is_lt)
        P.tt('dve', y[:, 0:w_], y[:, 0:w_], m1[:, 0:w_], ALU.subtract)
        P.tt('dve', y[:, 0:w_], y[:, 0:w_], m2[:, 0:w_], ALU.add)
        P.act(out, y[:, 0:w_], AF.Sin, scale=2.0 * math.pi * (1.0 - 1e-6))

    def dft_forward(self, st, L, tabs, srcA, srcB, evac):
        P = self.P
        TC, TS, TFC, TFS = tabs
        nch = L // 128
        for fc in range(nch):
            tc_, ts_ = TC[fc % 2], TS[fc % 2]
            P.ld('sp', tc_[:, 0:nch, :], TFC[fc], 'tabc', 2)
            P.ld('sp', ts_[:, 0:nch, :], TFS[fc], 'tabs', 2)
            pA, pB = self.ps[(fc % 2) * 2], self.ps[(fc % 2) * 2 + 1]
            for sc in range(nch):
                P.mm(pA[:], tc_[:, sc, :], srcA[:, sc, :], start=(sc == 0), stop=(sc == nch - 1))
            for sc in range(nch):
                P.mm(pB[:], ts_[:, sc, :], srcB[:, sc, :], start=(sc == 0), stop=(sc == nch - 1))
            evac(fc, pA, pB)

    def hyena_tables(self, L):
        sfx = "" if L == S else "c"
        return (getattr(self, "TFC" + sfx), getattr(self, "TFS" + sfx), getattr(self, "TIC" + sfx), getattr(self, "TIS" + sfx))

    def hyena_filters(self, l, L):
        P = self.P
        e = l // 2
        sfx = "" if L == S else "c"
        nch = L // 128
        if not hasattr(self, 'KR' + sfx):
            setattr(self, 'KR' + sfx, self.dscr('KR' + sfx, [2, L, 512]))
            setattr(self, 'KI' + sfx, self.dscr('KI' + sfx, [2, L, 512]))
        KR, KI = getattr(self, 'KR' + sfx), getattr(self, 'KI' + sfx)
        zTd = self.zT if L == S else self.zTc
        decd = self.decay if L == S else self.decayc
        TFC, TFS, _, _ = self.hyena_tables(L)
        with ExitStack() as st:
            zT = self.sb(st, "zTs", [33, L])
            P.ld('sp', zT[:], zTd, 'const', 1)
            w1 = self.sb(st, "fw1", [33, 64])
            w2 = self.sb(st, "fw2", [64, 64])
            w3 = self.sb(st, "fw3", [64, 64])
            wo = self.sb(st, "fwo", [64, 2048])
            P.ld('sp', w1[:], self.hy_f_w1[e], 'const', 1)
            P.ld('sp', w2[:], self.hy_f_w2[e], 'const', 1)
            P.ld('sp', w3[:], self.hy_f_w3[e], 'const', 1)
            P.ld('sp', wo[:], self.hy_f_wout[e], 'const', 1)
            fv = self.sb(st, "fvec", [64, 4])
            P.ld('sp', fv[:], self.hy_fvec[e], 'const', 1)
            A = self.sb(st, "fA", [64, 1])
            Bc = self.sb(st, "fBc", [64, 3])
            P.ts('dve', A[:], fv[:, 0:1], 1.0 / (2.0 * math.pi), None, ALU.mult)
            for i in range(3):
                P.tt('dve', Bc[:, i:i + 1], fv[:, i + 1:i + 2], A[:], ALU.mult)
            B_ = (self.sb(st, "sy", [64, 512]), self.sb(st, "syi", [64, 512], I32), self.sb(st, "syf", [64, 512]),
                  self.sb(st, "sm1", [64, 512]), self.sb(st, "sm2", [64, 512]))
            h1 = self.sb(st, "fh1", [64, 512])
            h2 = self.sb(st, "fh2", [64, 512])
            h3T = self.sb(st, "fh3T", [64, L])
            for cb in range((L + 511) // 512):
                w_ = min(512, L - cb * 512)
                c0 = cb * 512
                ps = self.ps[cb % 2]
                P.mm(ps[0:64, 0:w_], w1[:], zT[:, c0:c0 + w_])
                self.sin_block(B_, h1[:, 0:w_], ps[0:64, 0:w_], A[:], Bc[:, 0:1], w_)
                ps2 = self.ps[2 + cb % 2]
                P.mm(ps2[0:64, 0:w_], w2[:], h1[:, 0:w_])
                self.sin_block(B_, h2[:, 0:w_], ps2[0:64, 0:w_], A[:], Bc[:, 1:2], w_)
                ps3 = self.ps[4 + cb % 2]
                P.mm(ps3[0:64, 0:w_], w3[:], h2[:, 0:w_])
                self.sin_block(B_, h3T[:, c0:c0 + w_], ps3[0:64, 0:w_], A[:], Bc[:, 2:3], w_)
            if 'H3T' in self.dbg and L == S:
                P.ld('sp', self.dscr("H3T", [64, L]), h3T[:], 'dbg', 1)
            KS = self.sb(st, "KS", [128, nch, 512], BF16)
            KD = self.sb(st, "KD", [128, nch, 512], BF16)
            TC = [self.sb(st, f"TCk{i}", [128, nch, 128], BF16) for i in range(2)]
            TS = [self.sb(st, f"TSk{i}", [128, nch, 128], BF16) for i in range(2)]
            dec = [self.sb(st, f"dec{i}", [128, 512]) for i in range(2)]
            hf = [self.sb(st, f"hf{i}", [128, 512]) for i in range(2)]
            hb = [self.sb(st, f"hbk{i}", [128, 512]) for i in range(2)]
            brow = self.sb(st, "hybias", [1, 2, 512])
            P.ld('sp', brow[:], self.hy_bias[e:e + 1], 'const', 1)
            kr = [self.sb(st, f"kr{i}", [128, 512]) for i in range(2)]
            ki = [self.sb(st, f"ki{i}", [128, 512]) for i in range(2)]
            sc2 = 2.0 / (2 * L)
            for o in range(2):
                for tt_ in range(nch):
                    k = tt_ % 2
                    P.ld('sp', dec[k][:], decd[tt_ * 128:(tt_ + 1) * 128, :], 'decld', 2)
                    pf, pb_ = self.ps[4 + k * 2], self.ps[5 + k * 2]
                    P.mm(pf[:], h3T[:, tt_ * 128:(tt_ + 1) * 128], wo[:, (o * 2) * 512:(o * 2 + 1) * 512])
                    P.mm(pb_[:], h3T[:, tt_ * 128:(tt_ + 1) * 128], wo[:, (o * 2 + 1) * 512:(o * 2 + 2) * 512])
                    P.tt('dve', hf[k][:], pf[:], dec[k][:], ALU.mult)
                    P.tt('dve', hb[k][:], pb_[:], dec[k][:], ALU.mult)
                    if tt_ == 0:
                        P.tt('dve', hf[k][0:1, :], hf[k][0:1, :], brow[0:1, o, :], ALU.add)
                    P.tt('dve', KS[:, tt_, :], hf[k][:], hb[k][:], ALU.add)
                    P.tt('dve', KD[:, tt_, :], hb[k][:], hf[k][:], ALU.subtract)
                if 'KSd' in self.dbg and L == S and o == 0:
                    P.ld('sp', self.dscr("KSd", [128, nch, 512], BF16), KS[:], 'dbg', 1)

                def evac(fc, pA, pB, o=o):
                    k = fc % 2
                    P.ts('dve', kr[k][:], pA[:], sc2, None, ALU.mult)
                    P.act(ki[k][:], pB[:], AF.Copy, scale=sc2)
                    P.ld('pool', KR[o, fc * 128:(fc + 1) * 128, :], kr[k][:], 'krst', 2)
                    P.ld('pool', KI[o, fc * 128:(fc + 1) * 128, :], ki[k][:], 'kist', 2)
                self.dft_forward(st, L, (TC, TS, TFC, TFS), KS, KD, evac)

    def hyena_conv(self, l, L, t0):
        P = self.P
        sfx = "" if L == S else "c"
        nch = L // 128
        KR, KI = getattr(self, 'KR' + sfx), getattr(self, 'KI' + sfx)
        TFC, TFS, TIC, TIS = self.hyena_tables(L)
        if not hasattr(self, 'MIXT'):
            self.MIXT = self.dscr("MIXT", [D, T], BF16)
        with ExitStack() as st:
            U = [self.sb(st, f"U{i}", [128, nch, 512], BF16) for i in range(2)]
            Yr = self.sb(st, "Yr", [128, nch, 512], BF16)
            Yi = self.sb(st, "Yi", [128, nch, 512], BF16)
            TC = [self.sb(st, f"TCc{i}", [128, nch, 128], BF16) for i in range(2)]
            TS = [self.sb(st, f"TSc{i}", [128, nch, 128], BF16) for i in range(2)]
            kr = [self.sb(st, f"ckr{i}", [128, 512]) for i in range(2)]
            ki = [self.sb(st, f"cki{i}", [128, 512]) for i in range(2)]
            tmp = [self.sb(st, f"ctmp{i}", [128, 512]) for i in range(4)]
            gt = [self.sb(st, f"cgt{i}", [128, 512]) for i in range(2)]
            zb = [self.sb(st, f"czb{i}", [128, 512], BF16) for i in range(2)]
            zT = [self.sb(st, f"czT{i}", [128, 4, 128], BF16) for i in range(2)]
            for a0 in range(0, nch, 8):
                a1 = min(nch, a0 + 8)
                P.ld('sp', U[0][:, a0:a1, :], self.VH[t0 + a0 * 128:t0 + a1 * 128, :].rearrange("(a p) c -> p a c", p=128),
                     'uld', 2, w=[(U[0].name, a0 // 8)])
            for o in range(2):
                Uin, Uout = U[o % 2], U[(o + 1) % 2]
                gate_d = self.X1 if o == 0 else self.X2

                def evac(fc, pA, pB, o=o):
                    k = fc % 2
                    P.ld('sp', kr[k][:], KR[o, fc * 128:(fc + 1) * 128, :], 'krld', 2)
                    P.ld('sp', ki[k][:], KI[o, fc * 128:(fc + 1) * 128, :], 'kild', 2)
                    P.tt('dve', tmp[0][:], pA[:], kr[k][:], ALU.mult)
                    P.tt('dve', tmp[1][:], pB[:], ki[k][:], ALU.mult)
                    P.tt('pool', Yr[:, fc, :], tmp[0][:], tmp[1][:], ALU.add, w=[(Yr.name, fc)])
                    P.tt('dve', tmp[2][:], pA[:], ki[k][:], ALU.mult)
                    P.tt('dve', tmp[3][:], pB[:], kr[k][:], ALU.mult)
                    P.tt('pool', Yi[:, fc, :], tmp[2][:], tmp[3][:], ALU.subtract, w=[(Yi.name, fc)])
                if o == 0:
                    rU = [(Uin.name, a) for a in range((nch + 7) // 8)]
                else:
                    rU = [(Uin.name, 'z', a) for a in range(nch)]
                self._dft_fwd_tok(L, (TC, TS, TFC, TFS), Uin, rU, evac)
                for tc in range(nch):
                    tc_, ts_ = TC[tc % 2], TS[tc % 2]
                    P.ld('sp', tc_[:, 0:nch, :], TIC[tc], 'tabc', 2)
                    P.ld('sp', ts_[:, 0:nch, :], TIS[tc], 'tabs', 2)
                    pY = self.ps[4 + tc % 2]
                    for fk in range(nch):
                        P.mm(pY[:], tc_[:, fk, :], Yr[:, fk, :], start=(fk == 0), stop=False, r=[tc_.name, (Yr.name, fk)])
                    for fk in range(nch):
                        P.mm(pY[:], ts_[:, fk, :], Yi[:, fk, :], start=False, stop=(fk == nch - 1), r=[ts_.name, (Yi.name, fk)])
                    g_ = gt[tc % 2]
                    P.ld('sp', g_[:], gate_d[t0 + tc * 128:t0 + (tc + 1) * 128, :], 'gld', 2)
                    if o == 0:
                        P.tt('dve', Uout[:, tc, :], pY[:], g_[:], ALU.mult, w=[(Uout.name, 'z', tc)])
                    else:
                        z_ = zb[tc % 2]
                        P.tt('dve', z_[:], pY[:], g_[:], ALU.mult)
                        pb = self.ps[6 + tc % 2][:].bitcast(BF16)
                        for c in range(4):
                            P.tr(pb[:, c * 128:(c + 1) * 128], z_[:, c * 128:(c + 1) * 128], self.identb[:])
                        zt_ = zT[tc % 2]
                        P.copy('act', zt_[:], pb[:, 0:512].rearrange("p (c t) -> p c t", c=4))
                        P.ld('pool', self.MIXT[0:512, t0 + tc * 128:t0 + (tc + 1) * 128].rearrange("(c p) t -> p c t", p=128),
                             zt_[:], 'hyst', 2)

    def _dft_fwd_tok(self, L, tabs, src, rsrc, evac):
        P = self.P
        TC, TS, TFC, TFS = tabs
        nch = L // 128
        for fc in range(nch):
            tc_, ts_ = TC[fc % 2], TS[fc % 2]
            P.ld('sp', tc_[:, 0:nch, :], TFC[fc], 'tabc', 2)
            P.ld('sp', ts_[:, 0:nch, :], TFS[fc], 'tabs', 2)
            pA, pB = self.ps[(fc % 2) * 2], self.ps[(fc % 2) * 2 + 1]
            for sc in range(nch):
                P.mm(pA[:], tc_[:, sc, :], src[:, sc, :], start=(sc == 0), stop=(sc == nch - 1), r=[tc_.name] + rsrc)
            for sc in range(nch):
                P.mm(pB[:], ts_[:, sc, :], src[:, sc, :], start=(sc == 0), stop=(sc == nch - 1), r=[ts_.name] + rsrc)
            evac(fc, pA, pB)

    def even_attn(self, l, ctx_mode):
        nc, P = self.nc, self.P
        e = l // 2
        lam_init = 0.8 - 0.6 * math.exp(-0.3 * l)
        if not hasattr(self, 'MIXT'):
            self.MIXT = self.dscr("MIXT", [D, T], BF16)
        with ExitStack() as st:
            ones = self.sb(st, "ones_bf", [128, 128], BF16)
            P.memset('dve', ones[:], 1.0)
            lv = self.sb(st, "lv", [128, 4, 64])
            P.ld('sp', lv[:], self.diff_lambda[e:e + 1].rearrange("o a d -> o (a d)").partition_broadcast(128), 'const', 1)
            pr = self.sb(st, "lvpr", [128, 2, 64])
            s2 = self.sb(st, "lvs", [128, 2])
            P.tt('dve', pr[:, 0, :], lv[:, 0, :], lv[:, 1, :], ALU.mult)
            P.tt('dve', pr[:, 1, :], lv[:, 2, :], lv[:, 3, :], ALU.mult)
            P.op('dve', lambda en: en.tensor_reduce(s2[:], pr[:], AX.X, ALU.add), [pr.name], [s2.name])
            ex = self.sb(st, "lvex", [128, 2])
            P.act(ex[:], s2[:], AF.Exp)
            neglam = self.sb(st, "neglam", [128, 1])
            P.tt('dve', neglam[:], ex[:, 1:2], ex[:, 0:1], ALU.subtract)
            P.ts('dve', neglam[:], neglam[:], -lam_init, None, ALU.add)
            gsc = self.sb(st, "gsc", [128, 1])
            P.ld('sp', gsc[:], self.subln_g[e].rearrange("(p o) -> p o", o=1), 'const', 1)
            P.ts('dve', gsc[:], gsc[:], 1.0 - lam_init, None, ALU.mult)
            K0s = [self.sb(st, f"K0s{i}", [128, T], BF16) for i in range(2)]
            K1s = [self.sb(st, f"K1s{i}", [128, T], BF16) for i in range(2)]
            for i in range(2):
                P.memset('dve', K0s[i][64:128, :], 0.0)
                P.memset('dve', K1s[i][0:64, :], 0.0)
            QTs = [self.sb(st, f"QTs{i}", [128, T], BF16) for i in range(2)]
            Vs = [self.sb(st, f"Vs{i}", [128, NT, 128], BF16) for i in range(2)]
            PT_ = [self.sb(st, f"PTe{i}", [128, 512], BF16) for i in range(4)]
            r_ = [self.sb(st, f"att_r{i}", [128, 512]) for i in range(2)]
            o_ = [self.sb(st, f"att_o{i}", [128, 512]) for i in range(2)]
            oo = self.sb(st, "att_oo", [128, 512])
            sq = self.sb(st, "att_sq", [128, 512], BF16)
            sd = self.sb(st, "att_sd", [128, 512])
            ob = [self.sb(st, f"att_ob{i}", [128, 512], BF16) for i in range(2)]
            ones32 = self.sb(st, "ones_f32", [128, 128])
            P.memset('dve', ones32[:], 1.0)
            accD = [[self.sb(st, f"accD{i}_{m}", [128, 512]) for m in range(2)] for i in range(2)]
            accP = [[self.sb(st, f"accP{i}_{m}", [128, 512]) for m in range(2)] for i in range(2)]
            pSb = [self.ps[0], self.ps[1], self.ps[7]]
            pO = [self.ps[2], self.ps[3]]
            pD = [self.ps[4], self.ps[5]]
            LOOK = 2
            npt = 0
            nq = 0
            for h in range(4):
                k0_, k1_, qt_, v_ = K0s[h % 2], K1s[h % 2], QTs[h % 2], Vs[h % 2]
                P.ld('sp', k0_[0:64, :], self.KT[h, 0:64, :], 'attk', 2)
                P.ld('sp', k1_[64:128, :], self.KT[h, 64:128, :], 'attk1', 2)
                P.ld('sp', qt_[:], self.QT[h], 'attq', 2)
                for a0 in range(0, NT, 8):
                    a1 = min(NT, a0 + 8)
                    P.ld('sp', v_[:, a0:a1, :], self.VA[a0 * 128:a1 * 128, h * 128:(h + 1) * 128]
                         .rearrange("(a p) c -> p a c", p=128), 'attv', 2, w=[(v_.name, a0 // 8)])
                qblocks = [(qb * 512, 512, list(range(NT))) for qb in range(8)]
                if ctx_mode == 'full':
                    qblocks.append((S, CT, [NLT, NLT + 1]))
                items = []
                for (q0, qw, ktiles) in qblocks:
                    for ki, kt in enumerate(ktiles):
                        for m in range(2):
                            items.append((q0, qw, ki, kt, m, len(ktiles)))

                def emit_qk(it, idx):
                    q0, qw, ki, kt, m, nk = it
                    pS = pSb[idx % 3]
                    km = k0_ if m == 0 else k1_
                    P.mm(pS[:, 0:qw], km[:, kt * 128:(kt + 1) * 128], qt_[:, q0:q0 + qw])

                def emit_pv(it, idx):
                    nonlocal nq
                    q0, qw, ki, kt, m, nk = it
                    pS = pSb[idx % 3]
                    pt = PT_[idx % 4]
                    P.act(pt[:, 0:qw], pS[:, 0:qw], AF.Exp, scale=0.125)
                    P.mm(pO[m][:, 0:qw], v_[:, kt, :], pt[:, 0:qw], start=(ki == 0), stop=(ki == nk - 1),
                         r=[(v_.name, kt // 8), pt.name])
                    qpar = (q0 // 512) % 2
                    if ki % 3 == 2:
                        acc, eng = accP[qpar][m], 'pool'
                        first = (ki == 2)
                    else:
                        acc, eng = accD[qpar][m], 'dve'
                        first = (ki == 0)
                    if first:
                        P.copy(eng, acc[:, 0:qw], pt[:, 0:qw], bigchain=True)
                    else:
                        P.tt(eng, acc[:, 0:qw], acc[:, 0:qw], pt[:, 0:qw], ALU.add, bigchain=True)
                    if ki == nk - 1 and m == 1:
                        for mm_ in range(2):
                            usep = nk > 2
                            P.mm(pD[mm_][:, 0:qw], ones32[:], accD[qpar][mm_][:, 0:qw], start=True, stop=not usep)
                            if usep:
                                P.mm(pD[mm_][:, 0:qw], ones32[:], accP[qpar][mm_][:, 0:qw], start=False, stop=True)
                        for mm_ in range(2):
                            P.recip(r_[mm_][:, 0:qw], pD[mm_][:, 0:qw])
                            P.tt('dve', o_[mm_][:, 0:qw], pO[mm_][:, 0:qw], r_[mm_][:, 0:qw], ALU.mult)
                        P.stt(oo[:, 0:qw], o_[1][:, 0:qw], neglam[:], o_[0][:, 0:qw], ALU.mult, ALU.add)
                        P.act(sq[:, 0:qw], oo[:, 0:qw], AF.Square)
                        pM = self.ps[6]
                        P.mm(pM[:, 0:qw], ones[:], sq[:, 0:qw])
                        P.act(sd[:, 0:qw], pM[:, 0:qw], AF.Sqrt, scale=1.0 / 128, bias=self.epsc[:])
                        P.recip(sd[:, 0:qw], sd[:, 0:qw])
                        obk = ob[nq % 2]
                        nq += 1
                        P.stt(obk[:, 0:qw], oo[:, 0:qw], gsc[:], sd[:, 0:qw], ALU.mult, ALU.mult)
                        P.ld('pool', self.MIXT[512 + h * 128:512 + (h + 1) * 128, q0:q0 + qw], obk[:, 0:qw], 'attst', 2)

                base = npt
                for i in range(min(LOOK, len(items))):
                    emit_qk(items[i], base + i)
                for i in range(len(items)):
                    if i + LOOK < len(items):
                        emit_qk(items[i + LOOK], base + i + LOOK)
                    emit_pv(items[i], base + i)
                npt += len(items)

    def epi_alloc(self, st, l, ctx_mode):
        P = self.P
        E = {}
        E['rows'] = self.load_modrows(st, l, [3, 4], ctx_mode == 'full', 'ffn')
        E['nb'] = self.alloc_norm_bufs(st)
        E['f32'] = [self.sb(st, f"f32_{i}", [128, D]) for i in range(2)]
        E['fb'] = [self.sb(st, f"fb_{i}", [128, D], BF16) for i in range(2)]
        E['fT'] = [self.sb(st, f"fT_{i}", [128, 8, 128]) for i in range(2)]
        E['rw'] = self.sb(st, "rw32", [128, 8, NE])
        P.ld('sp', E['rw'][:], self.router_w[l].rearrange("(k p) e -> p k e", p=128), 'const', 1)
        E['aff'] = self.sb(st, "AFFsb", [128, NT, NE])
        E['mx'] = [self.sb(st, f"rmx{i}", [128, 1]) for i in range(2)]
        E['sm'] = [self.sb(st, f"rsm{i}", [128, 1]) for i in range(2)]
        E['ex'] = [self.sb(st, f"rex{i}", [128, NE]) for i in range(2)]
        if not hasattr(self, 'F'):
            self.F = self.dscr("F", [T, D], BF16)
            self.AFFD = self.dscr("AFFD", [128, NT, NE])
        return E

    def epilogue_tile(self, E, l, j, hnew, n):
        P = self.P
        s_ = 0 if j < NLT else 1
        k = n % 2
        f32, fb, fT = E['f32'][k], E['fb'][k], E['fT'][k]
        self.norm_mod(E['nb'], hnew, f32[:], E['rows'][(s_, 4)][:], E['rows'][(s_, 3)][:], n)
        P.copy('act', fb[:], f32[:])
        P.ld('pool', self.F[j * 128:(j + 1) * 128, :], fb[:], 'fst', 2)
        pa, pb_, pl = self.ps[6], self.ps[7], self.ps[5]
        for c in range(8):
            pp = pa if c < 4 else pb_
            P.tr(pp[:, (c % 4) * 128:(c % 4 + 1) * 128], f32[:, c * 128:(c + 1) * 128], self.ident[:])
        P.copy('act', fT[:, 0:4, :], pa[:].rearrange("p (c t) -> p c t", c=4))
        P.copy('dve', fT[:, 4:8, :], pb_[:].rearrange("p (c t) -> p c t", c=4))
        for c in range(8):
            P.mm(pl[:, 0:NE], fT[:, c, :], E['rw'][:, c, :], start=(c == 0), stop=(c == 7))
        mx, sm, ex = E['mx'][k], E['sm'][k], E['ex'][k]
        P.op('dve', lambda e: e.tensor_reduce(mx[:], pl[:, 0:NE], AX.X, ALU.max, negate=True),
             ['ps5'], [mx.name])
        P.act(ex[:], pl[:, 0:NE], AF.Exp, bias=mx[:], accum_out=sm[:])
        P.recip(sm[:], sm[:])
        P.ts('dve', E['aff'][:, j, :], ex[:], sm[:], None, ALU.mult)

    def even_wout(self, l, ctx_mode):
        P = self.P
        e = l // 2
        ntiles = NT if ctx_mode == 'full' else NLT
        with ExitStack() as st:
            E = self.epi_alloc(st, l, ctx_mode)
            g1 = self.load_modrows(st, l, [2], ctx_mode == 'full', 'g1')
            Wo = self.sb(st, "Wout", [128, 8, D], BF16)
            P.ld('pool', Wo[:], self.w_out[e].rearrange("(k p) n -> p k n", p=128), 'woutld', 1)
            Mt = [self.sb(st, f"Mt{i}", [128, 8, 512], BF16) for i in range(2)]
            hb = [self.sb(st, f"hbw{i}", [128, D]) for i in range(2)]
            hn = [self.sb(st, f"hnw{i}", [128, D]) for i in range(2)]
            for j in range(ntiles):
                s_ = 0 if j < NLT else 1
                if j % 4 == 0:
                    nt4 = min(4, ntiles - j)
                    m_ = Mt[(j // 4) % 2]
                    P.ld('sp', m_[:, :, 0:nt4 * 128], self.MIXT[:, j * 128:(j + nt4) * 128].rearrange("(k p) t -> p k t", p=128),
                         'mixld', 2)
                m_ = Mt[(j // 4) % 2]
                k = j % 2
                P.ld('sp', hb[k][:], self.hsrc(l, j), 'hld', 2)
                for half in range(2):
                    pY = self.ps[(j % 2) * 2 + half]
                    for kk in range(8):
                        P.mm(pY[:], m_[:, kk, (j % 4) * 128:(j % 4 + 1) * 128], Wo[:, kk, half * 512:(half + 1) * 512],
                             start=(kk == 0), stop=(kk == 7))
                    hs = slice(half * 512, (half + 1) * 512)
                    P.tt('dve', hn[k][:, hs], pY[:], g1[(s_, 2)][:, hs], ALU.mult)
                    P.tt('dve', hn[k][:, hs], hn[k][:, hs], hb[k][:, hs], ALU.add)
                P.ld('pool', self.hdst(j), hn[k][:], 'hst', 2)
                self.epilogue_tile(E, l, j, hn[k][:], j)
            P.ld('pool', self.AFFD, E['aff'][:], 'affst', 1)

    def odd_pool(self, l, ctx_mode):
        P = self.P
        o = l // 2
        ntiles = NT if ctx_mode == 'full' else NLT
        with ExitStack() as st:
            E = self.epi_alloc(st, l, ctx_mode)
            A = self.sb(st, "A_tm", [128, ntiles, D], BF16)
            with ExitStack() as st2:
                rows = self.load_modrows(st2, l, [0, 1], ctx_mode == 'full', 'mixp')
                hb = [self.sb(st2, f"hbp{i}", [128, D]) for i in range(2)]
                for j in range(ntiles):
                    s_ = 0 if j < NLT else 1
                    P.ld('sp', hb[j % 2][:], self.hsrc(l, j), 'hld', 2)
                    self.norm_mod(E['nb'], hb[j % 2][:], A[:, j, :], rows[(s_, 1)][:], rows[(s_, 0)][:], j)
            P.barrier()
            g1 = self.load_modrows(st, l, [2], ctx_mode == 'full', 'g1p')
            psr = self.sb(st, "pscale", [128, D])
            P.ld('sp', psr[:], self.pool_scale[o:o + 1, :].partition_broadcast(128), 'const', 1)
            for k_ in g1:
                P.tt('dve', g1[k_][:], g1[k_][:], psr[:], ALU.mult)
            MT = self.sb(st, "poolMT", [128, 4, 5, 128], BF16)
            P.ld('sp', MT[:], self.poolmt, 'const', 1)
            PW = self.sb(st, "poolW", [128, 4, 2, 256], BF16)
            P.ld('pool', PW[:], self.pool_w[o].rearrange("g (c p) d -> p g c d", p=128), 'woutld', 1)
            pT = [self.sb(st, f"poolpT{i}", [128, 8, 128], BF16) for i in range(2)]
            hb = [self.sb(st, f"hbq{i}", [128, D]) for i in range(2)]
            hn = [self.sb(st, f"hnq{i}", [128, D]) for i in range(2)]
            for j in range(ntiles):
                s_ = 0 if j < NLT else 1
                first = j in (0, NLT)
                last = j in (NLT - 1, NT - 1)
                k = j % 2
                P.ld('sp', hb[k][:], self.hsrc(l, j), 'hld', 2)
                pp = [self.ps[(j % 2) * 2], self.ps[(j % 2) * 2 + 1]]
                for cc in range(8):
                    g = cc // 2
                    dst = pp[cc // 4][:, (cc % 4) * 128:(cc % 4 + 1) * 128]
                    cs = slice(cc * 128, (cc + 1) * 128)
                    terms = []
                    if not first:
                        terms.append((A[:, j - 1, cs], MT[:, g, 3, :]))
                    terms.append((A[:, j, cs], MT[:, g, 1 if first else (2 if last else 0), :]))
                    if not last:
                        terms.append((A[:, j + 1, cs], MT[:, g, 4, :]))
                    for ti, (lh, rh) in enumerate(terms):
                        P.mm(dst, lh, rh, start=(ti == 0), stop=(ti == len(terms) - 1))
                pt = pT[k]
                P.copy('act', pt[:, 0:4, :], pp[0][:].rearrange("p (c t) -> p c t", c=4))
                P.copy('dve', pt[:, 4:8, :], pp[1][:].rearrange("p (c t) -> p c t", c=4))
                for half in range(2):
                    pY = self.ps[4 + (j % 2) * 2 + half]
                    for gg in range(2):
                        g = half * 2 + gg
                        for cc in range(2):
                            P.mm(pY[:, gg * 256:(gg + 1) * 256], pt[:, g * 2 + cc, :], PW[:, g, cc, :],
                                 start=(cc == 0), stop=(cc == 1))
                    hs = slice(half * 512, (half + 1) * 512)
                    P.tt('dve', hn[k][:, hs], pY[:], g1[(s_, 2)][:, hs], ALU.mult)
                    P.tt('dve', hn[k][:, hs], hn[k][:, hs], hb[k][:, hs], ALU.add)
                P.ld('pool', self.hdst(j), hn[k][:], 'hst', 2)
                self.epilogue_tile(E, l, j, hn[k][:], j)
            P.ld('pool', self.AFFD, E['aff'][:], 'affst', 1)

    def mixer_passthrough(self, l, ctx_mode):
        P = self.P
        ntiles = NT if ctx_mode == 'full' else NLT
        with ExitStack() as st:
            E = self.epi_alloc(st, l, ctx_mode)
            hb = [self.sb(st, f"hb{i}", [128, D]) for i in range(2)]
            for j in range(ntiles):
                P.ld('sp', hb[j % 2][:], self.hsrc(l, j), 'hld', 2)
                if l == 0:
                    P.ld('sp', self.hdst(j), hb[j % 2][:], 'hst', 2)
                self.epilogue_tile(E, l, j, hb[j % 2][:], j)
            P.ld('pool', self.AFFD, E['aff'][:], 'affst', 1)

    def moe_topk(self, l, ctx_mode):
        P = self.P
        groups = [(0, NLT, 512)] + ([(NLT, 2, 32)] if ctx_mode == 'full' else [])
        if not hasattr(self, 'IDXD'):
            self.IDXD = self.dscr("IDXD", [128, NE, 5], I32)
            self.GD = self.dscr("GD", [128, NE, 5])
        with ExitStack() as st:
            aff = self.sb(st, "aff_tm", [128, NT, NE])
            P.ld('sp', aff[:], self.AFFD, 'const', 1)
            ones = self.sb(st, "ones_scan", [NE, S])
            P.memset('dve', ones[:], 1.0)
            iota32 = self.sb(st, "iota512f", [128, 512])
            P.op('pool', lambda e: e.iota(iota32[:], [[1, 512]], base=0, channel_multiplier=0,
                                          allow_small_or_imprecise_dtypes=True), [], [iota32.name])
            iota = self.sb(st, "iota512", [128, 512], F16)
            P.copy('dve', iota[:], iota32[:])
            slot16 = self.sb(st, "slot_tm16", [128, NT, NE], F16)
            slot = self.sb(st, "slot_tm", [128, NT, NE])
            rhs = self.sb(st, "ohrhs", [128, NT, NE, 4], BF16)
            tokhl = self.sb(st, "tokhl", [128, NT, 2])
            P.ld('sp', tokhl[:], self.tokhl_in, 'const', 1)
            ahi = self.sb(st, "ahi", [128, NT, NE], BF16)
            alo = self.sb(st, "alo", [128, NT, NE])
            P.copy('dve', ahi[:], aff[:])
            P.tt('dve', alo[:], aff[:], ahi[:], ALU.subtract)
            for j in range(NT):
                P.copy('dve', rhs[:, j, :, 0:2], tokhl[:, j:j + 1, :].to_broadcast([128, NE, 2]))
            P.copy('dve', rhs[:, :, :, 2], ahi[:])
            P.copy('dve', rhs[:, :, :, 3], alo[:])
            with ExitStack() as st2:
                G_ = []
                for gi, (j0, ntl, cap) in enumerate(groups):
                    L = ntl * 128
                    g_ = dict(j0=j0, ntl=ntl, cap=cap, L=L)
                    g_['affT'] = self.sb(st2, f"affT{gi}", [NE, L])
                    g_['junk'] = self.sb(st2, f"tk_junk{gi}", [NE, L])
                    g_['maskT'] = self.sb(st2, f"maskT{gi}", [NE, L])
                    g_['posT'] = self.sb(st2, f"posT{gi}", [NE, L])
                    g_['slotT'] = self.sb(st2, f"slotT{gi}", [NE, L])
                    for n_ in ('lo', 'hi', 'mid', 'cnt', 'pred', 'd', 'd2'):
                        g_[n_] = self.sb(st2, f"tk_{n_}{gi}", [NE, 1])
                    G_.append(g_)
                for g_ in G_:
                    ntl, j0, affT = g_['ntl'], g_['j0'], g_['affT']
                    for q in range((ntl + 3) // 4):
                        pst = self.ps[q % 2]
                        m_ = min(4, ntl - q * 4)
                        for i in range(m_):
                            P.tr(pst[0:NE, i * 128:(i + 1) * 128], aff[:, j0 + q * 4 + i, :], self.ident[:])
                        P.copy('act', affT[:, q * 512:q * 512 + m_ * 128], pst[0:NE, 0:m_ * 128])
                    P.memset('dve', g_['lo'][:], 0.0)
                    P.memset('dve', g_['hi'][:], 1.0)
                for it in range(30):
                    for g_ in G_:
                        P.ts('dve', g_['mid'][:], g_['lo'][:], g_['hi'][:], 0.5, ALU.add, ALU.mult)
                    for g_ in G_:
                        P.ts('dve', g_['junk'][:], g_['affT'][:], g_['mid'][:], None, ALU.is_ge, ALU.add, accum_out=g_['cnt'][:])
                    for g_ in G_:
                        P.ts('dve', g_['pred'][:], g_['cnt'][:], float(g_['cap']), None, ALU.is_ge)
                    for g_ in G_:
                        P.tt('dve', g_['d'][:], g_['mid'][:], g_['lo'][:], ALU.subtract)
                    for g_ in G_:
                        P.tt('dve', g_['d2'][:], g_['hi'][:], g_['mid'][:], ALU.subtract)
                    for g_ in G_:
                        P.stt(g_['lo'][:], g_['d'][:], g_['pred'][:], g_['lo'][:], ALU.mult, ALU.add)
                    for g_ in G_:
                        P.stt(g_['hi'][:], g_['d2'][:], g_['pred'][:], g_['mid'][:], ALU.mult, ALU.add)
                BIG = 2048.0
                for g_ in G_:
                    L, ntl, j0 = g_['L'], g_['ntl'], g_['j0']
                    maskT, posT, slotT = g_['maskT'], g_['posT'], g_['slotT']
                    P.ts('dve', maskT[:], g_['affT'][:], g_['lo'][:], None, ALU.is_ge)
                    P.op('dve', lambda e, posT=posT, maskT=maskT, L=L: e.tensor_tensor_scan(
                        posT[:], ones[:, 0:L], maskT[:], 0.0, ALU.mult, ALU.add),
                        [ones.name, maskT.name], [posT.name])
                    P.stt(slotT[:], posT[:], -1.0 - BIG, maskT[:], ALU.add, ALU.mult)
                    P.ts('dve', slotT[:], slotT[:], BIG, None, ALU.add)
                    for q in range(ntl):
                        pst = self.ps[2 + q % 2]
                        P.tr(pst[:, 0:NE], slotT[:, q * 128:(q + 1) * 128], self.ident[0:NE, 0:NE])
                        P.copy('act', slot[:, j0 + q, :], pst[:, 0:NE])
            P.barrier()
            P.copy('dve', slot16[:], slot[:])
            if 'SLOTD' in self.dbg:
                P.ld('sp', self.dscr("SLOTD", [128, NT, NE]), slot[:], 'dbg', 1)
            with ExitStack() as st2:
                oh = [self.sb(st2, f"oh{i}", [128, NLT, 512], BF16) for i in range(2)]
                ohc = [self.sb(st2, f"ohc{i}", [128, 2, 32], BF16) for i in range(2)]
                res = self.sb(st2, "tk_res", [128, NE, 5, 4])
                P.memset('dve', res[:], 0.0)
                for e_ in range(NE):
                    o = oh[e_ % 2]
                    for j in range(NLT):
                        P.ts('dve', o[:, j, :], iota[:], slot16[:, j, e_:e_ + 1], None, ALU.is_equal)
                    pst = self.ps[4 + e_ % 2]
                    for sc in range(4):
                        for j in range(NLT):
                            P.mm(pst[:, sc * 4:(sc + 1) * 4], o[:, j, sc * 128:(sc + 1) * 128], rhs[:, j, e_, :],
                                 start=(j == 0), stop=(j == NLT - 1))
                    P.copy('act', res[:, e_, 0:4, :], pst[:, 0:16].rearrange("p (a b) -> p a b", a=4))
                    if ctx_mode == 'full':
                        oc = ohc[e_ % 2]
                        for jj in range(2):
                            P.ts('dve', oc[:, jj, :], iota[:, 0:32], slot16[:, NLT + jj, e_:e_ + 1], None, ALU.is_equal)
                        pst2 = self.ps[6 + e_ % 2]
                        for jj in range(2):
                            P.mm(pst2[0:32, 0:4], oc[:, jj, :], rhs[:, NLT + jj, e_, :], start=(jj == 0), stop=(jj == 1))
                        P.copy('act', res[0:32, e_, 4, :], pst2[0:32, 0:4])
                idxf = self.sb(st2, "idxf", [128, NE, 5])
                idxi = self.sb(st2, "idxi", [128, NE, 5], I32)
                gg = self.sb(st2, "gg", [128, NE, 5])
                P.stt(idxf[:], res[:, :, :, 0], 64.0, res[:, :, :, 1], ALU.mult, ALU.add)
                P.copy('dve', idxi[:], idxf[:])
                P.tt('dve', gg[:], res[:, :, :, 2], res[:, :, :, 3], ALU.add)
                P.ld('sp', self.IDXD, idxi[:], 'tkst', 1)
                P.ld('sp', self.GD, gg[:], 'tkst2', 1)

    def moe_experts(self, l, ctx_mode):
        nc, P = self.nc, self.P
        has_ctx = ctx_mode == 'full'
        NS = 544 if has_ctx else 512
        with ExitStack() as st:
            rows = self.load_modrows(st, l, [5], has_ctx, 'g2')
            idxs = self.sb(st, "idxs", [128, NE, 5], I32)
            idxc = self.sb(st, "idxc", [128, NE], I32)
            G = self.sb(st, "Gs", [128, NE, 5])
            P.ld('sp', idxs[:], self.IDXD, 'const', 1)
            P.ld('sp', G[:], self.GD, 'const', 1)
            if has_ctx:
                P.ts('dve', idxc[:], idxs[:, :, 4], -float(S), None, ALU.add)
            xs = [self.sb(st, f"xs{i}", [128, 5, D], BF16) for i in range(2)]
            xsT = [self.sb(st, f"xsT{i}", [128, 8, 544], BF16) for i in range(2)]
            h1T = self.sb(st, "h1T", [128, 16, 544], BF16)
            sa = [self.sb(st, f"sa{i}", [128, 544], BF16) for i in range(2)]
            WG = [self.sb(st, f"WG{i}", [128, 8, 512], BF16) for i in range(2)]
            WU = [self.sb(st, f"WU{i}", [128, 8, 512], BF16) for i in range(2)]
            WD = [[self.sb(st, f"WD{i}_{f}", [128, 4, D], BF16) for f in range(4)] for i in range(2)]
            yout = [self.sb(st, f"yout{i}", [128, D]) for i in range(2)]

            def gather(e_):
                x_ = xs[e_ % 2]
                for sc in range(4):
                    P.dma('pool', lambda e, x_=x_, sc=sc, e_=e_: e.indirect_dma_start(
                        out=x_[:, sc, :], out_offset=None, in_=self.F,
                        in_offset=bass.IndirectOffsetOnAxis(ap=idxs[:, e_, sc:sc + 1], axis=0)),
                        r=[idxs.name], w=[(x_.name, sc)], stream='gath', nbuf=2)
                if has_ctx:
                    P.dma('pool', lambda e, x_=x_, e_=e_: e.indirect_dma_start(
                        out=x_[0:32, 4, :], out_offset=None, in_=self.F,
                        in_offset=bass.IndirectOffsetOnAxis(ap=idxs[0:32, e_, 4:5], axis=0)),
                        r=[idxs.name], w=[(x_.name, 4)], stream='gath', nbuf=2)

            def load_gu(e_, fg):
                n_ = e_ * 4 + fg
                P.ld('pool', WG[n_ % 2][:], self.w_gate[l, e_, :, fg * 512:(fg + 1) * 512]
                     .rearrange("(k p) n -> p k n", p=128), 'wg', 2)
                P.ld('pool', WU[n_ % 2][:], self.w_up[l, e_, :, fg * 512:(fg + 1) * 512]
                     .rearrange("(k p) n -> p k n", p=128), 'wu', 2)

            def load_d(e_, fg):
                P.ld('pool', WD[e_ % 2][fg][:], self.w_down[l, e_, fg * 512:(fg + 1) * 512, :]
                     .rearrange("(k p) n -> p k n", p=128), 'wd', 8)

            def transposes(e_):
                x_, xt = xs[e_ % 2], xsT[e_ % 2]
                for sc in range(5 if has_ctx else 4):
                    pb = self.ps[5][:].bitcast(BF16)
                    if sc < 4:
                        for c in range(8):
                            P.tr(pb[:, c * 128:(c + 1) * 128], x_[:, sc, c * 128:(c + 1) * 128], self.identb[:],
                                 r=[(x_.name, sc), self.identb.name])
                        P.copy('act', xt[:, :, sc * 128:(sc + 1) * 128], pb.rearrange("p (c t) -> p c t", c=8))
                    else:
                        for c in range(8):
                            P.tr(pb[:, c * 32:(c + 1) * 32], x_[0:32, 4, c * 128:(c + 1) * 128], self.identb[0:32, 0:32],
                                 r=[(x_.name, 4), self.identb.name])
                        P.copy('act', xt[:, :, 512:544], pb[:, 0:256].rearrange("p (c t) -> p c t", c=8))

            gather(0)
            load_gu(0, 0)
            gather(1)
            transposes(0)
            nmm = 0
            for e_ in range(NE):
                x_, xt = xs[e_ % 2], xsT[e_ % 2]
                for fg in range(4):
                    if fg < 3:
                        load_gu(e_, fg + 1)
                    elif e_ + 1 < NE:
                        load_gu(e_ + 1, 0)
                    load_d(e_, fg)
                    n_ = e_ * 4 + fg
                    wg, wu = WG[n_ % 2], WU[n_ % 2]
                    for fc in range(4):
                        f = fg * 4 + fc
                        pA, pU = self.ps[nmm % 2], self.ps[2 + nmm % 2]
                        pC = self.ps[4]
                        k2 = nmm % 2
                        nmm += 1
                        for k in range(8):
                            P.mm(pA[:, 0:512], wg[:, k, fc * 128:(fc + 1) * 128], xt[:, k, 0:512], start=(k == 0), stop=(k == 7))
                        for k in range(8):
                            P.mm(pU[:, 0:512], wu[:, k, fc * 128:(fc + 1) * 128], xt[:, k, 0:512], start=(k == 0), stop=(k == 7))
                        P.act(sa[k2][:, 0:512], pA[:, 0:512], AF.Silu)
                        P.tt('dve', h1T[:, f, 0:512], sa[k2][:, 0:512], pU[:, 0:512], ALU.mult)
                        if has_ctx:
                            for k in range(8):
                                P.mm(pC[:, 0:32], wg[:, k, fc * 128:(fc + 1) * 128], xt[:, k, 512:544], start=(k == 0), stop=(k == 7))
                            for k in range(8):
                                P.mm(pC[:, 32:64], wu[:, k, fc * 128:(fc + 1) * 128], xt[:, k, 512:544], start=(k == 0), stop=(k == 7))
                            P.act(sa[k2][:, 512:544], pC[:, 0:32], AF.Silu)
                            P.tt('dve', h1T[:, f, 512:544], sa[k2][:, 512:544], pC[:, 32:64], ALU.mult)
                if e_ + 1 < NE:
                    transposes(e_ + 1)
                if e_ + 2 < NE:
                    gather(e_ + 2)
                wd = WD[e_ % 2]
                for sc in range(5 if has_ctx else 4):
                    np_ = 128 if sc < 4 else 32
                    yo = yout[(e_ * 5 + sc) % 2]
                    s_ = 0 if sc < 4 else 1
                    for half in range(2):
                        pY = self.ps[6 + half]
                        for f in range(16):
                            P.mm(pY[0:np_, :], h1T[:, f, sc * 128:sc * 128 + np_], wd[f // 4][:, f % 4, half * 512:(half + 1) * 512],
                                 start=(f == 0), stop=(f == 15))
                        P.stt(yo[0:np_, half * 512:(half + 1) * 512], pY[0:np_, :], G[0:np_, e_, sc:sc + 1],
                              rows[(s_, 5)][0:np_, half * 512:(half + 1) * 512], ALU.mult, ALU.mult)
                    if sc < 4:
                        P.dma('pool', lambda e, yo=yo, e_=e_, sc=sc: e.indirect_dma_start(
                            out=self.out, out_offset=bass.IndirectOffsetOnAxis(ap=idxs[:, e_, sc:sc + 1], axis=0),
                            in_=yo[:], in_offset=None, compute_op=ALU.add),
                            r=[idxs.name, yo.name], w=[], stream='scat', nbuf=1)
                    else:
                        P.dma('pool', lambda e, yo=yo, e_=e_: e.indirect_dma_start(
                            out=self.hc, out_offset=bass.IndirectOffsetOnAxis(ap=idxc[0:32, e_:e_ + 1], axis=0),
                            in_=yo[0:32, :], in_offset=None, compute_op=ALU.add),
                            r=[idxc.name, yo.name], w=[], stream='scat', nbuf=1)


def bf(a):
    return np.ascontiguousarray(np.asarray(a, np.float32).astype(ml_dtypes.bfloat16))


_CONST = {}


def host_consts():
    if _CONST:
        return _CONST
    c = {}
    c['ident'] = np.eye(128, dtype=np.float32)
    nf = 16
    inv = (10000.0 ** (-np.arange(nf, dtype=np.float32) / nf)).astype(np.float32)
    t = np.arange(S)
    row = (t // 64).astype(np.float32)
    col = (t % 64).astype(np.float32)
    cos = np.ones((128, T), np.float32)
    sin = np.zeros((128, T), np.float32)
    perm = np.zeros((128, 128), np.float32)
    for p in range(128):
        d = p % 64
        pos = row if d < 32 else col
        i = d % 16
        half = (d % 32) // 16
        ang = (pos * inv[i]).astype(np.float32)
        cos[p, :S] = np.cos(ang)
        sn = np.sin(ang)
        sin[p, :S] = -sn if half == 0 else sn
        partner = p + 16 if half == 0 else p - 16
        perm[partner, p] = 1.0
    c['rope_cos'] = cos
    c['rope_sin'] = sin
    c['rope_perm'] = bf(perm)
    bo = np.zeros((128, 128), np.float32)
    bo[:64, :64] = 1.0
    bo[64:, 64:] = 1.0
    c['blockones'] = bf(bo)
    tid = (np.arange(NT)[None, :] * 128 + np.arange(128)[:, None])
    c['tokhl'] = np.ascontiguousarray(np.stack([tid // 64, tid % 64], axis=-1).astype(np.float32))
    mt = np.zeros((128, 4, 5, 128), np.float32)
    Lp = 1024
    for gi, w_ in enumerate((2, 4, 8, 16)):
        Mfull = np.zeros((Lp, Lp), np.float64)
        for t_ in range(Lp):
            lo = max(t_ - w_ // 2, 0)
            hi = min(t_ + w_ // 2, Lp)
            Mfull[t_, lo:hi] = 1.0 / (hi - lo)
            Mfull[t_, t_] -= 1.0
        def blk(ti, si):
            return Mfull[ti * 128:(ti + 1) * 128, si * 128:(si + 1) * 128].T
        mt[:, gi, 0] = blk(3, 3)
        mt[:, gi, 1] = blk(0, 0)
        mt[:, gi, 2] = blk(7, 7)
        mt[:, gi, 3] = blk(3, 2)
        mt[:, gi, 4] = blk(3, 4)
    c['poolmt'] = bf(mt)
    for L, sfx in ((S, ""), (CT, "c")):
        f32 = np.float32
        t = np.linspace(0.0, 1.0, L, dtype=f32)[:, None]
        w = (2.0 * math.pi * np.arange(L, dtype=f32)[:, None] / L).astype(f32)
        f = np.linspace(1e-4, 15, 16, dtype=f32)[None, :]
        z = np.concatenate([t, np.cos(f * w), -np.sin(f * w)], axis=-1).astype(f32)
        c['zT' + sfx] = np.ascontiguousarray(z.T)
        max_decay = math.log(1e-2) / 0.3
        min_decay = math.log(1e-2) / 1.5
        deltas = np.abs(np.linspace(min_decay, max_decay, 512, dtype=f32))
        c['decay' + sfx] = np.exp(-t * deltas[None, :]).astype(f32)
        N = 2 * L
        n = L // 128
        sidx = np.arange(L, dtype=np.int64)
        arg = ((2 * sidx[None, :] + 1) * sidx[:, None]) % (2 * N)
        ang = arg.astype(np.float64) * (math.pi / N)
        for nm, fn, sign in (("C", np.cos, 1.0), ("S", np.sin, 1.0)):
            tf = fn(ang)
            blk = tf.reshape(n, 128, n, 128)
            c['TF' + nm + sfx] = bf(blk.transpose(2, 1, 0, 3))
            ti = tf.T if nm == "C" else -tf.T
            blk = ti.reshape(n, 128, n, 128)
            c['TI' + nm + sfx] = bf(blk.transpose(2, 1, 0, 3))
    _CONST.update(c)
    return _CONST


def host_inputs(I, b):
    c = dict(host_consts())
    m = {}
    m['x'] = np.ascontiguousarray(I['x'][b])
    m['ctx'] = np.ascontiguousarray(I['ctx'][b])
    m['cT'] = np.ascontiguousarray(np.stack([I['c'][b].reshape(8, 128).T, I['c_ctx'].reshape(8, 128).T], axis=-1))
    m['ada_w'] = I['ada_w']
    m['ada_b'] = I['ada_b']
    m['norm_g'] = np.ascontiguousarray(np.stack([I['norm_mix_g'], I['norm_ffn_g']], axis=1))
    m['w_in'] = I['w_in']
    cw = np.concatenate([I['hy_conv_w'], I['hy_conv_b'][:, None, :]], axis=1)
    m['convw'] = np.ascontiguousarray(cw.reshape(2, 4, 12, 128).transpose(0, 3, 2, 1))
    qg = np.tile(I['q_norm_g'], (1, 2))
    kg = np.tile(I['k_norm_g'], (1, 2))
    m['qkg'] = np.ascontiguousarray(np.stack([qg, kg], axis=-1))
    m['router_w'] = I['router_w']
    m['w_out'] = I['w_out']
    m['pool_w'] = I['pool_w']
    m['pool_scale'] = I['pool_scale']
    for k_ in ('hy_f_w1', 'hy_f_w2', 'hy_f_w3', 'hy_f_wout', 'hy_bias'):
        m[k_] = I[k_]
    m['hy_fvec'] = np.ascontiguousarray(np.stack([I['hy_f_freq'], I['hy_f_b1'], I['hy_f_b2'], I['hy_f_b3']], axis=-1))
    m['diff_lambda'] = I['diff_lambda']
    m['subln_g'] = I['subln_g']
    m['w_gate'] = I['exp_w_gate']
    m['w_up'] = I['exp_w_up']
    m['w_down'] = I['exp_w_down']
    m.update(c)
    return m


_NC_CACHE = {}


def kernel(**inputs):
    I = {k: np.asarray(v) for k, v in inputs.items()}
    if 'nc' not in _NC_CACHE:
        b_ = Builder()
        _NC_CACHE['nc'] = b_.build()
        _NC_CACHE['names'] = set(b_.dram)
    nc = _NC_CACHE['nc']
    names = _NC_CACHE['names']
    in_maps = []
    for b in range(8):
        m = host_inputs(I, b)
        in_maps.append({k: v for k, v in m.items() if k in names})
    res = run_bass_kernel_spmd(nc, in_maps, core_ids=list(range(8)))
    return np.stack([np.asarray(res.results[b]["y"]) for b in range(8)], axis=0).astype(np.float32)
```
